# Optimizing a Trainium2 kernel written in Bass

```python
import math
import jax, jax.numpy as jnp
from jax import lax
import numpy as np

D_MODEL = 1024
BATCH = 8
SEQ = 4096
DEPTH = 2

N_MEM = 256
MIX_WIDTH = D_MODEL
ML_HEADS = 4
ML_WIDTH = MIX_WIDTH // 2
ML_HEAD_DIM = ML_WIDTH // ML_HEADS
ML_CHUNK = 64
ML_CONV = 4
SWA_HEAD_DIM = 64
SWA_WIDTH = MIX_WIDTH - ML_WIDTH
SWA_HEADS = SWA_WIDTH // SWA_HEAD_DIM
SWA_KV_HEADS = SWA_HEADS // 4
SWA_GROUP = SWA_HEADS // SWA_KV_HEADS
SWA_KV_WIDTH = SWA_KV_HEADS * SWA_HEAD_DIM
WINDOW = 128
BLOCK = 128
REL_BUCKETS = 32
REL_MAX_DIST = 128
XA_HEADS = 4
XA_HEAD_DIM = D_MODEL // XA_HEADS
D_FF = 256 * ((8 * D_MODEL // 3 + 255) // 256)
FFN_CONV = 3
ALPHA = (2.0 * DEPTH) ** 0.25
BETA = (8.0 * DEPTH) ** -0.25
EPS = 1e-5
IN_SPLITS = (2 * ML_WIDTH, 3 * ML_WIDTH, 4 * ML_WIDTH, 4 * ML_WIDTH + ML_HEADS, 4 * ML_WIDTH + 2 * ML_HEADS, 4 * ML_WIDTH + 2 * ML_HEADS + SWA_WIDTH, 4 * ML_WIDTH + 2 * ML_HEADS + SWA_WIDTH + SWA_KV_WIDTH)
N_IN = IN_SPLITS[-1] + SWA_KV_WIDTH

kernel_name = "hybrid_mlstm_swa_deepnorm_block"

f32 = jnp.float32


def layer_norm(x, g, b):
    xf = x.astype(f32)
    mu = xf.mean(-1, keepdims=True)
    var = jnp.square(xf - mu).mean(-1, keepdims=True)
    return ((xf - mu) * lax.rsqrt(var + EPS) * g.astype(f32) + b.astype(f32)).astype(x.dtype)


def causal_dwconv(x, w, b):
    K = w.shape[0]
    S = x.shape[1]
    xp = jnp.pad(x, ((0, 0), (K - 1, 0), (0, 0)))
    y = b + xp[:, 0:S] * w[0]
    for j in range(1, K):
        y = y + xp[:, j:j + S] * w[j]
    return y


def t5_bucket(dist):
    n = jnp.maximum(dist, 0)
    max_exact = REL_BUCKETS // 2
    nf = jnp.maximum(n, 1).astype(f32)
    large = max_exact + (jnp.log(nf / max_exact) / math.log(REL_MAX_DIST / max_exact) * (REL_BUCKETS - max_exact)).astype(jnp.int32)
    large = jnp.minimum(large, REL_BUCKETS - 1)
    return jnp.where(n < max_exact, n, large)


def mlstm(q, k, v, o_pre, i_pre, f_pre, norm_g):
    B, S, _ = q.shape
    nc = S // ML_CHUNK
    L = ML_CHUNK

    def heads(t):
        return t.astype(f32).reshape(B, nc, L, ML_HEADS, ML_HEAD_DIM).transpose(1, 0, 3, 2, 4)

    def gates(t):
        return t.astype(f32).reshape(B, nc, L, ML_HEADS).transpose(1, 0, 3, 2)

    qc = heads(q)
    kc = heads(k) * (ML_HEAD_DIM ** -0.5)
    vc = heads(v)
    ic = gates(i_pre)
    lfc = jax.nn.log_sigmoid(gates(f_pre))
    causal = jnp.tril(jnp.ones((L, L), dtype=bool))

    def step(carry, inp):
        C, n, m = carry
        qb, kb, vb, ig, lf = inp
        b = jnp.cumsum(lf, axis=-1)
        Dm = jnp.where(causal, b[..., :, None] - b[..., None, :] + ig[..., None, :], -jnp.inf)
        inter = b + m[..., None]
        m_t = jnp.maximum(inter, Dm.max(-1))
        w_inter = jnp.exp(inter - m_t)
        s = jnp.einsum('bhtd,bhsd->bhts', qb, kb) * jnp.exp(Dm - m_t[..., None])
        num = w_inter[..., None] * jnp.einsum('bhtd,bhde->bhte', qb, C) + jnp.einsum('bhts,bhse->bhte', s, vb)
        den = w_inter * jnp.einsum('bhtd,bhd->bht', qb, n) + s.sum(-1)
        h = num / jnp.maximum(jnp.abs(den), jnp.exp(-m_t))[..., None]
        g = b[..., -1]
        a = g[..., None] - b + ig
        m_new = jnp.maximum(g + m, a.max(-1))
        decay = jnp.exp(g + m - m_new)
        wk = jnp.exp(a - m_new[..., None])
        C_new = decay[..., None, None] * C + jnp.einsum('bhsd,bhse->bhde', kb * wk[..., None], vb)
        n_new = decay[..., None] * n + jnp.einsum('bhs,bhsd->bhd', wk, kb)
        return (C_new, n_new, m_new), h

    init = (jnp.zeros((B, ML_HEADS, ML_HEAD_DIM, ML_HEAD_DIM), f32),
            jnp.zeros((B, ML_HEADS, ML_HEAD_DIM), f32),
            jnp.zeros((B, ML_HEADS), f32))
    _, h = lax.scan(step, init, (qc, kc, vc, ic, lfc))
    mu = h.mean(-1, keepdims=True)
    var = jnp.square(h - mu).mean(-1, keepdims=True)
    hn = ((h - mu) * lax.rsqrt(var + EPS)).transpose(1, 0, 3, 2, 4).reshape(B, S, ML_WIDTH)
    hn = hn * norm_g.astype(f32)
    return (jax.nn.sigmoid(o_pre.astype(f32)) * hn).astype(q.dtype)


def sliding_window_attention(q, k, v, sinks, rel_bias):
    B, S = q.shape[:2]
    nb = S // BLOCK
    qb = q.reshape(B, nb, BLOCK, SWA_KV_HEADS, SWA_GROUP, SWA_HEAD_DIM)

    def band(t):
        tb = t.reshape(B, nb, BLOCK, SWA_KV_HEADS, SWA_HEAD_DIM)
        prev = jnp.pad(tb, ((0, 0), (1, 0), (0, 0), (0, 0), (0, 0)))[:, :-1]
        return jnp.concatenate([prev, tb], axis=2)

    kb, vb = band(k), band(v)
    logits = jnp.einsum('bnqhgd,bnkhd->bnhgqk', qb, kb).astype(f32) * (SWA_HEAD_DIM ** -0.5)
    r = jnp.arange(BLOCK)[:, None]
    c = jnp.arange(2 * BLOCK)[None, :]
    dist = BLOCK + r - c
    bias = rel_bias.astype(f32)[t5_bucket(dist)]
    bias = bias.transpose(2, 0, 1).reshape(SWA_KV_HEADS, SWA_GROUP, BLOCK, 2 * BLOCK)
    kpos = jnp.arange(nb)[:, None, None] * BLOCK - BLOCK + c[None]
    valid = (dist >= 0) & (dist < WINDOW) & (kpos >= 0)
    logits = jnp.where(valid[None, :, None, None], logits + bias, -jnp.inf)
    sink = sinks.astype(f32).reshape(SWA_KV_HEADS, SWA_GROUP)[None, None, :, :, None, None]
    mx = jnp.maximum(logits.max(-1, keepdims=True), sink)
    p = jnp.exp(logits - mx)
    probs = (p / (p.sum(-1, keepdims=True) + jnp.exp(sink - mx))).astype(v.dtype)
    out = jnp.einsum('bnhgqk,bnkhd->bnqhgd', probs, vb)
    return out.reshape(B, S, SWA_WIDTH)


def hybrid_mixer(x, w_in, ml_conv_w, ml_conv_b, ml_i_bias, ml_f_bias, ml_norm_g, swa_sinks, rel_bias, w_out):
    B, S, _ = x.shape
    proj = x @ w_in
    ml_qk, ml_v, ml_o, ml_i, ml_f, sw_q, sw_k, sw_v = jnp.split(proj, IN_SPLITS, axis=-1)
    ml_qk = jax.nn.silu(causal_dwconv(ml_qk, ml_conv_w, ml_conv_b))
    ml_q, ml_k = jnp.split(ml_qk, 2, axis=-1)
    h_ml = mlstm(ml_q, ml_k, ml_v, ml_o, ml_i + ml_i_bias, ml_f + ml_f_bias, ml_norm_g)
    h_sw = sliding_window_attention(sw_q.reshape(B, S, SWA_HEADS, SWA_HEAD_DIM),
                                    sw_k.reshape(B, S, SWA_KV_HEADS, SWA_HEAD_DIM),
                                    sw_v.reshape(B, S, SWA_KV_HEADS, SWA_HEAD_DIM),
                                    swa_sinks, rel_bias)
    return jnp.concatenate([h_ml, h_sw], axis=-1) @ w_out


def memory_cross_attention(x, mem, wq, wkv, wo):
    B, S, _ = x.shape
    M = mem.shape[1]
    q = (x @ wq).reshape(B, S, XA_HEADS, XA_HEAD_DIM)
    k, v = jnp.split(mem @ wkv, 2, axis=-1)
    k = k.reshape(B, M, XA_HEADS, XA_HEAD_DIM)
    v = v.reshape(B, M, XA_HEADS, XA_HEAD_DIM)
    logits = jnp.einsum('bshd,bmhd->bhsm', q, k).astype(f32) * (XA_HEAD_DIM ** -0.5)
    p = jax.nn.softmax(logits, axis=-1).astype(x.dtype)
    o = jnp.einsum('bhsm,bmhd->bshd', p, v).reshape(B, S, D_MODEL)
    return o @ wo


def conv_ffn(x, w_up, conv_w, conv_b, w_down):
    u = causal_dwconv(x @ w_up, conv_w, conv_b)
    g, val = jnp.split(u, 2, axis=-1)
    return (jax.nn.gelu(g) * val) @ w_down


def setup_inputs(seed: int = 0) -> dict:
    key = jax.random.key(seed)
    ks = jax.random.split(key, 24)
    nrm = lambda k, shape, s: jax.random.normal(k, shape, f32) * s
    L = DEPTH
    return {
        'x': nrm(ks[0], (BATCH, SEQ, D_MODEL), 1.0),
        'mem': nrm(ks[1], (BATCH, N_MEM, D_MODEL), 1.0),
        'rel_bias': nrm(ks[2], (REL_BUCKETS, SWA_HEADS), 0.5),
        'w_in': nrm(ks[3], (L, D_MODEL, N_IN), D_MODEL ** -0.5),
        'ml_conv_w': nrm(ks[4], (L, ML_CONV, 2 * ML_WIDTH), ML_CONV ** -0.5),
        'ml_conv_b': nrm(ks[5], (L, 2 * ML_WIDTH), 0.02),
        'ml_i_bias': nrm(ks[6], (L, ML_HEADS), 0.1),
        'ml_f_bias': jnp.linspace(3.0, 6.0, ML_HEADS, dtype=f32)[None, :] + nrm(ks[7], (L, ML_HEADS), 0.1),
        'ml_norm_g': 1.0 + nrm(ks[8], (L, ML_WIDTH), 0.05),
        'swa_sinks': nrm(ks[9], (L, SWA_HEADS), 0.5),
        'w_out': nrm(ks[10], (L, MIX_WIDTH, D_MODEL), BETA * MIX_WIDTH ** -0.5),
        'ln1_g': 1.0 + nrm(ks[11], (L, D_MODEL), 0.05),
        'ln1_b': nrm(ks[12], (L, D_MODEL), 0.02),
        'xa_wq': nrm(ks[13], (L, D_MODEL, D_MODEL), D_MODEL ** -0.5),
        'xa_wkv': nrm(ks[14], (L, D_MODEL, 2 * D_MODEL), D_MODEL ** -0.5),
        'xa_wo': nrm(ks[15], (L, D_MODEL, D_MODEL), BETA * D_MODEL ** -0.5),
        'ln2_g': 1.0 + nrm(ks[16], (L, D_MODEL), 0.05),
        'ln2_b': nrm(ks[17], (L, D_MODEL), 0.02),
        'ffn_w_up': nrm(ks[18], (L, D_MODEL, 2 * D_FF), D_MODEL ** -0.5),
        'ffn_conv_w': nrm(ks[19], (L, FFN_CONV, 2 * D_FF), FFN_CONV ** -0.5),
        'ffn_conv_b': nrm(ks[20], (L, 2 * D_FF), 0.02),
        'ffn_w_down': nrm(ks[21], (L, D_FF, D_MODEL), BETA * D_FF ** -0.5),
        'ln3_g': 1.0 + nrm(ks[22], (L, D_MODEL), 0.05),
        'ln3_b': nrm(ks[23], (L, D_MODEL), 0.02),
    }


def reference(x, mem, rel_bias, w_in, ml_conv_w, ml_conv_b, ml_i_bias, ml_f_bias, ml_norm_g, swa_sinks, w_out, ln1_g, ln1_b, xa_wq, xa_wkv, xa_wo, ln2_g, ln2_b, ffn_w_up, ffn_conv_w, ffn_conv_b, ffn_w_down, ln3_g, ln3_b):
    for l in range(DEPTH):
        h = hybrid_mixer(x, w_in[l], ml_conv_w[l], ml_conv_b[l], ml_i_bias[l], ml_f_bias[l], ml_norm_g[l], swa_sinks[l], rel_bias, w_out[l])
        x = layer_norm(ALPHA * x + h, ln1_g[l], ln1_b[l])
        h = memory_cross_attention(x, mem, xa_wq[l], xa_wkv[l], xa_wo[l])
        x = layer_norm(ALPHA * x + h, ln2_g[l], ln2_b[l])
        h = conv_ffn(x, ffn_w_up[l], ffn_conv_w[l], ffn_conv_b[l], ffn_w_down[l])
        x = layer_norm(ALPHA * x + h, ln3_g[l], ln3_b[l])
    return x
```

```python
import math
from contextlib import ExitStack

import numpy as np
import concourse.bass as bass
import concourse.mybir as mybir
from concourse.bass_utils import run_bass_kernel_spmd

F32 = mybir.dt.float32
BF16 = mybir.dt.bfloat16
AF = mybir.ActivationFunctionType
ALU = mybir.AluOpType
AX = mybir.AxisListType

D = 1024
NIN = 2824
DFF = 2816
NMEM = 256
DEPTH = 2
ALPHA = (2.0 * DEPTH) ** 0.25
EPS = 1e-5
N_FM = 1664
N_TM = 1160
NEG = -30000.0


class TL:
    def __init__(self, sem, name):
        self.sem = sem
        self.cnt = 0
        self.name = name


class Res:
    __slots__ = ("w", "r", "key")

    def __init__(self, key):
        self.key = key
        self.w = {}
        self.r = {}


class K:
    def __init__(self, nc, es):
        self.nc = nc
        self.es = es
        self.eng = {"pe": nc.tensor, "act": nc.scalar, "dve": nc.vector, "pool": nc.gpsimd, "sp": nc.sync}
        self.etl = {k: self.new_tl("e_" + k) for k in self.eng}
        self.seen = {k: {} for k in self.eng}
        self.res = {}
        self.dtl = {}
        self.nsem = 5
        self.poison = {}
        self.hist = {}
        self.prog = {k_: [] for k_ in self.eng}

    def new_tl(self, name):
        sem = self.es.enter_context(self.nc.semaphore(name))
        return TL(sem, name)

    def R(self, *key):
        r = self.res.get(key)
        if r is None:
            r = Res(key)
            p = self.poison.get(key[0])
            if p:
                r.r = dict(p)
            self.res[key] = r
        return r

    def dma_tl(self, *key):
        t = self.dtl.get(key)
        if t is None:
            t = self.new_tl("d_" + "_".join(str(x) for x in key))
            self.dtl[key] = t
            self.nsem += 1
        return t

    def snapshot(self):
        s = {}
        for t in list(self.etl.values()) + list(self.dtl.values()):
            if t.cnt > 0:
                s[t] = t.cnt
        return s

    def poison_names(self, names, snap):
        for n in names:
            cur = self.poison.setdefault(n, {})
            for t, v in snap.items():
                cur[t] = max(cur.get(t, 0), v)
        for key, r in self.res.items():
            if key[0] in names:
                for t, v in snap.items():
                    r.r[t] = max(r.r.get(t, 0), v)

    def op(self, en, meth, *args, rd=(), wr=(), inc=True, dma=None, join=False, **kw):
        eng = self.eng[en]
        tl = self.etl[en]
        need = {}
        for r in rd:
            for t, v in r.w.items():
                if v > need.get(t, 0):
                    need[t] = v
        for r in wr:
            if not join:
                for t, v in r.w.items():
                    if v > need.get(t, 0):
                        need[t] = v
            for t, v in r.r.items():
                if v > need.get(t, 0):
                    need[t] = v
        seen = self.seen[en]
        for t, v in need.items():
            if t is tl and en == "pe":
                continue
            if seen.get(t, 0) >= v:
                continue
            assert v <= t.cnt, (en, meth, t.name, v, t.cnt)
            self.prog[en].append(("w", t.sem, v))
            seen[t] = v
            h = None
            if h is not None:
                snap = h.get(v)
                if snap:
                    for t2, v2 in snap.items():
                        if v2 > seen.get(t2, 0):
                            seen[t2] = v2
        incspec = None
        if dma is not None:
            dma.cnt += 16
            incspec = (dma.sem, 16)
            tok, val = dma, dma.cnt
        else:
            if inc:
                tl.cnt += 1
                incspec = (tl.sem, 1)
                val = tl.cnt
            else:
                assert en == "pe"
                val = tl.cnt + 1
            tok = tl
        self.prog[en].append(("i", meth, args, kw, incspec))
        for r in wr:
            if join:
                r.w[tok] = max(r.w.get(tok, 0), val)
            else:
                r.w = {tok: val}
                r.r = {}
        for r in rd:
            if val > r.r.get(tok, 0):
                r.r[tok] = val
        return None

    def finish(self):
        snap = self.snapshot()
        for en in ("sp", "act", "dve", "pool", "pe"):
            eng = self.eng[en]
            for t, v in snap.items():
                if t is self.etl[en]:
                    continue
                if self.seen[en].get(t, 0) >= v:
                    continue
                self.prog[en].append(("w", t.sem, v))
        self.emit()

    def emit(self):
        def mk(prog):
            def f(eng):
                for it in prog:
                    if it[0] == "w":
                        eng.wait_ge(it[1], it[2])
                    else:
                        inst = getattr(eng, it[1])(*it[2], **it[3])
                        if it[4] is not None:
                            inst.then_inc(it[4][0], it[4][1])
            return f
        with self.nc.Block() as block:
            block.sync(mk(self.prog["sp"]))
            block.scalar(mk(self.prog["act"]))
            block.vector(mk(self.prog["dve"]))
            block.gpsimd(mk(self.prog["pool"]))
            block.tensor(mk(self.prog["pe"]))


class Alloc:
    def __init__(self, nc):
        self.nc = nc
        self.base = (nc.sbuf_base + 63) // 64 * 64
        self.top = nc.sbuf_top
        self.cur = self.base
        self.n = 0

    def at(self, off, name, shape, dt):
        self.n += 1
        esz = 4 if dt == F32 else 2
        nbytes = int(np.prod(shape[1:])) * esz
        assert off % 32 == 0
        assert off + nbytes <= self.top, (name, off, nbytes, self.top)
        t = self.nc.alloc_sbuf_tensor_at(f"{name}_{self.n}", list(shape), dt, offset=off)
        return t, nbytes

    def new(self, name, shape, dt):
        t, nb = self.at(self.cur, name, shape, dt)
        self.cur += (nb + 63) // 64 * 64
        return t


class _Stop(Exception):
    pass


def build(SEQ=4096, SB=512, depth=DEPTH, stop=None, part=0, nparts=1):
    assert SB == 512
    NT = SB // 128
    NSB = SEQ // SB
    L = depth
    nc = bass.Bass("TRN2", target_bir_lowering=False)
    es = ExitStack()
    k = K(nc, es)
    al = Alloc(nc)

    def dram(name, shape, kind="ExternalInput", dt=F32):
        return nc.dram_tensor(name, list(shape), dt, kind=kind).ap()

    x_d = dram("x", [SEQ, D])
    mem_d = dram("mem", [NMEM, D])
    cst_d = dram("cst", [128, 3, 128])
    bias_d = dram("biasT", [128, 2, 8, 128])
    w_in_d = dram("w_in", [DEPTH, D, NIN])
    w_out_d = dram("w_out", [DEPTH, D, D])
    wq_d = dram("xa_wq", [DEPTH, D, D])
    wkv_d = dram("xa_wkv", [DEPTH, D, 2 * D])
    wxo_d = dram("xa_wo", [DEPTH, D, D])
    wup_d = dram("ffn_w_up", [DEPTH, D, 2 * DFF])
    wdn_d = dram("ffn_w_down", [DEPTH, DFF, D])
    pp_d = dram("pp", [DEPTH, 128, 216])
    bvs_d = dram("bvs", [DEPTH, 128, 528])
    lnv_d = dram("lnv", [DEPTH, 3, 128, 2048])
    out_d = dram("out", [SEQ, D], kind="ExternalOutput")

    cst = al.new("cst", [128, 3, 128], F32)
    ident_bf = al.new("identbf", [128, 128], BF16)
    biasT = al.new("biasT", [128, 2, 8, 128], BF16)
    pp = [al.new("pp", [128, 216], F32) for _ in range(L)]
    bvs = [al.new("bvs", [128, 528], F32) for _ in range(L)]
    esk = [al.new("esk", [128, 8], F32) for _ in range(L)]
    kTm = [al.new("kTm", [128, 8, NMEM], BF16) for _ in range(L)]
    vm = [al.new("vm", [128, 2, D], BF16) for _ in range(L)]
    Cst = [al.new("C", [128, 4, 129], F32) for _ in range(L)]
    Fc = [al.new("Fc", [128, 4], F32) for _ in range(L)]
    Mc = [al.new("Mc", [128, 4], F32) for _ in range(L)]
    hq = [al.new("hq", [128, 8, 3], F32) for _ in range(L)]
    hf = [al.new("hf", [128, 44, 2], F32) for _ in range(L)]
    skT = [[al.new("skT", [128, 128 + SB], BF16) for _ in range(2)] for _ in range(L)]
    sv = [al.new("sv", [128, 1 + NT, 2, 65], BF16) for _ in range(L)]
    v_ext = al.new("vext", [128, NT, 4, 129], BF16)
    corrq = al.new("corrq", [128, 8, 3], F32)
    corrf = al.new("corrf", [128, 44, 2], F32)
    tmpq = al.new("tmpq", [128, 44], F32)
    gwork = {n: al.new("g_" + n, [128, NT * 4], F32) for n in
             ("lf", "F", "d", "wk", "gam", "clamp", "P", "mrun", "mprev", "tmb")}
    gif = al.new("gif", [128, NT, 8], F32)
    smalls = al.new("smalls", [128, 64], F32)
    small_i = [0]
    Cb = [al.new("Cb", [128, 129], BF16) for _ in range(4)]
    kw_sb = [al.new("kw", [128, 128], BF16) for _ in range(4)]
    AT_sb = [al.new("AT", [128, 128], BF16) for _ in range(4)]
    hml = al.new("hml", [128, 4, 128], F32)
    PT_sb = [al.new("PT", [128, 4, 128], BF16) for _ in range(4)]
    accq = [al.new("accq", [128, 512], F32) for _ in range(2)]
    wfm = [al.new("wfm", [128, 8, 128], BF16) for _ in range(3)]
    wup = [al.new("wup", [128, 8, 128], BF16) for _ in range(4)]
    wvg = al.new("wvg", [128, 8, 136], BF16)
    lnv = [al.new("lnv", [128, 2048], F32) for _ in range(2)]
    x_sb = al.new("x", [128, NT, D], F32)
    xT = al.new("xT", [128, 8, SB], BF16)
    memT = al.new("memT", [128, 8, NMEM], BF16)
    r2 = al.cur
    wbig = []
    o = r2
    for i in range(2):
        t, nb = al.at(o, "wbig", [128, 8, D], BF16)
        wbig.append(t)
        o += nb
    r2_end_a = o
    o = r2
    hT, nb = al.at(o, "hT", [128, 22, SB], BF16); o += nb
    accg = []; accv = []; gact = []
    for i in range(2):
        t, nb = al.at(o, "accg", [128, 512], F32); accg.append(t); o += nb
        t, nb = al.at(o, "accv", [128, 512], F32); accv.append(t); o += nb
        t, nb = al.at(o, "gact", [128, 512], BF16); gact.append(t); o += nb
    al.cur = max(r2_end_a, o)
    R2_A = ["wbig"]
    R2_E = ["hT", "accg", "accv", "gact"]
    r1 = al.cur
    o = r1
    qT, nb = al.at(o, "qT", [128, 4, SB], BF16); o += nb
    kT, nb = al.at(o, "kT", [128, 4, SB], BF16); o += nb
    sqT, nb = al.at(o, "sqT", [128, 4, SB], BF16); o += nb
    og, nb = al.at(o, "og", [128, NT, 512], F32); o += nb
    hcat, nb = al.at(o, "hcat", [128, NT, D], BF16); o += nb
    hcT = []
    for i in range(2):
        t, nb = al.at(o, "hcT", [128, 8, 128], BF16); hcT.append(t); o += nb
    r1_end_ab = o
    o = r1
    qTx, nb = al.at(o, "qTx", [128, 8, SB], BF16); o += nb
    p_sb = []; pT_sb = []; oT_sb = []
    for i in range(2):
        t, nb = al.at(o, "p", [128, 4, NMEM], BF16); p_sb.append(t); o += nb
        t, nb = al.at(o, "pT", [128, 8, 128], BF16); pT_sb.append(t); o += nb
        t, nb = al.at(o, "oT", [128, 8, 128], BF16); oT_sb.append(t); o += nb
    r1_end_d = o
    o = r1
    wd, nb = al.at(o, "wd", [128, 22, D], BF16); o += nb
    r1_end_e = o
    al.cur = max(r1_end_ab, r1_end_d, r1_end_e)
    R1_AB = ["qT", "kT", "sqT", "og", "hcat", "hcT"]
    R1_D = ["qTx", "p", "pT", "oT"]
    R1_E = ["wd"]
    assert al.cur <= al.top, (al.cur, al.top)

    PS = [es.enter_context(nc.psum_tensor(f"ps{i}", [128, 1024], F32)) for i in range(4)]
    big_i = [0]
    small_i2 = [0]

    def ps_big():
        i = big_i[0] % 2
        big_i[0] += 1
        return PS[i], [k.R("ps", i, 0), k.R("ps", i, 1)]

    def ps_small():
        j = small_i2[0] % 4
        small_i2[0] += 1
        i, h = 2 + j // 2, j % 2
        return PS[i][:, h * 512:(h + 1) * 512], [k.R("ps", i, h)]

    def sm(n):
        if small_i[0] + n > 64:
            small_i[0] = 0
        a = small_i[0]
        small_i[0] += n
        return smalls[:, a:a + n], k.R("sm", a)

    ident = cst[:, 0, :]
    triU = cst[:, 1, :]
    ones = cst[:, 2, :]
    Rc = k.R("cst")

    states = []
    tile_off = part * (SEQ // 128)
    k.op("sp", "dma_start", out=cst[:], in_=cst_d[:, :, :], wr=[Rc], dma=k.dma_tl("cst"))
    k.op("dve", "tensor_copy", out=ident_bf[:], in_=ident, rd=[Rc], wr=[k.R("identbf")])
    Rib = k.R("identbf")
    k.op("pool", "dma_start", out=biasT[:], in_=bias_d[:, :, :, :], wr=[k.R("biasT")], dma=k.dma_tl("biasT"))
    for l in range(L):
        k.op("sp", "dma_start", out=pp[l][:], in_=pp_d[l], wr=[k.R("pp", l)], dma=k.dma_tl("pp", l))
        k.op("sp", "dma_start", out=bvs[l][:], in_=bvs_d[l], wr=[k.R("bvs", l)], dma=k.dma_tl("bvs", l))
        k.op("act", "activation", out=esk[l][:], in_=bvs[l][:, 512:520], func=AF.Exp,
             rd=[k.R("bvs", l)], wr=[k.R("esk", l)])
        k.op("dve", "memset", Cst[l][:], 0.0, wr=[k.R("C", l, h) for h in range(4)])
        k.op("dve", "memset", Fc[l][:], 0.0, wr=[k.R("Fc", l)])
        k.op("dve", "memset", Mc[l][:], 0.0, wr=[k.R("Mc", l)])
        k.op("dve", "memset", hq[l][:], 0.0, wr=[k.R("hq", l)])
        k.op("dve", "memset", hf[l][:], 0.0, wr=[k.R("hf", l)])
        for g in range(2):
            k.op("pool", "memset", skT[l][g][:], 0.0, wr=[k.R("sk", l, g, "p"), k.R("sk", l, g, "c")])
        k.op("pool", "memset", sv[l][:], 0.0, wr=[k.R("sv", l, n) for n in range(-1, NT)])
        k.op("pool", "memset", sv[l][:, :, :, 64:65], 1.0, wr=[k.R("sv", l, n) for n in range(-1, NT)])
        states += [
            (f"C{l}", Cst[l][:], [128, 4, 129], [k.R("C", l, h) for h in range(4)]),
            (f"Fc{l}", Fc[l][:], [128, 4], [k.R("Fc", l)]),
            (f"Mc{l}", Mc[l][:], [128, 4], [k.R("Mc", l)]),
            (f"hq{l}", hq[l][:], [128, 8, 3], [k.R("hq", l)]),
            (f"hf{l}", hf[l][:], [128, 44, 2], [k.R("hf", l)]),
            (f"sk0{l}", skT[l][0][:, 0:128], [128, 128], [k.R("sk", l, 0, "p")]),
            (f"sk1{l}", skT[l][1][:, 0:128], [128, 128], [k.R("sk", l, 1, "p")]),
            (f"sv{l}", sv[l][:, 0, :, :], [128, 2, 65], [k.R("sv", l, -1)]),
        ]
    k.op("pool", "memset", v_ext[:], 1.0, wr=[k.R("v", n) for n in range(NT)])
    if part > 0:
        stg = al.new("ststage", [128, 400], F32)
        so = 0
        for (nm, ap_, shp, rs) in states:
            d_ = dram("sti_" + nm, shp)
            if nm.startswith("sk") or nm.startswith("sv"):
                nel = int(np.prod(shp[1:]))
                if so + nel > 400:
                    so = 0
                st_ap = stg[:, so:so + nel]
                Rs = k.R("ststage", so)
                so += nel
                src = d_ if len(shp) == 2 else d_.rearrange("p a b -> p (a b)")
                k.op("sp", "dma_start", out=st_ap, in_=src, wr=[Rs], dma=k.dma_tl("st", nm))
                dst = ap_ if len(shp) == 2 else ap_
                if len(shp) == 3:
                    st_ap = st_ap.rearrange("p (a b) -> p a b", a=shp[1])
                k.op("dve", "tensor_copy", out=dst, in_=st_ap, rd=[Rs], wr=rs)
            else:
                k.op("sp", "dma_start", out=ap_, in_=d_, wr=rs, dma=k.dma_tl("st", nm))
    for mt in range(2):
        k.op("sp", "dma_start", out=x_sb[:, mt, :], in_=mem_d[mt * 128:(mt + 1) * 128, :],
             wr=[k.R("x", mt)], dma=k.dma_tl("xin", mt))
        ps, pr = ps_big()
        for kc in range(8):
            k.op("pe", "transpose", ps[:, kc * 128:(kc + 1) * 128], x_sb[:, mt, kc * 128:(kc + 1) * 128], ident,
                 rd=[k.R("x", mt), Rc], wr=[pr[kc // 4]], inc=(kc % 4 == 3), join=(kc % 4 != 0))
        k.op("act", "activation", out=memT[:, :, mt * 128:(mt + 1) * 128],
             in_=ps[:].rearrange("p (c t) -> p c t", c=8), func=AF.Copy, rd=pr, wr=[k.R("memT", mt)])
    for l in range(L):
        for c4 in range(4):
            slot = c4 % 2
            k.op("pool", "dma_start", out=wbig[slot][:, :, 0:512],
                 in_=wkv_d[l].rearrange("(kc p) n -> p kc n", p=128)[:, :, c4 * 512:(c4 + 1) * 512],
                 wr=[k.R("wbig", slot)], dma=k.dma_tl("wbig", slot))
            if c4 < 2:
                for dcl in range(4):
                    dc = c4 * 4 + dcl
                    ps, pr = ps_small()
                    for kc in range(8):
                        k.op("pe", "matmul", ps[:, 0:NMEM], lhsT=wbig[slot][:, kc, dcl * 128:(dcl + 1) * 128],
                             rhs=memT[:, kc, :], start=(kc == 0), stop=(kc == 7),
                             rd=[k.R("wbig", slot), k.R("memT", 0), k.R("memT", 1)], wr=pr,
                             inc=(kc == 7), join=(kc != 0))
                    k.op("act", "activation", out=kTm[l][:, dc, :], in_=ps[:, 0:NMEM], func=AF.Copy,
                         rd=pr, wr=[k.R("kTm", l, dc)])
            else:
                hh = c4 - 2
                for mt in range(2):
                    ps, pr = ps_small()
                    for kc in range(8):
                        k.op("pe", "matmul", ps[:, 0:512], lhsT=memT[:, kc, mt * 128:(mt + 1) * 128],
                             rhs=wbig[slot][:, kc, 0:512], start=(kc == 0), stop=(kc == 7),
                             rd=[k.R("wbig", slot), k.R("memT", mt)], wr=pr, inc=(kc == 7), join=(kc != 0))
                    k.op("act", "activation", out=vm[l][:, mt, hh * 512:(hh + 1) * 512], in_=ps[:, 0:512],
                         func=AF.Copy, rd=pr, wr=[k.R("vm", l, mt, hh)])

    lnv_i = [0]

    def load_lnv(l, j):
        s = lnv_i[0] % 2
        lnv_i[0] += 1
        k.op("sp", "dma_start", out=lnv[s][:], in_=lnv_d[l, j], wr=[k.R("lnv", s)], dma=k.dma_tl("lnv", s))
        return s

    def x_transposes(n):
        ps, pr = ps_big()
        for kc in range(8):
            k.op("pe", "transpose", ps[:, kc * 128:(kc + 1) * 128], x_sb[:, n, kc * 128:(kc + 1) * 128], ident,
                 rd=[k.R("x", n), Rc], wr=[pr[kc // 4]], inc=(kc % 4 == 3), join=(kc % 4 != 0))
        k.op("act", "activation", out=xT[:, :, n * 128:(n + 1) * 128],
             in_=ps[:].rearrange("p (c t) -> p c t", c=8), func=AF.Copy, rd=pr, wr=[k.R("xT", n)])

    def residual_ln(n, ps, pr, ls, last=False):
        Rx = k.R("x", n)
        xa = x_sb[:, n, :]
        k.op("dve", "scalar_tensor_tensor", out=xa, in0=xa, scalar=float(ALPHA), in1=ps[:, :],
             op0=ALU.mult, op1=ALU.add, rd=pr, wr=[Rx])
        st, Rst = sm(12)
        k.op("dve", "bn_stats", out=st[:, 0:6], in_=x_sb[:, n, 0:512], rd=[Rx], wr=[Rst])
        k.op("dve", "bn_stats", out=st[:, 6:12], in_=x_sb[:, n, 512:1024], rd=[Rx], wr=[Rst], join=True)
        mv, Rmv = sm(4)
        k.op("dve", "bn_aggr", out=mv[:, 0:2], in_=st, rd=[Rst], wr=[Rmv])
        k.op("dve", "tensor_scalar", out=mv[:, 2:3], in0=mv[:, 1:2], scalar1=float(EPS), scalar2=None,
             op0=ALU.add, rd=[Rmv], wr=[Rmv])
        k.op("act", "activation", out=mv[:, 2:3], in_=mv[:, 2:3], func=AF.Sqrt, rd=[Rmv], wr=[Rmv])
        k.op("dve", "reciprocal", out=mv[:, 2:3], in_=mv[:, 2:3], rd=[Rmv], wr=[Rmv])
        k.op("dve", "tensor_scalar", out=mv[:, 3:4], in0=mv[:, 0:1], scalar1=mv[:, 2:3], scalar2=-1.0,
             op0=ALU.mult, op1=ALU.mult, rd=[Rmv], wr=[Rmv])
        k.op("act", "activation", out=xa, in_=xa, func=AF.Identity, scale=mv[:, 2:3], bias=mv[:, 3:4],
             rd=[Rmv, Rx], wr=[Rx])
        k.op("pool", "tensor_tensor", out=xa, in0=xa, in1=lnv[ls][:, 0:1024], op=ALU.mult,
             rd=[Rx, k.R("lnv", ls)], wr=[Rx])
        k.op("pool", "tensor_tensor", out=xa, in0=xa, in1=lnv[ls][:, 1024:2048], op=ALU.add,
             rd=[Rx, k.R("lnv", ls)], wr=[Rx])

    def proj_tm(n, w_t, Rw, lhs_t, Rl):
        ps, pr = ps_big()
        nk = 8
        for half in range(2):
            for kc in range(nk):
                k.op("pe", "matmul", ps[:, half * 512:(half + 1) * 512], lhsT=lhs_t[:, kc, :],
                     rhs=w_t[:, kc, half * 512:(half + 1) * 512], start=(kc == 0), stop=(kc == nk - 1),
                     rd=[Rw, Rl], wr=[pr[half]], inc=(kc == nk - 1), join=(kc != 0))
        return ps, pr

    wfm_i = [0]
    wbig_i = [0]

    def load_wfm(src_ap):
        s = wfm_i[0] % 3
        wfm_i[0] += 1
        k.op("pool", "dma_start", out=wfm[s][:], in_=src_ap, wr=[k.R("wfm", s)], dma=k.dma_tl("wfm", s))
        return s

    def dump_x(t0):
        for n in range(NT):
            k.op("sp", "dma_start", out=out_d[t0 + n * 128:t0 + (n + 1) * 128, :], in_=x_sb[:, n, :],
                 rd=[k.R("x", n)], dma=k.dma_tl("xout", n))

    def chk(tag):
        if stop == tag:
            dump_x(0)
            raise _Stop()

    try:
        chk("pro")
        for sbi in range(NSB):
            t0 = sbi * SB
            first_sb = (sbi == 0)
            chk('sb%d' % sbi)
            for n in range(NT):
                k.op("sp", "dma_start", out=x_sb[:, n, :], in_=x_d[t0 + n * 128:t0 + (n + 1) * 128, :],
                     wr=[k.R("x", n)], dma=k.dma_tl("xin", n))
            for n in range(NT):
                x_transposes(n)
            chk('X')
            done = False
            for l in range(L):
                Rpp = k.R("pp", l)
                Rbv = k.R("bvs", l)
                cw = pp[l][:, 0:32].rearrange("p (c j) -> p c j", j=4)
                cb = pp[l][:, 32:40]
                fw = pp[l][:, 40:172].rearrange("p (c j) -> p c j", j=3)
                fb = pp[l][:, 172:216]
                win = w_in_d[l].rearrange("(kc p) n -> p kc n", p=128)
                sv_slot = 0
                k.op("pool", "dma_start", out=wbig[sv_slot][:], in_=win[:, :, N_FM:N_FM + 1024],
                     wr=[k.R("wbig", sv_slot)], dma=k.dma_tl("wbig", sv_slot))
                k.op("pool", "dma_start", out=wvg[:], in_=win[:, :, N_FM + 1024:N_FM + 1160],
                     wr=[k.R("wvg")], dma=k.dma_tl("wvg"))
                k.op("pool", "dma_start", out=wbig[1][:], in_=w_out_d[l].rearrange("(kc p) n -> p kc n", p=128),
                     wr=[k.R("wbig", 1)], dma=k.dma_tl("wbig", 1))
                Rhq = k.R("hq", l)
                Rcq = k.R("corrq")
                h0, h1, h2 = hq[l][:, :, 0], hq[l][:, :, 1], hq[l][:, :, 2]
                w0, w1, w2 = cw[:, :, 0], cw[:, :, 1], cw[:, :, 2]
                tq = tmpq[:, 0:8]
                Rtq = k.R("tmpq")

                def tt(out, a, b, op, rd, wr):
                    k.op("dve", "tensor_tensor", out=out, in0=a, in1=b, op=op, rd=rd, wr=wr)
                tt(corrq[:, :, 0], h2, w2, ALU.mult, [Rhq, Rpp], [Rcq])
                tt(tq, h1, w1, ALU.mult, [Rhq, Rpp], [Rtq])
                tt(corrq[:, :, 0], corrq[:, :, 0], tq, ALU.add, [Rcq, Rtq], [Rcq])
                tt(tq, h0, w0, ALU.mult, [Rhq, Rpp], [Rtq])
                tt(corrq[:, :, 0], corrq[:, :, 0], tq, ALU.add, [Rcq, Rtq], [Rcq])
                tt(corrq[:, :, 1], h2, w1, ALU.mult, [Rhq, Rpp], [Rcq])
                tt(tq, h1, w0, ALU.mult, [Rhq, Rpp], [Rtq])
                tt(corrq[:, :, 1], corrq[:, :, 1], tq, ALU.add, [Rcq, Rtq], [Rcq])
                tt(corrq[:, :, 2], h2, w0, ALU.mult, [Rhq, Rpp], [Rcq])
                chk('A0')
                slots = {}
                slots[0] = load_wfm(win[:, :, 0:128])
                slots[1] = load_wfm(win[:, :, 128:256])
                RxT = [k.R("xT", n) for n in range(NT)]
                for c in range(13):
                    if c + 2 < 13:
                        slots[c + 2] = load_wfm(win[:, :, (c + 2) * 128:(c + 3) * 128])
                    s = slots[c]
                    ps, pr = ps_small()
                    for kc in range(8):
                        k.op("pe", "matmul", ps[:, 0:SB], lhsT=wfm[s][:, kc, :], rhs=xT[:, kc, :],
                             start=(kc == 0), stop=(kc == 7), rd=[k.R("wfm", s)] + RxT, wr=pr,
                             inc=(kc == 7), join=(kc != 0))
                    if c < 8:
                        dst = qT if c < 4 else kT
                        Rd = k.R("qT" if c < 4 else "kT", c % 4)
                        a = c % 2
                        Ra = k.R("accq", a)
                        acc = accq[a]
                        k.op("act", "activation", out=acc[:, :], in_=ps[:, 0:SB], func=AF.Identity,
                             scale=cw[:, c, 3:4], bias=cb[:, c:c + 1], rd=pr + [Rpp], wr=[Ra])
                        for j, sh in ((2, 1), (1, 2), (0, 3)):
                            k.op("dve", "scalar_tensor_tensor", out=acc[:, sh:SB], in0=ps[:, 0:SB - sh],
                                 scalar=cw[:, c, j:j + 1], in1=acc[:, sh:SB], op0=ALU.mult, op1=ALU.add,
                                 rd=pr + [Rpp, Ra], wr=[Ra])
                        k.op("dve", "tensor_tensor", out=acc[:, 0:3], in0=acc[:, 0:3], in1=corrq[:, c, :],
                             op=ALU.add, rd=[Ra, Rcq], wr=[Ra])
                        k.op("act", "activation", out=hq[l][:, c, :], in_=ps[:, SB - 3:SB], func=AF.Copy,
                             rd=pr, wr=[Rhq])
                        k.op("act", "activation", out=dst[:, c % 4, :], in_=acc[:, :], func=AF.Silu,
                             rd=[Ra], wr=[Rd])
                    elif c < 12:
                        k.op("act", "activation", out=sqT[:, c - 8, :], in_=ps[:, 0:SB], func=AF.Copy, scale=0.125,
                             rd=pr, wr=[k.R("sqT", c - 8)])
                    else:
                        k.op("act", "activation", out=skT[l][0][0:64, 128:128 + SB], in_=ps[0:64, 0:SB],
                             func=AF.Copy, rd=pr, wr=[k.R("sk", l, 0, "c")])
                        k.op("dve", "tensor_copy", out=skT[l][1][64:128, 128:128 + SB], in_=ps[64:128, 0:SB],
                             rd=pr, wr=[k.R("sk", l, 1, "c")])
                chk('A1')
                Rwv = k.R("wbig", sv_slot)
                for n in range(NT):
                    ps, pr = ps_small()
                    for kc in range(8):
                        k.op("pe", "matmul", ps[:, 0:512], lhsT=xT[:, kc, n * 128:(n + 1) * 128],
                             rhs=wbig[sv_slot][:, kc, 0:512], start=(kc == 0), stop=(kc == 7),
                             rd=[Rwv, RxT[n]], wr=pr, inc=(kc == 7), join=(kc != 0))
                    k.op("act", "activation", out=v_ext[:, n, :, 0:128],
                         in_=ps[:, 0:512].rearrange("p (h d) -> p h d", h=4), func=AF.Copy, rd=pr, wr=[k.R("v", n)])
                    chk('A2v')
                    ps, pr = ps_small()
                    for kc in range(8):
                        k.op("pe", "matmul", ps[:, 0:512], lhsT=xT[:, kc, n * 128:(n + 1) * 128],
                             rhs=wbig[sv_slot][:, kc, 512:1024], start=(kc == 0), stop=(kc == 7),
                             rd=[Rwv, RxT[n]], wr=pr, inc=(kc == 7), join=(kc != 0))
                    k.op("act", "activation", out=og[:, n, :], in_=ps[:, 0:512], func=AF.Sigmoid, rd=pr,
                         wr=[k.R("og", n)])
                    chk('A2o')
                    ps, pr = ps_small()
                    for kc in range(8):
                        k.op("pe", "matmul", ps[:, 0:136], lhsT=xT[:, kc, n * 128:(n + 1) * 128],
                             rhs=wvg[:, kc, :], start=(kc == 0), stop=(kc == 7),
                             rd=[k.R("wvg"), RxT[n]], wr=pr, inc=(kc == 7), join=(kc != 0))
                    k.op("act", "activation", out=sv[l][:, 1 + n, :, 0:64],
                         in_=ps[:, 0:128].rearrange("p (h d) -> p h d", h=2), func=AF.Copy, rd=pr,
                         wr=[k.R("sv", l, n)])
                    chk('A2s%d' % n)
                    k.op("act", "activation", out=gif[:, n, :], in_=ps[:, 128:136], func=AF.Copy, rd=pr, wr=[k.R("gif", n)])
                    chk('A2g%d' % n)
                chk('A')
                Rg = k.R("gates")
                Rgif = [k.R("gif", n) for n in range(NT)]
                G = gwork
                NG4 = NT * 4

                def v3(t):
                    return t[:, :].rearrange("p (n h) -> p n h", h=4)
                ib = bvs[l][:, 520:524]
                fbias = bvs[l][:, 524:528]
                for n in range(NT):
                    k.op("dve", "tensor_tensor", out=v3(G["lf"])[:, n, :], in0=gif[:, n, 4:8], in1=fbias, op=ALU.add,
                         rd=Rgif + [Rbv], wr=[Rg])
                    k.op("dve", "tensor_tensor", out=v3(G["d"])[:, n, :], in0=gif[:, n, 0:4], in1=ib, op=ALU.add,
                         rd=Rgif + [Rbv], wr=[Rg])
                k.op("act", "activation", out=G["lf"][:, :], in_=G["lf"][:, :], func=AF.Exp, scale=-1.0, rd=[Rg], wr=[Rg])
                k.op("act", "activation", out=G["lf"][:, :], in_=G["lf"][:, :], func=AF.Ln, bias=1.0, rd=[Rg], wr=[Rg])
                k.op("dve", "tensor_scalar", out=G["lf"][:, :], in0=G["lf"][:, :], scalar1=-1.0, scalar2=None,
                     op0=ALU.mult, rd=[Rg], wr=[Rg])
                ps, pr = ps_small()
                k.op("pe", "matmul", ps[:, 0:NG4], lhsT=triU, rhs=G["lf"][:, :], start=True, stop=True,
                     rd=[Rc, Rg], wr=pr)
                ps2, pr2 = ps_small()
                k.op("pe", "matmul", ps2[:, 0:NG4], lhsT=ones, rhs=G["lf"][:, :], start=True, stop=True,
                     rd=[Rc, Rg], wr=pr2)
                k.op("dve", "tensor_copy", out=G["tmb"][:, :], in_=ps2[:, 0:NG4], rd=pr2, wr=[Rg])
                RF = k.R("Fc", l)
                P3 = v3(G["P"])
                T3 = v3(G["tmb"])
                k.op("dve", "tensor_copy", out=P3[:, 0, :], in_=Fc[l][:, :], rd=[RF], wr=[Rg])
                for n in range(1, NT):
                    k.op("dve", "tensor_tensor", out=P3[:, n, :], in0=P3[:, n - 1, :], in1=T3[:, n - 1, :], op=ALU.add,
                         rd=[Rg], wr=[Rg])
                k.op("dve", "tensor_tensor", out=Fc[l][:, :], in0=P3[:, NT - 1, :], in1=T3[:, NT - 1, :], op=ALU.add,
                     rd=[Rg], wr=[RF])
                k.op("dve", "tensor_tensor", out=G["F"][:, :], in0=G["P"][:, :], in1=ps[:, 0:NG4], op=ALU.add,
                     rd=[Rg] + pr, wr=[Rg])
                k.op("dve", "tensor_tensor", out=G["d"][:, :], in0=G["d"][:, :], in1=G["F"][:, :], op=ALU.subtract,
                     rd=[Rg], wr=[Rg])
                ps, pr = ps_small()
                k.op("pe", "transpose", ps[0:NG4, 0:128], G["d"][:, :], ident, rd=[Rc, Rg], wr=pr)
                tm, Rtm = sm(1)
                k.op("dve", "tensor_reduce", out=tm[0:NG4, :], in_=ps[0:NG4, 0:128], axis=AX.X, op=ALU.max,
                     rd=pr, wr=[Rtm])
                dg, Rdg = sm(NG4)
                k.op("dve", "tensor_scalar", out=dg[0:NG4, :], in0=ident[0:NG4, 0:NG4], scalar1=tm[0:NG4, 0:1],
                     scalar2=None, op0=ALU.mult, rd=[Rtm, Rc], wr=[Rdg])
                ps, pr = ps_small()
                k.op("pe", "matmul", ps[:, 0:NG4], lhsT=ones[0:NG4, :], rhs=dg[0:NG4, :], start=True, stop=True,
                     rd=[Rc, Rdg], wr=pr)
                RM = k.R("Mc", l)
                M3 = v3(G["mrun"])
                MP3 = v3(G["mprev"])
                pst = ps[:, 0:NG4].rearrange("p (n h) -> p n h", h=4)
                k.op("dve", "tensor_copy", out=MP3[:, 0, :], in_=Mc[l][:, :], rd=[RM], wr=[Rg])
                for n in range(NT):
                    k.op("dve", "tensor_tensor", out=M3[:, n, :], in0=MP3[:, n, :], in1=pst[:, n, :], op=ALU.max,
                         rd=[Rg] + pr, wr=[Rg])
                    if n + 1 < NT:
                        k.op("dve", "tensor_copy", out=MP3[:, n + 1, :], in_=M3[:, n, :], rd=[Rg], wr=[Rg])
                k.op("dve", "tensor_copy", out=Mc[l][:, :], in_=M3[:, NT - 1, :], rd=[Rg], wr=[RM])
                k.op("dve", "tensor_tensor", out=G["wk"][:, :], in0=G["d"][:, :], in1=G["mrun"][:, :], op=ALU.subtract,
                     rd=[Rg], wr=[Rg])
                k.op("act", "activation", out=G["wk"][:, :], in_=G["wk"][:, :], func=AF.Exp, rd=[Rg], wr=[Rg])
                k.op("dve", "tensor_scalar", out=G["wk"][:, :], in0=G["wk"][:, :], scalar1=float(128 ** -0.5),
                     scalar2=None, op0=ALU.mult, rd=[Rg], wr=[Rg])
                k.op("dve", "tensor_tensor", out=G["gam"][:, :], in0=G["mprev"][:, :], in1=G["mrun"][:, :],
                     op=ALU.subtract, rd=[Rg], wr=[Rg])
                k.op("act", "activation", out=G["gam"][:, :], in_=G["gam"][:, :], func=AF.Exp, rd=[Rg], wr=[Rg])
                k.op("dve", "tensor_tensor", out=G["clamp"][:, :], in0=G["F"][:, :], in1=G["mrun"][:, :], op=ALU.add,
                     rd=[Rg], wr=[Rg])
                k.op("act", "activation", out=G["clamp"][:, :], in_=G["clamp"][:, :], func=AF.Exp, scale=-1.0,
                     rd=[Rg], wr=[Rg])
                chk('B1')
                ls = load_lnv(l, 0)
                Rwo = k.R("wbig", 1)
                for n in range(NT):
                    tsl = slice(n * 128, (n + 1) * 128)
                    pst_, prt = ps_small()
                    pst_bf = pst_.bitcast(BF16)
                    pss, prs = ps_small()
                    for h in range(4):
                        k.op("pe", "transpose", pst_bf[:, h * 128:(h + 1) * 128], kT[:, h, tsl], ident_bf[:],
                             rd=[k.R("kT", h), Rib], wr=prt, inc=(h == 3), join=(h != 0))
                    for h in range(4):
                        k.op("pe", "matmul", pss[:, h * 128:(h + 1) * 128], lhsT=kT[:, h, tsl], rhs=qT[:, h, tsl],
                             start=True, stop=True, rd=[k.R("kT", h), k.R("qT", h)], wr=prs, inc=(h == 3),
                             join=(h != 0))
                    for h in range(4):
                        wkc = G["wk"][:, n * 4 + h:n * 4 + h + 1]
                        k.op("act", "activation", out=kw_sb[h][:, :], in_=pst_bf[:, h * 128:(h + 1) * 128],
                             func=AF.Copy, scale=wkc, rd=prt + [Rg], wr=[k.R("kw", h)])
                        k.op("dve", "scalar_tensor_tensor", out=AT_sb[h][:, :], in0=pss[:, h * 128:(h + 1) * 128],
                             scalar=wkc, in1=triU, op0=ALU.mult, op1=ALU.mult, rd=prs + [Rg, Rc],
                             wr=[k.R("AT", h)])
                        gmc = G["gam"][:, n * 4 + h:n * 4 + h + 1]
                        k.op("act", "activation", out=Cb[h][:, :], in_=Cst[l][:, h, :], func=AF.Copy, scale=gmc,
                             rd=[k.R("C", l, h), Rg], wr=[k.R("Cb", h)])
                    for h in range(4):
                        gmc = G["gam"][:, n * 4 + h:n * 4 + h + 1]
                        psn, prn = ps_small()
                        k.op("pe", "matmul", psn[:, 0:129], lhsT=AT_sb[h][:, :], rhs=v_ext[:, n, h, :],
                             start=True, stop=False, rd=[k.R("AT", h), k.R("v", n)], wr=prn, inc=False)
                        k.op("pe", "matmul", psn[:, 0:129], lhsT=qT[:, h, tsl], rhs=Cb[h][:, :],
                             start=False, stop=True, rd=[k.R("qT", h), k.R("Cb", h)], wr=prn, join=True)
                        psc, prc = ps_small()
                        k.op("pe", "matmul", psc[:, 0:129], lhsT=kw_sb[h][:, :], rhs=v_ext[:, n, h, :],
                             start=True, stop=True, rd=[k.R("kw", h), k.R("v", n)], wr=prc)
                        k.op("dve", "scalar_tensor_tensor", out=Cst[l][:, h, :], in0=Cst[l][:, h, :], scalar=gmc,
                             in1=psc[:, 0:129], op0=ALU.mult, op1=ALU.add, rd=prc + [Rg], wr=[k.R("C", l, h)])
                        dn, Rdn = sm(2)
                        k.op("act", "activation", out=dn[:, 0:1], in_=psn[:, 128:129], func=AF.Abs, rd=prn, wr=[Rdn])
                        k.op("dve", "tensor_tensor", out=dn[:, 0:1], in0=dn[:, 0:1],
                             in1=G["clamp"][:, n * 4 + h:n * 4 + h + 1], op=ALU.max, rd=[Rdn, Rg], wr=[Rdn])
                        k.op("dve", "reciprocal", out=dn[:, 1:2], in_=dn[:, 0:1], rd=[Rdn], wr=[Rdn])
                        k.op("act", "activation", out=hml[:, h, :], in_=psn[:, 0:128], func=AF.Copy, scale=dn[:, 1:2],
                             rd=prn + [Rdn], wr=[k.R("hml", h)])
                        st, Rst = sm(6)
                        k.op("dve", "bn_stats", out=st, in_=hml[:, h, :], rd=[k.R("hml", h)], wr=[Rst])
                        mv, Rmv = sm(4)
                        k.op("dve", "bn_aggr", out=mv[:, 0:2], in_=st, rd=[Rst], wr=[Rmv])
                        k.op("dve", "tensor_scalar", out=mv[:, 2:3], in0=mv[:, 1:2], scalar1=float(EPS), scalar2=None,
                             op0=ALU.add, rd=[Rmv], wr=[Rmv])
                        k.op("act", "activation", out=mv[:, 2:3], in_=mv[:, 2:3], func=AF.Sqrt, rd=[Rmv], wr=[Rmv])
                        k.op("dve", "reciprocal", out=mv[:, 2:3], in_=mv[:, 2:3], rd=[Rmv], wr=[Rmv])
                        k.op("dve", "tensor_scalar", out=hml[:, h, :], in0=hml[:, h, :], scalar1=mv[:, 0:1],
                             scalar2=mv[:, 2:3], op0=ALU.subtract, op1=ALU.mult, rd=[Rmv, k.R("hml", h)],
                             wr=[k.R("hml", h)])
                    Rh = [k.R("hml", h) for h in range(4)]
                    hml2 = hml[:].rearrange("p h d -> p (h d)")
                    k.op("pool", "tensor_tensor", out=hml2, in0=hml2, in1=bvs[l][:, 0:512], op=ALU.mult,
                         rd=Rh + [Rbv], wr=Rh)
                    k.op("pool", "tensor_tensor", out=hcat[:, n, 0:512], in0=hml2, in1=og[:, n, :], op=ALU.mult,
                         rd=Rh + [k.R("og", n)], wr=[k.R("hcat", n, 0)])
                    absn = tile_off + sbi * NT + n
                    jl = [1] if absn == 0 else [0, 1]
                    for g in range(2):
                        pso, pro = ps_small()
                        for j in jl:
                            psl, prl = ps_small()
                            kcol = slice((n + j) * 128, (n + j + 1) * 128)
                            Rk = k.R("sk", l, g, "c") if (n + j) >= 1 else k.R("sk", l, g, "p")
                            k.op("pe", "matmul", psl[:, 0:512], lhsT=skT[l][g][:, kcol], rhs=sqT[:, :, tsl],
                                 start=True, stop=False, rd=[Rk] + [k.R("sqT", c) for c in range(4)], wr=prl,
                                 inc=False)
                            k.op("pe", "matmul", psl[:, 0:512], lhsT=ident_bf[:], rhs=biasT[:, j, g * 4:(g + 1) * 4, :],
                                 start=False, stop=True, rd=[Rib, k.R("biasT")], wr=prl, join=True)
                            pi = 2 * g + j
                            k.op("act", "activation", out=PT_sb[pi][:].rearrange("p c q -> p (c q)"), in_=psl[:, 0:512],
                                 func=AF.Exp, rd=prl, wr=[k.R("PT", pi)])
                        for c in range(4):
                            for j in jl:
                                pi = 2 * g + j
                                k.op("pe", "matmul", pso[:, c * 65:(c + 1) * 65], lhsT=PT_sb[pi][:, c, :],
                                     rhs=sv[l][:, n + j, g, :], start=(j == jl[0]), stop=(j == 1),
                                     rd=[k.R("PT", pi), k.R("sv", l, n + j - 1)], wr=pro, inc=(c == 3 and j == 1),
                                     join=not (c == 0 and j == jl[0]))
                        den, Rden = sm(8)
                        po3 = pso[:, 0:260].rearrange("p (c e) -> p c e", e=65)
                        k.op("dve", "tensor_tensor", out=den[:, 0:4], in0=po3[:, :, 64], in1=esk[l][:, g * 4:(g + 1) * 4],
                             op=ALU.add, rd=pro + [k.R("esk", l)], wr=[Rden])
                        k.op("dve", "reciprocal", out=den[:, 4:8], in_=den[:, 0:4], rd=[Rden], wr=[Rden])
                        for c in range(4):
                            hh = g * 4 + c
                            eng = "act" if c % 2 == 0 else "dve"
                            if eng == "act":
                                k.op("act", "activation", out=hcat[:, n, 512 + hh * 64:512 + (hh + 1) * 64],
                                     in_=po3[:, c, 0:64], func=AF.Copy, scale=den[:, 4 + c:5 + c],
                                     rd=pro + [Rden], wr=[k.R("hcat", n, 1 + hh)])
                            else:
                                k.op("dve", "tensor_scalar", out=hcat[:, n, 512 + hh * 64:512 + (hh + 1) * 64],
                                     in0=po3[:, c, 0:64], scalar1=den[:, 4 + c:5 + c], scalar2=None, op0=ALU.mult,
                                     rd=pro + [Rden], wr=[k.R("hcat", n, 1 + hh)])
                    Rhc = [k.R("hcat", n, i) for i in range(9)]
                    hs = n % 2
                    psT, prT = ps_small()
                    psT_bf = psT.bitcast(BF16)
                    for kc in range(8):
                        k.op("pe", "transpose", psT_bf[:, kc * 128:(kc + 1) * 128], hcat[:, n, kc * 128:(kc + 1) * 128],
                             ident_bf[:], rd=Rhc + [Rib], wr=prT, inc=(kc == 7), join=(kc != 0))
                    k.op("dve", "tensor_copy", out=hcT[hs][:].rearrange("p c t -> p (c t)"), in_=psT_bf[:, 0:1024],
                         rd=prT, wr=[k.R("hcT", hs)])
                    ps, pr = proj_tm(n, wbig[1], Rwo, hcT[hs], k.R("hcT", hs))
                    residual_ln(n, ps, pr, ls)
                for g in range(2):
                    k.op("pool", "tensor_copy", out=skT[l][g][:, 0:128], in_=skT[l][g][:, SB:SB + 128],
                         rd=[k.R("sk", l, g, "c")], wr=[k.R("sk", l, g, "p")])
                k.op("pool", "tensor_copy", out=sv[l][:, 0, :, :], in_=sv[l][:, NT, :, :],
                     rd=[k.R("sv", l, NT - 1)], wr=[k.R("sv", l, -1)])
                if stop == (l, 1):
                    dump_x(t0); done = True; break
                k.poison_names(R1_D, k.snapshot())
                k.op("pool", "dma_start", out=wbig[0][:], in_=wxo_d[l].rearrange("(kc p) n -> p kc n", p=128),
                     wr=[k.R("wbig", 0)], dma=k.dma_tl("wbig", 0))
                for n in range(NT):
                    x_transposes(n)
                RxT = [k.R("xT", n) for n in range(NT)]
                wqv = wq_d[l].rearrange("(kc p) n -> p kc n", p=128)
                slots = {0: load_wfm(wqv[:, :, 0:128]), 1: load_wfm(wqv[:, :, 128:256])}
                for dc in range(8):
                    if dc + 2 < 8:
                        slots[dc + 2] = load_wfm(wqv[:, :, (dc + 2) * 128:(dc + 3) * 128])
                    s = slots[dc]
                    ps, pr = ps_small()
                    for kc in range(8):
                        k.op("pe", "matmul", ps[:, 0:SB], lhsT=wfm[s][:, kc, :], rhs=xT[:, kc, :],
                             start=(kc == 0), stop=(kc == 7), rd=[k.R("wfm", s)] + RxT, wr=pr,
                             inc=(kc == 7), join=(kc != 0))
                    k.op("act", "activation", out=qTx[:, dc, :], in_=ps[:, 0:SB], func=AF.Copy, scale=1.0 / 16.0,
                         rd=pr, wr=[k.R("qTx", dc)])
                ls = load_lnv(l, 1)
                Rwx = k.R("wbig", 0)
                RkT = [k.R("kTm", l, dc) for dc in range(8)]
                for n in range(NT):
                    tsl = slice(n * 128, (n + 1) * 128)
                    bs = n % 2
                    psl, prl = ps_big()
                    for h in range(4):
                        for hf_ in range(2):
                            dc = 2 * h + hf_
                            k.op("pe", "matmul", psl[:, h * 256:(h + 1) * 256], lhsT=qTx[:, dc, tsl], rhs=kTm[l][:, dc, :],
                                 start=(hf_ == 0), stop=(hf_ == 1), rd=[k.R("qTx", dc), RkT[dc]], wr=[prl[h // 2]],
                                 inc=(hf_ == 1 and h % 2 == 1), join=not (hf_ == 0 and h % 2 == 0))
                    mx, Rmx = sm(12)
                    k.op("dve", "tensor_reduce", out=mx[:, 0:4], in_=psl[:, :].rearrange("p (h m) -> p h m", h=4),
                         axis=AX.X, op=ALU.max, rd=prl, wr=[Rmx])
                    k.op("dve", "tensor_scalar", out=mx[:, 4:8], in0=mx[:, 0:4], scalar1=-1.0, scalar2=None,
                         op0=ALU.mult, rd=[Rmx], wr=[Rmx])
                    Rp = k.R("p", bs)
                    for h in range(4):
                        k.op("act", "activation", out=p_sb[bs][:, h, :], in_=psl[:, h * 256:(h + 1) * 256], func=AF.Exp,
                             bias=mx[:, 4 + h:5 + h], accum_out=mx[:, 8 + h:9 + h], rd=prl + [Rmx], wr=[Rp, Rmx],
                             join=(h != 0))
                    k.op("dve", "reciprocal", out=mx[:, 0:4], in_=mx[:, 8:12], rd=[Rmx], wr=[Rmx])
                    for h in range(4):
                        k.op("dve", "tensor_scalar", out=p_sb[bs][:, h, :], in0=p_sb[bs][:, h, :], scalar1=mx[:, h:h + 1],
                             scalar2=None, op0=ALU.mult, rd=[Rp, Rmx], wr=[Rp])
                    psT, prT = ps_small()
                    psT_bf = psT.bitcast(BF16)
                    for h in range(4):
                        for mt in range(2):
                            i = h * 2 + mt
                            k.op("pe", "transpose", psT_bf[:, i * 128:(i + 1) * 128], p_sb[bs][:, h, mt * 128:(mt + 1) * 128],
                                 ident_bf[:], rd=[Rp, Rib], wr=prT, inc=(i == 7), join=(i != 0))
                    k.op("dve", "tensor_copy", out=pT_sb[bs][:].rearrange("p c t -> p (c t)"), in_=psT_bf[:, 0:1024],
                         rd=prT, wr=[k.R("pT", bs)])
                    pso, pro = ps_big()
                    for dc in range(8):
                        h = dc // 2
                        for mt in range(2):
                            k.op("pe", "matmul", pso[:, dc * 128:(dc + 1) * 128], lhsT=vm[l][:, mt, dc * 128:(dc + 1) * 128],
                                 rhs=pT_sb[bs][:, h * 2 + mt, :], start=(mt == 0), stop=(mt == 1),
                                 rd=[k.R("pT", bs), k.R("vm", l, mt, dc // 4)], wr=[pro[dc // 4]],
                                 inc=(mt == 1 and dc % 4 == 3), join=not (mt == 0 and dc % 4 == 0))
                    k.op("act", "activation", out=oT_sb[bs][:].rearrange("p c t -> p (c t)"), in_=pso[:, :], func=AF.Copy,
                         rd=pro, wr=[k.R("oT", bs)])
                    ps, pr = proj_tm(n, wbig[0], Rwx, oT_sb[bs], k.R("oT", bs))
                    residual_ln(n, ps, pr, ls)
                if stop == (l, 2):
                    dump_x(t0); done = True; break
                snap = k.snapshot()
                k.poison_names(R1_E + R2_E, snap)
                wdv = wdn_d[l].rearrange("(j p) n -> p j n", p=128)
                Rwd = k.R("wd")
                for wp in range(2):
                    k.op("pool", "dma_start", out=wd[:, wp * 11:(wp + 1) * 11, :], in_=wdv[:, wp * 11:(wp + 1) * 11, :],
                         wr=[Rwd], dma=k.dma_tl("wd"), join=(wp == 1))
                for n in range(NT):
                    x_transposes(n)
                RxT = [k.R("xT", n) for n in range(NT)]
                Rhf = k.R("hf", l)
                Rcf = k.R("corrf")
                Rtq = k.R("tmpq")
                f0, f1 = hf[l][:, :, 0], hf[l][:, :, 1]
                tt(corrf[:, :, 0], f1, fw[:, :, 1], ALU.mult, [Rhf, Rpp], [Rcf])
                tt(tmpq[:, :], f0, fw[:, :, 0], ALU.mult, [Rhf, Rpp], [Rtq])
                tt(corrf[:, :, 0], corrf[:, :, 0], tmpq[:, :], ALU.add, [Rcf, Rtq], [Rcf])
                tt(corrf[:, :, 1], f1, fw[:, :, 0], ALU.mult, [Rhf, Rpp], [Rcf])
                wupv = wup_d[l].rearrange("(kc p) n -> p kc n", p=128)
                wup_i = [0]

                def load_wup(c):
                    s = wup_i[0] % 4
                    wup_i[0] += 1
                    k.op("pool", "dma_start", out=wup[s][:], in_=wupv[:, :, c * 128:(c + 1) * 128],
                         wr=[k.R("wup", s)], dma=k.dma_tl("wup", s))
                    return s
                order = []
                for j in range(22):
                    order += [j, 22 + j]
                slots = {}
                for i in range(3):
                    slots[order[i]] = load_wup(order[i])
                for idx, c in enumerate(order):
                    if idx + 3 < 44:
                        slots[order[idx + 3]] = load_wup(order[idx + 3])
                    s = slots[c]
                    j = c % 22
                    isg = c < 22
                    ps, pr = ps_small()
                    for kc in range(8):
                        k.op("pe", "matmul", ps[:, 0:SB], lhsT=wup[s][:, kc, :], rhs=xT[:, kc, :],
                             start=(kc == 0), stop=(kc == 7), rd=[k.R("wup", s)] + RxT, wr=pr,
                             inc=(kc == 7), join=(kc != 0))
                    a = j % 2
                    acc = accg[a] if isg else accv[a]
                    Ra = k.R("accg" if isg else "accv", a)
                    k.op("act", "activation", out=acc[:, :], in_=ps[:, 0:SB], func=AF.Identity,
                         scale=fw[:, c, 2:3], bias=fb[:, c:c + 1], rd=pr + [Rpp], wr=[Ra])
                    for jj, sh in ((1, 1), (0, 2)):
                        k.op("dve", "scalar_tensor_tensor", out=acc[:, sh:SB], in0=ps[:, 0:SB - sh],
                             scalar=fw[:, c, jj:jj + 1], in1=acc[:, sh:SB], op0=ALU.mult, op1=ALU.add,
                             rd=pr + [Rpp, Ra], wr=[Ra])
                    k.op("dve", "tensor_tensor", out=acc[:, 0:2], in0=acc[:, 0:2], in1=corrf[:, c, :],
                         op=ALU.add, rd=[Ra, Rcf], wr=[Ra])
                    k.op("act", "activation", out=hf[l][:, c, :], in_=ps[:, SB - 2:SB], func=AF.Copy, rd=pr, wr=[Rhf])
                    if isg:
                        k.op("act", "activation", out=gact[a][:, :], in_=acc[:, :], func=AF.Gelu_apprx_tanh,
                             rd=[Ra], wr=[k.R("gact", a)])
                    else:
                        k.op("dve", "tensor_tensor", out=hT[:, j, :], in0=gact[a][:, :], in1=acc[:, :], op=ALU.mult,
                             rd=[k.R("gact", a), Ra], wr=[k.R("hT", j)])
                ls = load_lnv(l, 2)
                RhT = [k.R("hT", j) for j in range(22)]
                for n in range(NT):
                    tsl = slice(n * 128, (n + 1) * 128)
                    ps, pr = ps_big()
                    for half in range(2):
                        for j in range(22):
                            k.op("pe", "matmul", ps[:, half * 512:(half + 1) * 512], lhsT=hT[:, j, tsl],
                                 rhs=wd[:, j, half * 512:(half + 1) * 512], start=(j == 0), stop=(j == 21),
                                 rd=[Rwd, RhT[j]], wr=[pr[half]], inc=(j == 21), join=(j != 0))
                    residual_ln(n, ps, pr, ls)
                snap = k.snapshot()
                k.poison_names(R1_AB + R2_A, snap)
                if stop == (l, 3):
                    dump_x(t0); done = True; break
                if l + 1 < L:
                    for n in range(NT):
                        x_transposes(n)
            if not done:
                dump_x(t0)
    except _Stop:
        pass
    if part < nparts - 1:
        for (nm, ap_, shp, rs) in states:
            d_ = dram("sto_" + nm, shp, kind="ExternalOutput")
            k.op("pool", "dma_start", out=d_, in_=ap_, rd=rs, dma=k.dma_tl("st", nm))
    k.finish()
    es.close()
    return nc


def _t5_bucket(dist):
    n = np.maximum(dist, 0)
    max_exact = 16
    nf = np.maximum(n, 1).astype(np.float32)
    large = max_exact + (np.log(nf / max_exact) / math.log(128 / max_exact) * (32 - max_exact)).astype(np.int32)
    large = np.minimum(large, 31)
    return np.where(n < max_exact, n, large)


def _perm_cols():
    ML = 512
    q = list(range(0, 512))
    kk = list(range(512, 1024))
    v = list(range(1024, 1536))
    o = list(range(1536, 2048))
    i_ = list(range(2048, 2052))
    f_ = list(range(2052, 2056))
    sq0 = 2056
    sk0 = 2056 + 512
    sv0 = sk0 + 128
    sq = []
    for c in range(4):
        sq += list(range(sq0 + c * 64, sq0 + (c + 1) * 64))
        sq += list(range(sq0 + (4 + c) * 64, sq0 + (5 + c) * 64))
    sk = list(range(sk0, sk0 + 128))
    sv_ = list(range(sv0, sv0 + 128))
    perm = q + kk + sq + sk + v + o + sv_ + i_ + f_
    assert len(perm) == NIN and sorted(perm) == list(range(NIN))
    return np.array(perm)


def prep_inputs(inp, depth=DEPTH):
    f = lambda a: np.ascontiguousarray(np.asarray(a, dtype=np.float32))
    L = DEPTH
    perm = _perm_cols()
    w_in = f(np.asarray(inp["w_in"])[:, :, perm])
    cst = np.zeros((128, 3, 128), np.float32)
    cst[:, 0, :] = np.eye(128, dtype=np.float32)
    cst[:, 1, :] = np.triu(np.ones((128, 128), np.float32))
    cst[:, 2, :] = 1.0
    rel = np.asarray(inp["rel_bias"], np.float32)
    kk = np.arange(128)[:, None]
    qq = np.arange(128)[None, :]
    biasT = np.zeros((128, 2, 8, 128), np.float32)
    for j in range(2):
        dist = qq - kk + (128 if j == 0 else 0)
        valid = (dist >= 0) & (dist < 128)
        bucket = _t5_bucket(dist)
        gathered = rel[bucket]
        for h in range(8):
            biasT[:, j, h, :] = np.where(valid, gathered[:, :, h], np.float32(NEG))
    pp = np.zeros((L, 128, 216), np.float32)
    bvs = np.zeros((L, 128, 528), np.float32)
    lnv = np.zeros((L, 3, 128, 2048), np.float32)
    for l in range(L):
        pp[l, :, 0:32] = np.transpose(np.asarray(inp["ml_conv_w"])[l].reshape(4, 8, 128), (2, 1, 0)).reshape(128, 32)
        pp[l, :, 32:40] = np.asarray(inp["ml_conv_b"])[l].reshape(8, 128).T
        pp[l, :, 40:172] = np.transpose(np.asarray(inp["ffn_conv_w"])[l].reshape(3, 44, 128), (2, 1, 0)).reshape(128, 132)
        pp[l, :, 172:216] = np.asarray(inp["ffn_conv_b"])[l].reshape(44, 128).T
        bvs[l, :, 0:512] = np.asarray(inp["ml_norm_g"])[l][None, :]
        bvs[l, :, 512:520] = np.asarray(inp["swa_sinks"])[l][None, :]
        bvs[l, :, 520:524] = np.asarray(inp["ml_i_bias"])[l][None, :]
        bvs[l, :, 524:528] = np.asarray(inp["ml_f_bias"])[l][None, :]
        for j, (g, b) in enumerate((("ln1_g", "ln1_b"), ("ln2_g", "ln2_b"), ("ln3_g", "ln3_b"))):
            lnv[l, j, :, 0:1024] = np.asarray(inp[g])[l][None, :]
            lnv[l, j, :, 1024:2048] = np.asarray(inp[b])[l][None, :]
    shared = {
        "cst": cst, "biasT": biasT, "w_in": w_in, "w_out": f(inp["w_out"]), "xa_wq": f(inp["xa_wq"]),
        "xa_wkv": f(inp["xa_wkv"]), "xa_wo": f(inp["xa_wo"]), "ffn_w_up": f(inp["ffn_w_up"]),
        "ffn_w_down": f(inp["ffn_w_down"]), "pp": pp, "bvs": bvs, "lnv": lnv,
    }
    return shared


def kernel(**inputs):
    x = np.asarray(inputs["x"], dtype=np.float32)
    mem = np.asarray(inputs["mem"], dtype=np.float32)
    B, S, _ = x.shape
    shared = prep_inputs(inputs)
    NPARTS = 2
    SP = S // NPARTS
    outs = []
    carry = [dict() for _ in range(B)]
    for part in range(NPARTS):
        nc = build(SEQ=SP, SB=512, depth=DEPTH, part=part, nparts=NPARTS)
        in_maps = []
        for b in range(B):
            m = dict(shared)
            m["x"] = np.ascontiguousarray(x[b, part * SP:(part + 1) * SP])
            m["mem"] = np.ascontiguousarray(mem[b])
            m.update(carry[b])
            in_maps.append(m)
        res = run_bass_kernel_spmd(nc, in_maps, core_ids=list(range(B)))
        outs.append(np.stack([np.asarray(r["out"], dtype=np.float32) for r in res.results], axis=0))
        carry = [{("sti_" + kk[4:]): np.asarray(v) for kk, v in r.items() if kk.startswith("sto_")}
                 for r in res.results]
    return np.concatenate(outs, axis=1)
```

```python
import math
from contextlib import ExitStack

import numpy as np
import concourse.bass as bass
import concourse.mybir as mybir
from concourse.bass_utils import run_bass_kernel_spmd

F32 = mybir.dt.float32
BF16 = mybir.dt.bfloat16
AF = mybir.ActivationFunctionType
ALU = mybir.AluOpType
AX = mybir.AxisListType

D = 1024
NIN = 2824
DFF = 2816
NMEM = 256
DEPTH = 2
ALPHA = (2.0 * DEPTH) ** 0.25
EPS = 1e-5
N_FM = 1664
N_TM = 1160
NEG = -30000.0


class TL:
    def __init__(self, sem, name):
        self.sem = sem
        self.cnt = 0
        self.name = name


class Res:
    __slots__ = ("w", "r", "key")

    def __init__(self, key):
        self.key = key
        self.w = {}
        self.r = {}


class K:
    def __init__(self, nc, es):
        self.nc = nc
        self.es = es
        self.eng = {"pe": nc.tensor, "act": nc.scalar, "dve": nc.vector, "pool": nc.gpsimd, "sp": nc.sync}
        self.etl = {k: self.new_tl("e_" + k) for k in self.eng}
        self.seen = {k: {} for k in self.eng}
        self.res = {}
        self.dtl = {}
        self.nsem = 5
        self.poison = {}
        self.hist = {}
        self.prog = {k_: [] for k_ in self.eng}

    def new_tl(self, name):
        sem = self.es.enter_context(self.nc.semaphore(name))
        return TL(sem, name)

    def R(self, *key):
        r = self.res.get(key)
        if r is None:
            r = Res(key)
            p = self.poison.get(key[0])
            if p:
                r.r = dict(p)
            self.res[key] = r
        return r

    def dma_tl(self, *key):
        t = self.dtl.get(key)
        if t is None:
            t = self.new_tl("d_" + "_".join(str(x) for x in key))
            self.dtl[key] = t
            self.nsem += 1
        return t

    def snapshot(self):
        s = {}
        for t in list(self.etl.values()) + list(self.dtl.values()):
            if t.cnt > 0:
                s[t] = t.cnt
        return s

    def poison_names(self, names, snap):
        for n in names:
            cur = self.poison.setdefault(n, {})
            for t, v in snap.items():
                cur[t] = max(cur.get(t, 0), v)
        for key, r in self.res.items():
            if key[0] in names:
                for t, v in snap.items():
                    r.r[t] = max(r.r.get(t, 0), v)

    def op(self, en, meth, *args, rd=(), wr=(), inc=True, dma=None, join=False, **kw):
        eng = self.eng[en]
        tl = self.etl[en]
        need = {}
        for r in rd:
            for t, v in r.w.items():
                if v > need.get(t, 0):
                    need[t] = v
        for r in wr:
            if not join:
                for t, v in r.w.items():
                    if v > need.get(t, 0):
                        need[t] = v
            for t, v in r.r.items():
                if v > need.get(t, 0):
                    need[t] = v
        seen = self.seen[en]
        for t, v in need.items():
            if t is tl and en == "pe":
                continue
            if seen.get(t, 0) >= v:
                continue
            assert v <= t.cnt, (en, meth, t.name, v, t.cnt)
            self.prog[en].append(("w", t.sem, v))
            seen[t] = v
            h = None
            if h is not None:
                snap = h.get(v)
                if snap:
                    for t2, v2 in snap.items():
                        if v2 > seen.get(t2, 0):
                            seen[t2] = v2
        incspec = None
        if dma is not None:
            dma.cnt += 16
            incspec = (dma.sem, 16)
            tok, val = dma, dma.cnt
        else:
            if inc:
                tl.cnt += 1
                incspec = (tl.sem, 1)
                val = tl.cnt
            else:
                assert en == "pe"
                val = tl.cnt + 1
            tok = tl
        self.prog[en].append(("i", meth, args, kw, incspec))
        for r in wr:
            if join:
                r.w[tok] = max(r.w.get(tok, 0), val)
            else:
                r.w = {tok: val}
                r.r = {}
        for r in rd:
            if val > r.r.get(tok, 0):
                r.r[tok] = val
        return None

    def finish(self):
        snap = self.snapshot()
        for en in ("sp", "act", "dve", "pool", "pe"):
            eng = self.eng[en]
            for t, v in snap.items():
                if t is self.etl[en]:
                    continue
                if self.seen[en].get(t, 0) >= v:
                    continue
                self.prog[en].append(("w", t.sem, v))
        self.emit()

    def emit(self):
        import os as _os2
        _nn = int(_os2.environ.get("KNOP", "0"))

        def mk(prog):
            def f(eng):
                for _ in range(_nn):
                    eng.nop()
                for it in prog:
                    if it[0] == "w":
                        eng.wait_ge(it[1], it[2])
                    else:
                        inst = getattr(eng, it[1])(*it[2], **it[3])
                        if it[4] is not None:
                            inst.then_inc(it[4][0], it[4][1])
            return f
        with self.nc.Block() as block:
            block.sync(mk(self.prog["sp"]))
            block.scalar(mk(self.prog["act"]))
            block.vector(mk(self.prog["dve"]))
            block.gpsimd(mk(self.prog["pool"]))
            block.tensor(mk(self.prog["pe"]))


class Alloc:
    def __init__(self, nc):
        self.nc = nc
        self.base = (nc.sbuf_base + 63) // 64 * 64
        self.top = nc.sbuf_top
        self.cur = self.base
        self.n = 0

    def at(self, off, name, shape, dt):
        self.n += 1
        esz = 4 if dt == F32 else 2
        nbytes = int(np.prod(shape[1:])) * esz
        assert off % 32 == 0
        assert off + nbytes <= self.top, (name, off, nbytes, self.top)
        t = self.nc.alloc_sbuf_tensor_at(f"{name}_{self.n}", list(shape), dt, offset=off)
        return t, nbytes

    def new(self, name, shape, dt):
        t, nb = self.at(self.cur, name, shape, dt)
        self.cur += (nb + 63) // 64 * 64
        return t


class _Stop(Exception):
    pass


def build(SEQ=4096, SB=512, depth=DEPTH, stop=None, part=0, nparts=1):
    assert SB == 512
    NT = SB // 128
    NSB = SEQ // SB
    L = depth
    nc = bass.Bass("TRN2", target_bir_lowering=False)
    es = ExitStack()
    k = K(nc, es)
    al = Alloc(nc)

    def dram(name, shape, kind="ExternalInput", dt=F32):
        return nc.dram_tensor(name, list(shape), dt, kind=kind).ap()

    x_d = dram("x", [SEQ, D])
    mem_d = dram("mem", [NMEM, D])
    cst_d = dram("cst", [128, 3, 128])
    bias_d = dram("biasT", [128, 2, 8, 128])
    w_in_d = dram("w_in", [DEPTH, D, NIN])
    w_out_d = dram("w_out", [DEPTH, D, D])
    wq_d = dram("xa_wq", [DEPTH, D, D])
    wkv_d = dram("xa_wkv", [DEPTH, D, 2 * D])
    wxo_d = dram("xa_wo", [DEPTH, D, D])
    wup_d = dram("ffn_w_up", [DEPTH, D, 2 * DFF])
    wdn_d = dram("ffn_w_down", [DEPTH, DFF, D])
    pp_d = dram("pp", [DEPTH, 128, 216])
    bvs_d = dram("bvs", [DEPTH, 128, 528])
    lnv_d = dram("lnv", [DEPTH, 3, 128, 2048])
    out_d = dram("out", [SEQ, D], kind="ExternalOutput")

    cst = al.new("cst", [128, 3, 128], F32)
    ident_bf = al.new("identbf", [128, 128], BF16)
    biasT = al.new("biasT", [128, 2, 8, 128], BF16)
    pp = [al.new("pp", [128, 216], F32) for _ in range(L)]
    bvs = [al.new("bvs", [128, 528], F32) for _ in range(L)]
    esk = [al.new("esk", [128, 8], F32) for _ in range(L)]
    kTm = [al.new("kTm", [128, 8, NMEM], BF16) for _ in range(L)]
    vm = [al.new("vm", [128, 2, D], BF16) for _ in range(L)]
    Cst = [al.new("C", [128, 4, 129], F32) for _ in range(L)]
    Fc = [al.new("Fc", [128, 4], F32) for _ in range(L)]
    Mc = [al.new("Mc", [128, 4], F32) for _ in range(L)]
    hq = [al.new("hq", [128, 8, 3], F32) for _ in range(L)]
    hf = [al.new("hf", [128, 44, 2], F32) for _ in range(L)]
    skT = [[al.new("skT", [128, 128 + SB], BF16) for _ in range(2)] for _ in range(L)]
    sv = [al.new("sv", [128, 1 + NT, 2, 65], BF16) for _ in range(L)]
    v_ext = al.new("vext", [128, NT, 4, 129], BF16)
    corrq = al.new("corrq", [128, 8, 3], F32)
    corrf = al.new("corrf", [128, 44, 2], F32)
    tmpq = al.new("tmpq", [128, 44], F32)
    gwork = {n: al.new("g_" + n, [128, NT * 4], F32) for n in
             ("lf", "F", "d", "wk", "gam", "clamp", "P", "mrun", "mprev", "tmb")}
    gif = al.new("gif", [128, NT, 8], F32)
    pdummy = al.new("pdummy", [128, 8], F32)
    smalls = al.new("smalls", [128, 512], F32)
    small_i = [0]
    Cb = [al.new("Cb", [128, 129], BF16) for _ in range(4)]
    kw_sb = [al.new("kw", [128, 128], BF16) for _ in range(4)]
    AT_sb = [al.new("AT", [128, 128], BF16) for _ in range(4)]
    hml = al.new("hml", [128, 4, 128], F32)
    PT_sb = [al.new("PT", [128, 4, 128], BF16) for _ in range(4)]
    accq = [al.new("accq", [128, 512], F32) for _ in range(2)]
    wfm = [al.new("wfm", [128, 8, 256], BF16) for _ in range(3)]
    wup = [al.new("wup", [128, 8, 256], BF16) for _ in range(4)]
    wvg = al.new("wvg", [128, 8, 136], BF16)
    lnv = [al.new("lnv", [128, 2048], F32) for _ in range(2)]
    x_sb = al.new("x", [128, NT, D], F32)
    xT = al.new("xT", [128, 8, SB], BF16)
    r2 = al.cur
    wbig = []
    o = r2
    for i in range(2):
        t, nb = al.at(o, "wbig", [128, 8, D], BF16)
        wbig.append(t)
        o += nb
    r2_end_a = o
    o = r2
    hT, nb = al.at(o, "hT", [128, 22, SB], BF16); o += nb
    accg = []; accv = []; gact = []
    for i in range(2):
        t, nb = al.at(o, "accg", [128, 512], F32); accg.append(t); o += nb
        t, nb = al.at(o, "accv", [128, 512], F32); accv.append(t); o += nb
        t, nb = al.at(o, "gact", [128, 512], BF16); gact.append(t); o += nb
    al.cur = max(r2_end_a, o)
    R2_A = ["wbig"]
    R2_E = ["hT", "accg", "accv", "gact"]
    r1 = al.cur
    memT, _nb = al.at(r1, "memT", [128, 8, NMEM], BF16)
    o = r1
    qT, nb = al.at(o, "qT", [128, 4, SB], BF16); o += nb
    kT, nb = al.at(o, "kT", [128, 4, SB], BF16); o += nb
    sqT, nb = al.at(o, "sqT", [128, 4, SB], BF16); o += nb
    og, nb = al.at(o, "og", [128, NT, 512], F32); o += nb
    hcat, nb = al.at(o, "hcat", [128, NT, D], BF16); o += nb
    hcT = []
    for i in range(2):
        t, nb = al.at(o, "hcT", [128, 8, 128], BF16); hcT.append(t); o += nb
    r1_end_ab = o
    o = r1
    qTx, nb = al.at(o, "qTx", [128, 8, SB], BF16); o += nb
    p_sb = []; pT_sb = []; oT_sb = []
    for i in range(2):
        t, nb = al.at(o, "p", [128, 4, NMEM], BF16); p_sb.append(t); o += nb
        t, nb = al.at(o, "pT", [128, 8, 128], BF16); pT_sb.append(t); o += nb
        t, nb = al.at(o, "oT", [128, 8, 128], BF16); oT_sb.append(t); o += nb
    r1_end_d = o
    o = r1
    wd, nb = al.at(o, "wd", [128, 22, D], BF16); o += nb
    r1_end_e = o
    al.cur = max(r1_end_ab, r1_end_d, r1_end_e)
    R1_AB = ["qT", "kT", "sqT", "og", "hcat", "hcT"]
    R1_D = ["qTx", "p", "pT", "oT"]
    R1_E = ["wd"]
    assert al.cur <= al.top, (al.cur, al.top)

    PS = [es.enter_context(nc.psum_tensor(f"ps{i}", [128, 1024], F32)) for i in range(4)]
    big_i = [0]
    small_i2 = [0]

    def ps_big():
        i = big_i[0] % 2
        big_i[0] += 1
        return PS[i], [k.R("ps", i, 0), k.R("ps", i, 1)]

    def ps_small():
        j = small_i2[0] % 4
        small_i2[0] += 1
        i, h = 2 + j // 2, j % 2
        return PS[i][:, h * 512:(h + 1) * 512], [k.R("ps", i, h)]

    def sm(n):
        assert n <= 16
        g_ = small_i[0] % 32
        small_i[0] += 1
        return smalls[:, g_ * 16:g_ * 16 + n], k.R("sm", g_)

    ident = cst[:, 0, :]
    triU = cst[:, 1, :]
    ones = cst[:, 2, :]
    Rc = k.R("cst")

    states = []
    tile_off = part * (SEQ // 128)
    k.op("sp", "dma_start", out=cst[:], in_=cst_d[:, :, :], wr=[Rc], dma=k.dma_tl("cst"))
    k.op("dve", "tensor_copy", out=ident_bf[:], in_=ident, rd=[Rc], wr=[k.R("identbf")])
    Rib = k.R("identbf")
    k.op("pool", "dma_start", out=biasT[:], in_=bias_d[:, :, :, :], wr=[k.R("biasT")], dma=k.dma_tl("biasT"))
    for l in range(L):
        k.op("sp", "dma_start", out=pp[l][:], in_=pp_d[l], wr=[k.R("pp", l)], dma=k.dma_tl("pp", l))
        k.op("sp", "dma_start", out=bvs[l][:], in_=bvs_d[l], wr=[k.R("bvs", l)], dma=k.dma_tl("bvs", l))
        k.op("act", "activation", out=esk[l][:], in_=bvs[l][:, 512:520], func=AF.Exp,
             rd=[k.R("bvs", l)], wr=[k.R("esk", l)])
        k.op("dve", "memset", Cst[l][:], 0.0, wr=[k.R("C", l, h) for h in range(4)])
        k.op("dve", "memset", Fc[l][:], 0.0, wr=[k.R("Fc", l)])
        k.op("dve", "memset", Mc[l][:], 0.0, wr=[k.R("Mc", l)])
        k.op("dve", "memset", hq[l][:], 0.0, wr=[k.R("hq", l)])
        k.op("dve", "memset", hf[l][:], 0.0, wr=[k.R("hf", l)])
        for g in range(2):
            k.op("dve", "memset", skT[l][g][:], 0.0, wr=[k.R("sk", l, g, "p"), k.R("sk", l, g, "c")])
        k.op("dve", "memset", sv[l][:], 0.0, wr=[k.R("sv", l, n) for n in range(-1, NT)])
        k.op("dve", "memset", sv[l][:, :, :, 64:65], 1.0, wr=[k.R("sv", l, n) for n in range(-1, NT)])
        states += [
            (f"C{l}", Cst[l][:], [128, 4, 129], [k.R("C", l, h) for h in range(4)]),
            (f"Fc{l}", Fc[l][:], [128, 4], [k.R("Fc", l)]),
            (f"Mc{l}", Mc[l][:], [128, 4], [k.R("Mc", l)]),
            (f"hq{l}", hq[l][:], [128, 8, 3], [k.R("hq", l)]),
            (f"hf{l}", hf[l][:], [128, 44, 2], [k.R("hf", l)]),
            (f"sk0{l}", skT[l][0][:, 0:128], [128, 128], [k.R("sk", l, 0, "p")]),
            (f"sk1{l}", skT[l][1][:, 0:128], [128, 128], [k.R("sk", l, 1, "p")]),
            (f"sv{l}", sv[l][:, 0, :, :], [128, 2, 65], [k.R("sv", l, -1)]),
        ]
    k.op("dve", "memset", v_ext[:], 1.0, wr=[k.R("v", n) for n in range(NT)])
    if part > 0:
        stg, _nb2 = al.at(r1 + 8192, "ststage", [128, 400], F32)
        so = 0
        for (nm, ap_, shp, rs) in states:
            d_ = dram("sti_" + nm, shp)
            if nm.startswith("sk") or nm.startswith("sv"):
                nel = int(np.prod(shp[1:]))
                if so + nel > 400:
                    so = 0
                st_ap = stg[:, so:so + nel]
                Rs = k.R("ststage", so)
                so += nel
                src = d_ if len(shp) == 2 else d_.rearrange("p a b -> p (a b)")
                k.op("sp", "dma_start", out=st_ap, in_=src, wr=[Rs], dma=k.dma_tl("st", nm))
                dst = ap_ if len(shp) == 2 else ap_
                if len(shp) == 3:
                    st_ap = st_ap.rearrange("p (a b) -> p a b", a=shp[1])
                k.op("dve", "tensor_copy", out=dst, in_=st_ap, rd=[Rs], wr=rs)
            else:
                k.op("sp", "dma_start", out=ap_, in_=d_, wr=rs, dma=k.dma_tl("st", nm))
    for mt in range(2):
        k.op("sp", "dma_start", out=x_sb[:, mt, :], in_=mem_d[mt * 128:(mt + 1) * 128, :],
             wr=[k.R("x", mt)], dma=k.dma_tl("xin", mt))
        ps, pr = ps_big()
        for kc in range(8):
            k.op("pe", "transpose", ps[:, kc * 128:(kc + 1) * 128], x_sb[:, mt, kc * 128:(kc + 1) * 128], ident,
                 rd=[k.R("x", mt), Rc], wr=[pr[kc // 4]], inc=(kc % 4 == 3), join=(kc % 4 != 0))
        k.op("act", "activation", out=memT[:, :, mt * 128:(mt + 1) * 128],
             in_=ps[:].rearrange("p (c t) -> p c t", c=8), func=AF.Copy, rd=pr, wr=[k.R("memT", mt)])
    for l in range(L):
        for c4 in range(4):
            slot = c4 % 2
            k.op("pool", "dma_start", out=wbig[slot][:, :, 0:512],
                 in_=wkv_d[l].rearrange("(kc p) n -> p kc n", p=128)[:, :, c4 * 512:(c4 + 1) * 512],
                 wr=[k.R("wbig", slot)], dma=k.dma_tl("wbig", slot))
            if c4 < 2:
                for dcl in range(4):
                    dc = c4 * 4 + dcl
                    ps, pr = ps_small()
                    for kc in range(8):
                        k.op("pe", "matmul", ps[:, 0:NMEM], lhsT=wbig[slot][:, kc, dcl * 128:(dcl + 1) * 128],
                             rhs=memT[:, kc, :], start=(kc == 0), stop=(kc == 7),
                             rd=[k.R("wbig", slot), k.R("memT", 0), k.R("memT", 1)], wr=pr,
                             inc=(kc == 7), join=(kc != 0))
                    k.op("act", "activation", out=kTm[l][:, dc, :], in_=ps[:, 0:NMEM], func=AF.Copy,
                         rd=pr, wr=[k.R("kTm", l, dc)])
            else:
                hh = c4 - 2
                for mt in range(2):
                    ps, pr = ps_small()
                    for kc in range(8):
                        k.op("pe", "matmul", ps[:, 0:512], lhsT=memT[:, kc, mt * 128:(mt + 1) * 128],
                             rhs=wbig[slot][:, kc, 0:512], start=(kc == 0), stop=(kc == 7),
                             rd=[k.R("wbig", slot), k.R("memT", mt)], wr=pr, inc=(kc == 7), join=(kc != 0))
                    k.op("act", "activation", out=vm[l][:, mt, hh * 512:(hh + 1) * 512], in_=ps[:, 0:512],
                         func=AF.Copy, rd=pr, wr=[k.R("vm", l, mt, hh)])

    k.poison_names(R1_AB + R1_D + R1_E, k.snapshot())
    lnv_i = [0]

    def load_lnv(l, j):
        s = lnv_i[0] % 2
        lnv_i[0] += 1
        k.op("sp", "dma_start", out=lnv[s][:], in_=lnv_d[l, j], wr=[k.R("lnv", s)], dma=k.dma_tl("lnv", s))
        return s

    def x_transposes(n):
        ps, pr = ps_big()
        for kc in range(8):
            k.op("pe", "transpose", ps[:, kc * 128:(kc + 1) * 128], x_sb[:, n, kc * 128:(kc + 1) * 128], ident,
                 rd=[k.R("x", n), Rc], wr=[pr[kc // 4]], inc=(kc % 4 == 3), join=(kc % 4 != 0))
        k.op("act", "activation", out=xT[:, :, n * 128:(n + 1) * 128],
             in_=ps[:].rearrange("p (c t) -> p c t", c=8), func=AF.Copy, rd=pr, wr=[k.R("xT", n)])

    def residual_ln(n, ps, pr, ls, last=False):
        Rx = k.R("x", n)
        xa = x_sb[:, n, :]
        k.op("dve", "scalar_tensor_tensor", out=xa, in0=xa, scalar=float(ALPHA), in1=ps[:, :],
             op0=ALU.mult, op1=ALU.add, rd=pr, wr=[Rx])
        st, Rst = sm(12)
        k.op("dve", "bn_stats", out=st[:, 0:6], in_=x_sb[:, n, 0:512], rd=[Rx], wr=[Rst])
        k.op("dve", "bn_stats", out=st[:, 6:12], in_=x_sb[:, n, 512:1024], rd=[Rx], wr=[Rst], join=True)
        mv, Rmv = sm(4)
        k.op("dve", "bn_aggr", out=mv[:, 0:2], in_=st, rd=[Rst], wr=[Rmv])
        k.op("dve", "tensor_scalar", out=mv[:, 2:3], in0=mv[:, 1:2], scalar1=float(EPS), scalar2=None,
             op0=ALU.add, rd=[Rmv], wr=[Rmv])
        k.op("act", "activation", out=mv[:, 2:3], in_=mv[:, 2:3], func=AF.Sqrt, rd=[Rmv], wr=[Rmv])
        k.op("dve", "reciprocal", out=mv[:, 2:3], in_=mv[:, 2:3], rd=[Rmv], wr=[Rmv])
        k.op("pool", "memset", pdummy[:, 0:4], 0.0, rd=[Rx], wr=[k.R("pdummy")])
        k.op("dve", "scalar_tensor_tensor", out=xa, in0=xa, scalar=mv[:, 0:1], in1=lnv[ls][:, 0:1024],
             op0=ALU.subtract, op1=ALU.mult, rd=[Rx, Rmv, k.R("lnv", ls)], wr=[Rx])
        k.op("dve", "scalar_tensor_tensor", out=xa, in0=xa, scalar=mv[:, 2:3], in1=lnv[ls][:, 1024:2048],
             op0=ALU.mult, op1=ALU.add, rd=[Rx, Rmv, k.R("lnv", ls)], wr=[Rx])

    def proj_tm(n, w_t, Rw, lhs_t, Rl):
        ps, pr = ps_big()
        nk = 8
        for half in range(2):
            for kc in range(nk):
                k.op("pe", "matmul", ps[:, half * 512:(half + 1) * 512], lhsT=lhs_t[:, kc, :],
                     rhs=w_t[:, kc, half * 512:(half + 1) * 512], start=(kc == 0), stop=(kc == nk - 1),
                     rd=[Rw, Rl], wr=[pr[half]], inc=(kc == nk - 1), join=(kc != 0))
        return ps, pr

    wfm_i = [0]
    wbig_i = [0]
    sbi_cur = [0]
    scratch = {}

    def wload(dst_ap, Rdst, tlkey, src_ap, key, join=False):
        if sbi_cur[0] == 0:
            k.op("pool", "dma_start", out=dst_ap, in_=src_ap, wr=[Rdst], dma=k.dma_tl(*tlkey), join=join)
            d = nc.dram_tensor("scr_" + "_".join(str(x_) for x_ in key), list(dst_ap.shape), BF16, kind="Internal").ap()
            scratch[key] = d
            k.op("sp", "dma_start", out=d, in_=dst_ap, rd=[Rdst], wr=[k.R("scr", key)], dma=k.dma_tl("wsc", *tlkey))
        else:
            k.op("sp", "dma_start", out=dst_ap, in_=scratch[key], rd=[k.R("scr", key)], wr=[Rdst],
                 dma=k.dma_tl(*tlkey), join=join)

    def load_wfm(src_v, c0, c1, key):
        s = wfm_i[0] % 3
        wfm_i[0] += 1
        wload(wfm[s][:, :, 0:c1 - c0], k.R("wfm", s), ("wfm", s), src_v[:, :, c0:c1], key + (c0,))
        return s

    def dump_x(t0):
        for n in range(NT):
            k.op("sp", "dma_start", out=out_d[t0 + n * 128:t0 + (n + 1) * 128, :], in_=x_sb[:, n, :],
                 rd=[k.R("x", n)], dma=k.dma_tl("xout", n))

    def chk(tag):
        if stop == tag:
            dump_x(0)
            raise _Stop()

    try:
        chk("pro")
        for sbi in range(NSB):
            t0 = sbi * SB
            first_sb = (sbi == 0)
            sbi_cur[0] = sbi
            chk('sb%d' % sbi)
            for n in range(NT):
                k.op("sp", "dma_start", out=x_sb[:, n, :], in_=x_d[t0 + n * 128:t0 + (n + 1) * 128, :],
                     wr=[k.R("x", n)], dma=k.dma_tl("xin", n))
            for n in range(NT):
                x_transposes(n)
            chk('X')
            done = False
            for l in range(L):
                Rpp = k.R("pp", l)
                Rbv = k.R("bvs", l)
                cw = pp[l][:, 0:32].rearrange("p (c j) -> p c j", j=4)
                cb = pp[l][:, 32:40]
                fw = pp[l][:, 40:172].rearrange("p (c j) -> p c j", j=3)
                fb = pp[l][:, 172:216]
                win = w_in_d[l].rearrange("(kc p) n -> p kc n", p=128)
                sv_slot = 0
                wload(wbig[sv_slot][:], k.R("wbig", sv_slot), ("wbig", sv_slot), win[:, :, N_FM:N_FM + 1024], ("wvo", l))
                wload(wvg[:], k.R("wvg"), ("wvg",), win[:, :, N_FM + 1024:N_FM + 1160], ("wvg", l))
                wload(wbig[1][:], k.R("wbig", 1), ("wbig", 1), w_out_d[l].rearrange("(kc p) n -> p kc n", p=128), ("wout", l))
                Rhq = k.R("hq", l)
                Rcq = k.R("corrq")
                h0, h1, h2 = hq[l][:, :, 0], hq[l][:, :, 1], hq[l][:, :, 2]
                w0, w1, w2 = cw[:, :, 0], cw[:, :, 1], cw[:, :, 2]
                tq = tmpq[:, 0:8]
                Rtq = k.R("tmpq")

                def tt(out, a, b, op, rd, wr):
                    k.op("dve", "tensor_tensor", out=out, in0=a, in1=b, op=op, rd=rd, wr=wr)
                tt(corrq[:, :, 0], h2, w2, ALU.mult, [Rhq, Rpp], [Rcq])
                tt(tq, h1, w1, ALU.mult, [Rhq, Rpp], [Rtq])
                tt(corrq[:, :, 0], corrq[:, :, 0], tq, ALU.add, [Rcq, Rtq], [Rcq])
                tt(tq, h0, w0, ALU.mult, [Rhq, Rpp], [Rtq])
                tt(corrq[:, :, 0], corrq[:, :, 0], tq, ALU.add, [Rcq, Rtq], [Rcq])
                tt(corrq[:, :, 1], h2, w1, ALU.mult, [Rhq, Rpp], [Rcq])
                tt(tq, h1, w0, ALU.mult, [Rhq, Rpp], [Rtq])
                tt(corrq[:, :, 1], corrq[:, :, 1], tq, ALU.add, [Rcq, Rtq], [Rcq])
                tt(corrq[:, :, 2], h2, w0, ALU.mult, [Rhq, Rpp], [Rcq])
                chk('A0')
                slots = {}
                slots[0] = load_wfm(win, 0, 256, ('win', l))
                slots[1] = load_wfm(win, 256, 512, ('win', l))
                RxT = [k.R("xT", n) for n in range(NT)]
                for c in range(13):
                    bq = c // 2
                    if c % 2 == 0 and bq + 2 < 7:
                        slots[bq + 2] = load_wfm(win, (bq + 2) * 256, min((bq + 3) * 256, N_FM), ('win', l))
                    s = slots[bq]
                    sub = c % 2
                    ps, pr = ps_small()
                    for kc in range(8):
                        k.op("pe", "matmul", ps[:, 0:SB], lhsT=wfm[s][:, kc, sub * 128:(sub + 1) * 128], rhs=xT[:, kc, :],
                             start=(kc == 0), stop=(kc == 7), rd=[k.R("wfm", s)] + RxT, wr=pr,
                             inc=(kc == 7), join=(kc != 0))
                    if c < 8:
                        dst = qT if c < 4 else kT
                        Rd = k.R("qT" if c < 4 else "kT", c % 4)
                        a = c % 2
                        Ra = k.R("accq", a)
                        acc = accq[a]
                        k.op("act", "activation", out=acc[:, :], in_=ps[:, 0:SB], func=AF.Identity,
                             scale=cw[:, c, 3:4], bias=cb[:, c:c + 1], rd=pr + [Rpp], wr=[Ra])
                        for j, sh in ((2, 1), (1, 2), (0, 3)):
                            k.op("dve", "scalar_tensor_tensor", out=acc[:, sh:SB], in0=ps[:, 0:SB - sh],
                                 scalar=cw[:, c, j:j + 1], in1=acc[:, sh:SB], op0=ALU.mult, op1=ALU.add,
                                 rd=pr + [Rpp, Ra], wr=[Ra])
                        k.op("dve", "tensor_tensor", out=acc[:, 0:3], in0=acc[:, 0:3], in1=corrq[:, c, :],
                             op=ALU.add, rd=[Ra, Rcq], wr=[Ra])
                        k.op("act", "activation", out=hq[l][:, c, :], in_=ps[:, SB - 3:SB], func=AF.Copy,
                             rd=pr, wr=[Rhq])
                        k.op("act", "activation", out=dst[:, c % 4, :], in_=acc[:, :], func=AF.Silu,
                             rd=[Ra], wr=[Rd])
                    elif c < 12:
                        k.op("act", "activation", out=sqT[:, c - 8, :], in_=ps[:, 0:SB], func=AF.Copy, scale=0.125,
                             rd=pr, wr=[k.R("sqT", c - 8)])
                    else:
                        k.op("act", "activation", out=skT[l][0][0:64, 128:128 + SB], in_=ps[0:64, 0:SB],
                             func=AF.Copy, rd=pr, wr=[k.R("sk", l, 0, "c")])
                        k.op("dve", "tensor_copy", out=skT[l][1][64:128, 128:128 + SB], in_=ps[64:128, 0:SB],
                             rd=pr, wr=[k.R("sk", l, 1, "c")])
                chk('A1')
                Rwv = k.R("wbig", sv_slot)
                for n in range(NT):
                    ps, pr = ps_small()
                    for kc in range(8):
                        k.op("pe", "matmul", ps[:, 0:512], lhsT=xT[:, kc, n * 128:(n + 1) * 128],
                             rhs=wbig[sv_slot][:, kc, 0:512], start=(kc == 0), stop=(kc == 7),
                             rd=[Rwv, RxT[n]], wr=pr, inc=(kc == 7), join=(kc != 0))
                    k.op("act", "activation", out=v_ext[:, n, :, 0:128],
                         in_=ps[:, 0:512].rearrange("p (h d) -> p h d", h=4), func=AF.Copy, rd=pr, wr=[k.R("v", n)])
                    chk('A2v')
                    ps, pr = ps_small()
                    for kc in range(8):
                        k.op("pe", "matmul", ps[:, 0:512], lhsT=xT[:, kc, n * 128:(n + 1) * 128],
                             rhs=wbig[sv_slot][:, kc, 512:1024], start=(kc == 0), stop=(kc == 7),
                             rd=[Rwv, RxT[n]], wr=pr, inc=(kc == 7), join=(kc != 0))
                    k.op("act", "activation", out=og[:, n, :], in_=ps[:, 0:512], func=AF.Sigmoid, rd=pr,
                         wr=[k.R("og", n)])
                    chk('A2o')
                    ps, pr = ps_small()
                    for kc in range(8):
                        k.op("pe", "matmul", ps[:, 0:136], lhsT=xT[:, kc, n * 128:(n + 1) * 128],
                             rhs=wvg[:, kc, :], start=(kc == 0), stop=(kc == 7),
                             rd=[k.R("wvg"), RxT[n]], wr=pr, inc=(kc == 7), join=(kc != 0))
                    k.op("act", "activation", out=sv[l][:, 1 + n, :, 0:64],
                         in_=ps[:, 0:128].rearrange("p (h d) -> p h d", h=2), func=AF.Copy, rd=pr,
                         wr=[k.R("sv", l, n)])
                    chk('A2s%d' % n)
                    k.op("act", "activation", out=gif[:, n, :], in_=ps[:, 128:136], func=AF.Copy, rd=pr, wr=[k.R("gif", n)])
                    chk('A2g%d' % n)
                chk('A')
                Rg = k.R("gates")
                Rgif = [k.R("gif", n) for n in range(NT)]
                G = gwork
                NG4 = NT * 4

                def v3(t):
                    return t[:, :].rearrange("p (n h) -> p n h", h=4)
                ib = bvs[l][:, 520:524]
                fbias = bvs[l][:, 524:528]
                for n in range(NT):
                    k.op("dve", "tensor_tensor", out=v3(G["lf"])[:, n, :], in0=gif[:, n, 4:8], in1=fbias, op=ALU.add,
                         rd=Rgif + [Rbv], wr=[Rg])
                    k.op("dve", "tensor_tensor", out=v3(G["d"])[:, n, :], in0=gif[:, n, 0:4], in1=ib, op=ALU.add,
                         rd=Rgif + [Rbv], wr=[Rg])
                k.op("act", "activation", out=G["lf"][:, :], in_=G["lf"][:, :], func=AF.Exp, scale=-1.0, rd=[Rg], wr=[Rg])
                k.op("act", "activation", out=G["lf"][:, :], in_=G["lf"][:, :], func=AF.Ln, bias=1.0, rd=[Rg], wr=[Rg])
                k.op("dve", "tensor_scalar", out=G["lf"][:, :], in0=G["lf"][:, :], scalar1=-1.0, scalar2=None,
                     op0=ALU.mult, rd=[Rg], wr=[Rg])
                ps, pr = ps_small()
                k.op("pe", "matmul", ps[:, 0:NG4], lhsT=triU, rhs=G["lf"][:, :], start=True, stop=True,
                     rd=[Rc, Rg], wr=pr)
                ps2, pr2 = ps_small()
                k.op("pe", "matmul", ps2[:, 0:NG4], lhsT=ones, rhs=G["lf"][:, :], start=True, stop=True,
                     rd=[Rc, Rg], wr=pr2)
                k.op("dve", "tensor_copy", out=G["tmb"][:, :], in_=ps2[:, 0:NG4], rd=pr2, wr=[Rg])
                RF = k.R("Fc", l)
                P3 = v3(G["P"])
                T3 = v3(G["tmb"])
                k.op("dve", "tensor_copy", out=P3[:, 0, :], in_=Fc[l][:, :], rd=[RF], wr=[Rg])
                for n in range(1, NT):
                    k.op("dve", "tensor_tensor", out=P3[:, n, :], in0=P3[:, n - 1, :], in1=T3[:, n - 1, :], op=ALU.add,
                         rd=[Rg], wr=[Rg])
                k.op("dve", "tensor_tensor", out=Fc[l][:, :], in0=P3[:, NT - 1, :], in1=T3[:, NT - 1, :], op=ALU.add,
                     rd=[Rg], wr=[RF])
                k.op("dve", "tensor_tensor", out=G["F"][:, :], in0=G["P"][:, :], in1=ps[:, 0:NG4], op=ALU.add,
                     rd=[Rg] + pr, wr=[Rg])
                k.op("dve", "tensor_tensor", out=G["d"][:, :], in0=G["d"][:, :], in1=G["F"][:, :], op=ALU.subtract,
                     rd=[Rg], wr=[Rg])
                ps, pr = ps_small()
                k.op("pe", "transpose", ps[0:NG4, 0:128], G["d"][:, :], ident, rd=[Rc, Rg], wr=pr)
                tm, Rtm = sm(1)
                k.op("dve", "tensor_reduce", out=tm[0:NG4, :], in_=ps[0:NG4, 0:128], axis=AX.X, op=ALU.max,
                     rd=pr, wr=[Rtm])
                dg, Rdg = sm(NG4)
                k.op("dve", "tensor_scalar", out=dg[0:NG4, :], in0=ident[0:NG4, 0:NG4], scalar1=tm[0:NG4, 0:1],
                     scalar2=None, op0=ALU.mult, rd=[Rtm, Rc], wr=[Rdg])
                ps, pr = ps_small()
                k.op("pe", "matmul", ps[:, 0:NG4], lhsT=ones[0:NG4, :], rhs=dg[0:NG4, :], start=True, stop=True,
                     rd=[Rc, Rdg], wr=pr)
                RM = k.R("Mc", l)
                M3 = v3(G["mrun"])
                MP3 = v3(G["mprev"])
                pst = ps[:, 0:NG4].rearrange("p (n h) -> p n h", h=4)
                k.op("dve", "tensor_copy", out=MP3[:, 0, :], in_=Mc[l][:, :], rd=[RM], wr=[Rg])
                for n in range(NT):
                    k.op("dve", "tensor_tensor", out=M3[:, n, :], in0=MP3[:, n, :], in1=pst[:, n, :], op=ALU.max,
                         rd=[Rg] + pr, wr=[Rg])
                    if n + 1 < NT:
                        k.op("dve", "tensor_copy", out=MP3[:, n + 1, :], in_=M3[:, n, :], rd=[Rg], wr=[Rg])
                k.op("dve", "tensor_copy", out=Mc[l][:, :], in_=M3[:, NT - 1, :], rd=[Rg], wr=[RM])
                k.op("dve", "tensor_tensor", out=G["wk"][:, :], in0=G["d"][:, :], in1=G["mrun"][:, :], op=ALU.subtract,
                     rd=[Rg], wr=[Rg])
                k.op("act", "activation", out=G["wk"][:, :], in_=G["wk"][:, :], func=AF.Exp, rd=[Rg], wr=[Rg])
                k.op("dve", "tensor_scalar", out=G["wk"][:, :], in0=G["wk"][:, :], scalar1=float(128 ** -0.5),
                     scalar2=None, op0=ALU.mult, rd=[Rg], wr=[Rg])
                k.op("dve", "tensor_tensor", out=G["gam"][:, :], in0=G["mprev"][:, :], in1=G["mrun"][:, :],
                     op=ALU.subtract, rd=[Rg], wr=[Rg])
                k.op("act", "activation", out=G["gam"][:, :], in_=G["gam"][:, :], func=AF.Exp, rd=[Rg], wr=[Rg])
                k.op("dve", "tensor_tensor", out=G["clamp"][:, :], in0=G["F"][:, :], in1=G["mrun"][:, :], op=ALU.add,
                     rd=[Rg], wr=[Rg])
                k.op("act", "activation", out=G["clamp"][:, :], in_=G["clamp"][:, :], func=AF.Exp, scale=-1.0,
                     rd=[Rg], wr=[Rg])
                chk('B1')
                ls = load_lnv(l, 0)
                Rwo = k.R("wbig", 1)
                for n in range(NT):
                    tsl = slice(n * 128, (n + 1) * 128)
                    pst_, prt = ps_small()
                    pst_bf = pst_.bitcast(BF16)
                    pss, prs = ps_small()
                    for h in range(4):
                        k.op("pe", "transpose", pst_bf[:, h * 128:(h + 1) * 128], kT[:, h, tsl], ident_bf[:],
                             rd=[k.R("kT", h), Rib], wr=prt, inc=(h == 3), join=(h != 0))
                    for h in range(4):
                        k.op("pe", "matmul", pss[:, h * 128:(h + 1) * 128], lhsT=kT[:, h, tsl], rhs=qT[:, h, tsl],
                             start=True, stop=True, rd=[k.R("kT", h), k.R("qT", h)], wr=prs, inc=(h == 3),
                             join=(h != 0))
                    for h in range(4):
                        wkc = G["wk"][:, n * 4 + h:n * 4 + h + 1]
                        k.op("act", "activation", out=kw_sb[h][:, :], in_=pst_bf[:, h * 128:(h + 1) * 128],
                             func=AF.Copy, scale=wkc, rd=prt + [Rg], wr=[k.R("kw", h)])
                        k.op("dve", "scalar_tensor_tensor", out=AT_sb[h][:, :], in0=pss[:, h * 128:(h + 1) * 128],
                             scalar=wkc, in1=triU, op0=ALU.mult, op1=ALU.mult, rd=prs + [Rg, Rc],
                             wr=[k.R("AT", h)])
                        gmc = G["gam"][:, n * 4 + h:n * 4 + h + 1]
                        k.op("act", "activation", out=Cb[h][:, :], in_=Cst[l][:, h, :], func=AF.Copy, scale=gmc,
                             rd=[k.R("C", l, h), Rg], wr=[k.R("Cb", h)])
                    for h in range(4):
                        gmc = G["gam"][:, n * 4 + h:n * 4 + h + 1]
                        psn, prn = ps_small()
                        k.op("pe", "matmul", psn[:, 0:129], lhsT=AT_sb[h][:, :], rhs=v_ext[:, n, h, :],
                             start=True, stop=False, rd=[k.R("AT", h), k.R("v", n)], wr=prn, inc=False)
                        k.op("pe", "matmul", psn[:, 0:129], lhsT=qT[:, h, tsl], rhs=Cb[h][:, :],
                             start=False, stop=True, rd=[k.R("qT", h), k.R("Cb", h)], wr=prn, join=True)
                        psc, prc = ps_small()
                        k.op("pe", "matmul", psc[:, 0:129], lhsT=kw_sb[h][:, :], rhs=v_ext[:, n, h, :],
                             start=True, stop=True, rd=[k.R("kw", h), k.R("v", n)], wr=prc)
                        k.op("dve", "scalar_tensor_tensor", out=Cst[l][:, h, :], in0=Cst[l][:, h, :], scalar=gmc,
                             in1=psc[:, 0:129], op0=ALU.mult, op1=ALU.add, rd=prc + [Rg], wr=[k.R("C", l, h)])
                        dn, Rdn = sm(2)
                        k.op("act", "activation", out=dn[:, 0:1], in_=psn[:, 128:129], func=AF.Abs, rd=prn, wr=[Rdn])
                        k.op("dve", "tensor_tensor", out=dn[:, 0:1], in0=dn[:, 0:1],
                             in1=G["clamp"][:, n * 4 + h:n * 4 + h + 1], op=ALU.max, rd=[Rdn, Rg], wr=[Rdn])
                        k.op("dve", "reciprocal", out=dn[:, 1:2], in_=dn[:, 0:1], rd=[Rdn], wr=[Rdn])
                        k.op("act", "activation", out=hml[:, h, :], in_=psn[:, 0:128], func=AF.Copy, scale=dn[:, 1:2],
                             rd=prn + [Rdn], wr=[k.R("hml", h)])
                        st, Rst = sm(6)
                        k.op("dve", "bn_stats", out=st, in_=hml[:, h, :], rd=[k.R("hml", h)], wr=[Rst])
                        mv, Rmv = sm(4)
                        k.op("dve", "bn_aggr", out=mv[:, 0:2], in_=st, rd=[Rst], wr=[Rmv])
                        k.op("dve", "tensor_scalar", out=mv[:, 2:3], in0=mv[:, 1:2], scalar1=float(EPS), scalar2=None,
                             op0=ALU.add, rd=[Rmv], wr=[Rmv])
                        k.op("act", "activation", out=mv[:, 2:3], in_=mv[:, 2:3], func=AF.Sqrt, rd=[Rmv], wr=[Rmv])
                        k.op("dve", "reciprocal", out=mv[:, 2:3], in_=mv[:, 2:3], rd=[Rmv], wr=[Rmv])
                        k.op("dve", "tensor_scalar", out=hml[:, h, :], in0=hml[:, h, :], scalar1=mv[:, 0:1],
                             scalar2=mv[:, 2:3], op0=ALU.subtract, op1=ALU.mult, rd=[Rmv, k.R("hml", h)],
                             wr=[k.R("hml", h)])
                    Rh = [k.R("hml", h) for h in range(4)]
                    hml2 = hml[:].rearrange("p h d -> p (h d)")
                    k.op("dve", "tensor_tensor", out=hml2, in0=hml2, in1=bvs[l][:, 0:512], op=ALU.mult,
                         rd=Rh + [Rbv], wr=Rh)
                    k.op("dve", "tensor_tensor", out=hcat[:, n, 0:512], in0=hml2, in1=og[:, n, :], op=ALU.mult,
                         rd=Rh + [k.R("og", n)], wr=[k.R("hcat", n, 0)])
                    absn = tile_off + sbi * NT + n
                    jl = [1] if absn == 0 else [0, 1]
                    for g in range(2):
                        pso, pro = ps_small()
                        for j in jl:
                            psl, prl = ps_small()
                            kcol = slice((n + j) * 128, (n + j + 1) * 128)
                            Rk = k.R("sk", l, g, "c") if (n + j) >= 1 else k.R("sk", l, g, "p")
                            k.op("pe", "matmul", psl[:, 0:512], lhsT=skT[l][g][:, kcol], rhs=sqT[:, :, tsl],
                                 start=True, stop=False, rd=[Rk] + [k.R("sqT", c) for c in range(4)], wr=prl,
                                 inc=False)
                            k.op("pe", "matmul", psl[:, 0:512], lhsT=ident_bf[:], rhs=biasT[:, j, g * 4:(g + 1) * 4, :],
                                 start=False, stop=True, rd=[Rib, k.R("biasT")], wr=prl, join=True)
                            pi = 2 * g + j
                            k.op("act", "activation", out=PT_sb[pi][:].rearrange("p c q -> p (c q)"), in_=psl[:, 0:512],
                                 func=AF.Exp, rd=prl, wr=[k.R("PT", pi)])
                        for c in range(4):
                            for j in jl:
                                pi = 2 * g + j
                                k.op("pe", "matmul", pso[:, c * 65:(c + 1) * 65], lhsT=PT_sb[pi][:, c, :],
                                     rhs=sv[l][:, n + j, g, :], start=(j == jl[0]), stop=(j == 1),
                                     rd=[k.R("PT", pi), k.R("sv", l, n + j - 1)], wr=pro, inc=(c == 3 and j == 1),
                                     join=not (c == 0 and j == jl[0]))
                        den, Rden = sm(8)
                        po3 = pso[:, 0:260].rearrange("p (c e) -> p c e", e=65)
                        k.op("dve", "tensor_tensor", out=den[:, 0:4], in0=po3[:, :, 64], in1=esk[l][:, g * 4:(g + 1) * 4],
                             op=ALU.add, rd=pro + [k.R("esk", l)], wr=[Rden])
                        k.op("dve", "reciprocal", out=den[:, 4:8], in_=den[:, 0:4], rd=[Rden], wr=[Rden])
                        for c in range(4):
                            hh = g * 4 + c
                            eng = "act" if c % 2 == 0 else "dve"
                            if eng == "act":
                                k.op("act", "activation", out=hcat[:, n, 512 + hh * 64:512 + (hh + 1) * 64],
                                     in_=po3[:, c, 0:64], func=AF.Copy, scale=den[:, 4 + c:5 + c],
                                     rd=pro + [Rden], wr=[k.R("hcat", n, 1 + hh)])
                            else:
                                k.op("dve", "tensor_scalar", out=hcat[:, n, 512 + hh * 64:512 + (hh + 1) * 64],
                                     in0=po3[:, c, 0:64], scalar1=den[:, 4 + c:5 + c], scalar2=None, op0=ALU.mult,
                                     rd=pro + [Rden], wr=[k.R("hcat", n, 1 + hh)])
                    Rhc = [k.R("hcat", n, i) for i in range(9)]
                    hs = n % 2
                    psT, prT = ps_small()
                    psT_bf = psT.bitcast(BF16)
                    for kc in range(8):
                        k.op("pe", "transpose", psT_bf[:, kc * 128:(kc + 1) * 128], hcat[:, n, kc * 128:(kc + 1) * 128],
                             ident_bf[:], rd=Rhc + [Rib], wr=prT, inc=(kc == 7), join=(kc != 0))
                    k.op("dve", "tensor_copy", out=hcT[hs][:].rearrange("p c t -> p (c t)"), in_=psT_bf[:, 0:1024],
                         rd=prT, wr=[k.R("hcT", hs)])
                    ps, pr = proj_tm(n, wbig[1], Rwo, hcT[hs], k.R("hcT", hs))
                    residual_ln(n, ps, pr, ls)
                for g in range(2):
                    k.op("act", "activation", func=AF.Copy, out=skT[l][g][:, 0:128], in_=skT[l][g][:, SB:SB + 128],
                         rd=[k.R("sk", l, g, "c")], wr=[k.R("sk", l, g, "p")])
                k.op("act", "activation", func=AF.Copy, out=sv[l][:, 0, :, :], in_=sv[l][:, NT, :, :],
                     rd=[k.R("sv", l, NT - 1)], wr=[k.R("sv", l, -1)])
                if stop == (l, 1):
                    dump_x(t0); done = True; break
                k.poison_names(R1_D, k.snapshot())
                wload(wbig[0][:], k.R("wbig", 0), ("wbig", 0), wxo_d[l].rearrange("(kc p) n -> p kc n", p=128), ("wxo", l))
                for n in range(NT):
                    x_transposes(n)
                RxT = [k.R("xT", n) for n in range(NT)]
                wqv = wq_d[l].rearrange("(kc p) n -> p kc n", p=128)
                slots = {0: load_wfm(wqv, 0, 256, ('wq', l)), 1: load_wfm(wqv, 256, 512, ('wq', l))}
                for dc in range(8):
                    bq = dc // 2
                    if dc % 2 == 0 and bq + 2 < 4:
                        slots[bq + 2] = load_wfm(wqv, (bq + 2) * 256, (bq + 3) * 256, ('wq', l))
                    s = slots[bq]
                    sub = dc % 2
                    ps, pr = ps_small()
                    for kc in range(8):
                        k.op("pe", "matmul", ps[:, 0:SB], lhsT=wfm[s][:, kc, sub * 128:(sub + 1) * 128], rhs=xT[:, kc, :],
                             start=(kc == 0), stop=(kc == 7), rd=[k.R("wfm", s)] + RxT, wr=pr,
                             inc=(kc == 7), join=(kc != 0))
                    k.op("act", "activation", out=qTx[:, dc, :], in_=ps[:, 0:SB], func=AF.Copy, scale=1.0 / 16.0,
                         rd=pr, wr=[k.R("qTx", dc)])
                ls = load_lnv(l, 1)
                Rwx = k.R("wbig", 0)
                RkT = [k.R("kTm", l, dc) for dc in range(8)]
                for n in range(NT):
                    tsl = slice(n * 128, (n + 1) * 128)
                    bs = n % 2
                    psl, prl = ps_big()
                    for h in range(4):
                        for hf_ in range(2):
                            dc = 2 * h + hf_
                            k.op("pe", "matmul", psl[:, h * 256:(h + 1) * 256], lhsT=qTx[:, dc, tsl], rhs=kTm[l][:, dc, :],
                                 start=(hf_ == 0), stop=(hf_ == 1), rd=[k.R("qTx", dc), RkT[dc]], wr=[prl[h // 2]],
                                 inc=(hf_ == 1 and h % 2 == 1), join=not (hf_ == 0 and h % 2 == 0))
                    mx, Rmx = sm(12)
                    k.op("dve", "tensor_reduce", out=mx[:, 0:4], in_=psl[:, :].rearrange("p (h m) -> p h m", h=4),
                         axis=AX.X, op=ALU.max, rd=prl, wr=[Rmx])
                    k.op("dve", "tensor_scalar", out=mx[:, 4:8], in0=mx[:, 0:4], scalar1=-1.0, scalar2=None,
                         op0=ALU.mult, rd=[Rmx], wr=[Rmx])
                    Rp = k.R("p", bs)
                    for h in range(4):
                        k.op("act", "activation", out=p_sb[bs][:, h, :], in_=psl[:, h * 256:(h + 1) * 256], func=AF.Exp,
                             bias=mx[:, 4 + h:5 + h], accum_out=mx[:, 8 + h:9 + h], rd=prl + [Rmx], wr=[Rp, Rmx],
                             join=(h != 0))
                    k.op("dve", "reciprocal", out=mx[:, 0:4], in_=mx[:, 8:12], rd=[Rmx], wr=[Rmx])
                    for h in range(4):
                        k.op("dve", "tensor_scalar", out=p_sb[bs][:, h, :], in0=p_sb[bs][:, h, :], scalar1=mx[:, h:h + 1],
                             scalar2=None, op0=ALU.mult, rd=[Rp, Rmx], wr=[Rp])
                    psT, prT = ps_small()
                    psT_bf = psT.bitcast(BF16)
                    for h in range(4):
                        for mt in range(2):
                            i = h * 2 + mt
                            k.op("pe", "transpose", psT_bf[:, i * 128:(i + 1) * 128], p_sb[bs][:, h, mt * 128:(mt + 1) * 128],
                                 ident_bf[:], rd=[Rp, Rib], wr=prT, inc=(i == 7), join=(i != 0))
                    k.op("dve", "tensor_copy", out=pT_sb[bs][:].rearrange("p c t -> p (c t)"), in_=psT_bf[:, 0:1024],
                         rd=prT, wr=[k.R("pT", bs)])
                    pso, pro = ps_big()
                    for dc in range(8):
                        h = dc // 2
                        for mt in range(2):
                            k.op("pe", "matmul", pso[:, dc * 128:(dc + 1) * 128], lhsT=vm[l][:, mt, dc * 128:(dc + 1) * 128],
                                 rhs=pT_sb[bs][:, h * 2 + mt, :], start=(mt == 0), stop=(mt == 1),
                                 rd=[k.R("pT", bs), k.R("vm", l, mt, dc // 4)], wr=[pro[dc // 4]],
                                 inc=(mt == 1 and dc % 4 == 3), join=not (mt == 0 and dc % 4 == 0))
                    k.op("act", "activation", out=oT_sb[bs][:].rearrange("p c t -> p (c t)"), in_=pso[:, :], func=AF.Copy,
                         rd=pro, wr=[k.R("oT", bs)])
                    ps, pr = proj_tm(n, wbig[0], Rwx, oT_sb[bs], k.R("oT", bs))
                    residual_ln(n, ps, pr, ls)
                if stop == (l, 2):
                    dump_x(t0); done = True; break
                snap = k.snapshot()
                k.poison_names(R1_E + R2_E, snap)
                wdv = wdn_d[l].rearrange("(j p) n -> p j n", p=128)
                Rwd = k.R("wd")
                for wp in range(2):
                    wload(wd[:, wp * 11:(wp + 1) * 11, :], Rwd, ("wd",), wdv[:, wp * 11:(wp + 1) * 11, :], ("wd", l, wp),
                          join=(wp == 1))
                for n in range(NT):
                    x_transposes(n)
                RxT = [k.R("xT", n) for n in range(NT)]
                Rhf = k.R("hf", l)
                Rcf = k.R("corrf")
                Rtq = k.R("tmpq")
                f0, f1 = hf[l][:, :, 0], hf[l][:, :, 1]
                tt(corrf[:, :, 0], f1, fw[:, :, 1], ALU.mult, [Rhf, Rpp], [Rcf])
                tt(tmpq[:, :], f0, fw[:, :, 0], ALU.mult, [Rhf, Rpp], [Rtq])
                tt(corrf[:, :, 0], corrf[:, :, 0], tmpq[:, :], ALU.add, [Rcf, Rtq], [Rcf])
                tt(corrf[:, :, 1], f1, fw[:, :, 0], ALU.mult, [Rhf, Rpp], [Rcf])
                wupv = wup_d[l].rearrange("(kc p) n -> p kc n", p=128)
                wup_i = [0]

                def load_wup(kind, b):
                    s = wup_i[0] % 4
                    wup_i[0] += 1
                    c0 = (0 if kind == "g" else DFF) + b * 256
                    wload(wup[s][:], k.R("wup", s), ("wup", s), wupv[:, :, c0:c0 + 256], ("wup", l, kind, b))
                    return s
                order = []
                for j in range(22):
                    order += [j, 22 + j]
                slots = {}
                for b in range(2):
                    slots[("g", b)] = load_wup("g", b)
                    slots[("v", b)] = load_wup("v", b)
                for idx, c in enumerate(order):
                    jj_ = c % 22
                    if c < 22 and jj_ % 2 == 0 and jj_ // 2 >= 1 and jj_ // 2 + 1 < 11:
                        b = jj_ // 2 + 1
                        slots[("g", b)] = load_wup("g", b)
                        slots[("v", b)] = load_wup("v", b)
                    s = slots[("g" if c < 22 else "v", jj_ // 2)]
                    sub = jj_ % 2
                    j = c % 22
                    isg = c < 22
                    ps, pr = ps_small()
                    for kc in range(8):
                        k.op("pe", "matmul", ps[:, 0:SB], lhsT=wup[s][:, kc, sub * 128:(sub + 1) * 128], rhs=xT[:, kc, :],
                             start=(kc == 0), stop=(kc == 7), rd=[k.R("wup", s)] + RxT, wr=pr,
                             inc=(kc == 7), join=(kc != 0))
                    a = j % 2
                    acc = accg[a] if isg else accv[a]
                    Ra = k.R("accg" if isg else "accv", a)
                    k.op("act", "activation", out=acc[:, :], in_=ps[:, 0:SB], func=AF.Identity,
                         scale=fw[:, c, 2:3], bias=fb[:, c:c + 1], rd=pr + [Rpp], wr=[Ra])
                    for jj, sh in ((1, 1), (0, 2)):
                        k.op("dve", "scalar_tensor_tensor", out=acc[:, sh:SB], in0=ps[:, 0:SB - sh],
                             scalar=fw[:, c, jj:jj + 1], in1=acc[:, sh:SB], op0=ALU.mult, op1=ALU.add,
                             rd=pr + [Rpp, Ra], wr=[Ra])
                    k.op("dve", "tensor_tensor", out=acc[:, 0:2], in0=acc[:, 0:2], in1=corrf[:, c, :],
                         op=ALU.add, rd=[Ra, Rcf], wr=[Ra])
                    k.op("act", "activation", out=hf[l][:, c, :], in_=ps[:, SB - 2:SB], func=AF.Copy, rd=pr, wr=[Rhf])
                    if isg:
                        k.op("act", "activation", out=gact[a][:, :], in_=acc[:, :], func=AF.Gelu_apprx_tanh,
                             rd=[Ra], wr=[k.R("gact", a)])
                    else:
                        k.op("dve", "tensor_tensor", out=hT[:, j, :], in0=gact[a][:, :], in1=acc[:, :], op=ALU.mult,
                             rd=[k.R("gact", a), Ra], wr=[k.R("hT", j)])
                ls = load_lnv(l, 2)
                RhT = [k.R("hT", j) for j in range(22)]
                for n in range(NT):
                    tsl = slice(n * 128, (n + 1) * 128)
                    ps, pr = ps_big()
                    for half in range(2):
                        for j in range(22):
                            k.op("pe", "matmul", ps[:, half * 512:(half + 1) * 512], lhsT=hT[:, j, tsl],
                                 rhs=wd[:, j, half * 512:(half + 1) * 512], start=(j == 0), stop=(j == 21),
                                 rd=[Rwd, RhT[j]], wr=[pr[half]], inc=(j == 21), join=(j != 0))
                    residual_ln(n, ps, pr, ls)
                snap = k.snapshot()
                k.poison_names(R1_AB + R2_A, snap)
                if stop == (l, 3):
                    dump_x(t0); done = True; break
                if l + 1 < L:
                    for n in range(NT):
                        x_transposes(n)
            if not done:
                dump_x(t0)
    except _Stop:
        pass
    if part < nparts - 1:
        k.poison_names(["ststage_o"], k.snapshot())
        stg_o, _nb3 = al.at(r1 + 16384, "ststage_o", [128, 800], F32)
        so = 0
        for (nm, ap_, shp, rs) in states:
            d_ = dram("sto_" + nm, shp, kind="ExternalOutput")
            if nm.startswith("sk") or nm.startswith("sv"):
                nel = int(np.prod(shp[1:]))
                st_ap = stg_o[:, so:so + nel]
                Rs = k.R("ststage_o", so)
                so += nel
                src = st_ap if len(shp) == 2 else st_ap.rearrange("p (a b) -> p a b", a=shp[1])
                k.op("dve", "tensor_copy", out=src, in_=ap_, rd=rs, wr=[Rs])
                dst = d_ if len(shp) == 2 else d_.rearrange("p a b -> p (a b)")
                k.op("sp", "dma_start", out=dst, in_=st_ap, rd=[Rs], dma=k.dma_tl("st", nm))
            else:
                k.op("sp", "dma_start", out=d_, in_=ap_, rd=rs, dma=k.dma_tl("st", nm))
    k.finish()
    es.close()
    return nc


def _t5_bucket(dist):
    n = np.maximum(dist, 0)
    max_exact = 16
    nf = np.maximum(n, 1).astype(np.float32)
    large = max_exact + (np.log(nf / max_exact) / math.log(128 / max_exact) * (32 - max_exact)).astype(np.int32)
    large = np.minimum(large, 31)
    return np.where(n < max_exact, n, large)


def _perm_cols():
    ML = 512
    q = list(range(0, 512))
    kk = list(range(512, 1024))
    v = list(range(1024, 1536))
    o = list(range(1536, 2048))
    i_ = list(range(2048, 2052))
    f_ = list(range(2052, 2056))
    sq0 = 2056
    sk0 = 2056 + 512
    sv0 = sk0 + 128
    sq = []
    for c in range(4):
        sq += list(range(sq0 + c * 64, sq0 + (c + 1) * 64))
        sq += list(range(sq0 + (4 + c) * 64, sq0 + (5 + c) * 64))
    sk = list(range(sk0, sk0 + 128))
    sv_ = list(range(sv0, sv0 + 128))
    perm = q + kk + sq + sk + v + o + sv_ + i_ + f_
    assert len(perm) == NIN and sorted(perm) == list(range(NIN))
    return np.array(perm)


def prep_inputs(inp, depth=DEPTH):
    f = lambda a: np.ascontiguousarray(np.asarray(a, dtype=np.float32))
    L = DEPTH
    perm = _perm_cols()
    w_in = f(np.asarray(inp["w_in"])[:, :, perm])
    cst = np.zeros((128, 3, 128), np.float32)
    cst[:, 0, :] = np.eye(128, dtype=np.float32)
    cst[:, 1, :] = np.triu(np.ones((128, 128), np.float32))
    cst[:, 2, :] = 1.0
    rel = np.asarray(inp["rel_bias"], np.float32)
    kk = np.arange(128)[:, None]
    qq = np.arange(128)[None, :]
    biasT = np.zeros((128, 2, 8, 128), np.float32)
    for j in range(2):
        dist = qq - kk + (128 if j == 0 else 0)
        valid = (dist >= 0) & (dist < 128)
        bucket = _t5_bucket(dist)
        gathered = rel[bucket]
        for h in range(8):
            biasT[:, j, h, :] = np.where(valid, gathered[:, :, h], np.float32(NEG))
    pp = np.zeros((L, 128, 216), np.float32)
    bvs = np.zeros((L, 128, 528), np.float32)
    lnv = np.zeros((L, 3, 128, 2048), np.float32)
    for l in range(L):
        pp[l, :, 0:32] = np.transpose(np.asarray(inp["ml_conv_w"])[l].reshape(4, 8, 128), (2, 1, 0)).reshape(128, 32)
        pp[l, :, 32:40] = np.asarray(inp["ml_conv_b"])[l].reshape(8, 128).T
        pp[l, :, 40:172] = np.transpose(np.asarray(inp["ffn_conv_w"])[l].reshape(3, 44, 128), (2, 1, 0)).reshape(128, 132)
        pp[l, :, 172:216] = np.asarray(inp["ffn_conv_b"])[l].reshape(44, 128).T
        bvs[l, :, 0:512] = np.asarray(inp["ml_norm_g"])[l][None, :]
        bvs[l, :, 512:520] = np.asarray(inp["swa_sinks"])[l][None, :]
        bvs[l, :, 520:524] = np.asarray(inp["ml_i_bias"])[l][None, :]
        bvs[l, :, 524:528] = np.asarray(inp["ml_f_bias"])[l][None, :]
        for j, (g, b) in enumerate((("ln1_g", "ln1_b"), ("ln2_g", "ln2_b"), ("ln3_g", "ln3_b"))):
            lnv[l, j, :, 0:1024] = np.asarray(inp[g])[l][None, :]
            lnv[l, j, :, 1024:2048] = np.asarray(inp[b])[l][None, :]
    shared = {
        "cst": cst, "biasT": biasT, "w_in": w_in, "w_out": f(inp["w_out"]), "xa_wq": f(inp["xa_wq"]),
        "xa_wkv": f(inp["xa_wkv"]), "xa_wo": f(inp["xa_wo"]), "ffn_w_up": f(inp["ffn_w_up"]),
        "ffn_w_down": f(inp["ffn_w_down"]), "pp": pp, "bvs": bvs, "lnv": lnv,
    }
    return shared


def kernel(**inputs):
    x = np.asarray(inputs["x"], dtype=np.float32)
    mem = np.asarray(inputs["mem"], dtype=np.float32)
    B, S, _ = x.shape
    shared = prep_inputs(inputs)
    NPARTS = 2
    SP = S // NPARTS
    outs = []
    carry = [dict() for _ in range(B)]
    for part in range(NPARTS):
        nc = build(SEQ=SP, SB=512, depth=DEPTH, part=part, nparts=NPARTS)
        in_maps = []
        for b in range(B):
            m = dict(shared)
            m["x"] = np.ascontiguousarray(x[b, part * SP:(part + 1) * SP])
            m["mem"] = np.ascontiguousarray(mem[b])
            m.update(carry[b])
            in_maps.append(m)
        res = run_bass_kernel_spmd(nc, in_maps, core_ids=list(range(B)))
        outs.append(np.stack([np.asarray(r["out"], dtype=np.float32) for r in res.results], axis=0))
        carry = [{("sti_" + kk[4:]): np.asarray(v) for kk, v in r.items() if kk.startswith("sto_")}
                 for r in res.results]
    return np.concatenate(outs, axis=1)
```

```python
import math
from contextlib import ExitStack

import numpy as np
import concourse.bass as bass
import concourse.mybir as mybir
from concourse.bass_utils import run_bass_kernel_spmd

F32 = mybir.dt.float32
BF16 = mybir.dt.bfloat16
AF = mybir.ActivationFunctionType
ALU = mybir.AluOpType
AX = mybir.AxisListType

D = 1024
NIN = 2824
DFF = 2816
NMEM = 256
DEPTH = 2
ALPHA = (2.0 * DEPTH) ** 0.25
EPS = 1e-5
N_FM = 1664
N_TM = 1160
NEG = -30000.0


class TL:
    def __init__(self, sem, name):
        self.sem = sem
        self.cnt = 0
        self.name = name


class Res:
    __slots__ = ("w", "r", "key")

    def __init__(self, key):
        self.key = key
        self.w = {}
        self.r = {}


class K:
    def __init__(self, nc, es):
        self.nc = nc
        self.es = es
        self.eng = {"pe": nc.tensor, "act": nc.scalar, "dve": nc.vector, "pool": nc.gpsimd, "sp": nc.sync}
        self.etl = {k: self.new_tl("e_" + k) for k in self.eng}
        self.seen = {k: {} for k in self.eng}
        self.res = {}
        self.dtl = {}
        self.nsem = 5
        self.poison = {}
        self.hist = {}
        self.prog = {k_: [] for k_ in self.eng}

    def new_tl(self, name):
        sem = self.es.enter_context(self.nc.semaphore(name))
        return TL(sem, name)

    def R(self, *key):
        r = self.res.get(key)
        if r is None:
            r = Res(key)
            p = self.poison.get(key[0])
            if p:
                r.r = dict(p)
            self.res[key] = r
        return r

    def dma_tl(self, *key):
        t = self.dtl.get(key)
        if t is None:
            t = self.new_tl("d_" + "_".join(str(x) for x in key))
            self.dtl[key] = t
            self.nsem += 1
        return t

    def snapshot(self):
        s = {}
        for t in list(self.etl.values()) + list(self.dtl.values()):
            if t.cnt > 0:
                s[t] = t.cnt
        return s

    def poison_names(self, names, snap):
        for n in names:
            cur = self.poison.setdefault(n, {})
            for t, v in snap.items():
                cur[t] = max(cur.get(t, 0), v)
        for key, r in self.res.items():
            if key[0] in names:
                for t, v in snap.items():
                    r.r[t] = max(r.r.get(t, 0), v)

    def op(self, en, meth, *args, rd=(), wr=(), inc=True, dma=None, join=False, **kw):
        eng = self.eng[en]
        tl = self.etl[en]
        need = {}
        for r in rd:
            for t, v in r.w.items():
                if v > need.get(t, 0):
                    need[t] = v
        for r in wr:
            if not join:
                for t, v in r.w.items():
                    if v > need.get(t, 0):
                        need[t] = v
            for t, v in r.r.items():
                if v > need.get(t, 0):
                    need[t] = v
        seen = self.seen[en]
        for t, v in need.items():
            if t is tl and en == "pe":
                continue
            if seen.get(t, 0) >= v:
                continue
            assert v <= t.cnt, (en, meth, t.name, v, t.cnt)
            self.prog[en].append(("w", t.sem, v))
            seen[t] = v
            h = None
            if h is not None:
                snap = h.get(v)
                if snap:
                    for t2, v2 in snap.items():
                        if v2 > seen.get(t2, 0):
                            seen[t2] = v2
        incspec = None
        if dma is not None:
            dma.cnt += 16
            incspec = (dma.sem, 16)
            tok, val = dma, dma.cnt
        else:
            if inc:
                tl.cnt += 1
                incspec = (tl.sem, 1)
                val = tl.cnt
            else:
                assert en == "pe"
                val = tl.cnt + 1
            tok = tl
        self.prog[en].append(("i", meth, args, kw, incspec))
        for r in wr:
            if join:
                r.w[tok] = max(r.w.get(tok, 0), val)
            else:
                r.w = {tok: val}
                r.r = {}
        for r in rd:
            if val > r.r.get(tok, 0):
                r.r[tok] = val
        return None

    def finish(self):
        snap = self.snapshot()
        for en in ("sp", "act", "dve", "pool", "pe"):
            eng = self.eng[en]
            for t, v in snap.items():
                if t is self.etl[en]:
                    continue
                if self.seen[en].get(t, 0) >= v:
                    continue
                self.prog[en].append(("w", t.sem, v))
        self.emit()

    def emit(self):
        import os as _os2
        _nn = int(_os2.environ.get("KNOP", "0"))

        def mk(prog):
            def f(eng):
                for _ in range(_nn):
                    eng.nop()
                for it in prog:
                    if it[0] == "w":
                        eng.wait_ge(it[1], it[2])
                    else:
                        inst = getattr(eng, it[1])(*it[2], **it[3])
                        if it[4] is not None:
                            inst.then_inc(it[4][0], it[4][1])
            return f
        with self.nc.Block() as block:
            block.sync(mk(self.prog["sp"]))
            block.scalar(mk(self.prog["act"]))
            block.vector(mk(self.prog["dve"]))
            block.gpsimd(mk(self.prog["pool"]))
            block.tensor(mk(self.prog["pe"]))


class Alloc:
    def __init__(self, nc):
        self.nc = nc
        self.base = (nc.sbuf_base + 63) // 64 * 64
        self.top = nc.sbuf_top
        self.cur = self.base
        self.n = 0

    def at(self, off, name, shape, dt):
        self.n += 1
        esz = 4 if dt == F32 else 2
        nbytes = int(np.prod(shape[1:])) * esz
        assert off % 32 == 0
        assert off + nbytes <= self.top, (name, off, nbytes, self.top)
        t = self.nc.alloc_sbuf_tensor_at(f"{name}_{self.n}", list(shape), dt, offset=off)
        return t, nbytes

    def new(self, name, shape, dt):
        t, nb = self.at(self.cur, name, shape, dt)
        self.cur += (nb + 63) // 64 * 64
        return t


class _Stop(Exception):
    pass


def build(SEQ=4096, SB=512, depth=DEPTH, stop=None, part=0, nparts=1):
    assert SB == 512
    NT = SB // 128
    NSB = SEQ // SB
    L = depth
    nc = bass.Bass("TRN2", target_bir_lowering=False)
    es = ExitStack()
    k = K(nc, es)
    al = Alloc(nc)

    def dram(name, shape, kind="ExternalInput", dt=F32):
        return nc.dram_tensor(name, list(shape), dt, kind=kind).ap()

    x_d = dram("x", [SEQ, D])
    mem_d = dram("mem", [NMEM, D])
    cst_d = dram("cst", [128, 3, 128])
    bias_d = dram("biasT", [128, 2, 8, 128])
    w_in_d = dram("w_in", [DEPTH, D, NIN])
    w_out_d = dram("w_out", [DEPTH, D, D])
    wq_d = dram("xa_wq", [DEPTH, D, D])
    wkv_d = dram("xa_wkv", [DEPTH, D, 2 * D])
    wxo_d = dram("xa_wo", [DEPTH, D, D])
    wup_d = dram("ffn_w_up", [DEPTH, D, 2 * DFF])
    wdn_d = dram("ffn_w_down", [DEPTH, DFF, D])
    pp_d = dram("pp", [DEPTH, 128, 216])
    bvs_d = dram("bvs", [DEPTH, 128, 528])
    lnv_d = dram("lnv", [DEPTH, 3, 128, 2048])
    out_d = dram("out", [SEQ, D], kind="ExternalOutput")

    cst = al.new("cst", [128, 3, 128], F32)
    ident_bf = al.new("identbf", [128, 128], BF16)
    biasT = al.new("biasT", [128, 2, 8, 128], BF16)
    pp = [al.new("pp", [128, 216], F32) for _ in range(L)]
    bvs = [al.new("bvs", [128, 528], F32) for _ in range(L)]
    esk = [al.new("esk", [128, 8], F32) for _ in range(L)]
    kTm = [al.new("kTm", [128, 8, NMEM], BF16) for _ in range(L)]
    vm = [al.new("vm", [128, 2, D], BF16) for _ in range(L)]
    Cst = [al.new("C", [128, 4, 129], F32) for _ in range(L)]
    Fc = [al.new("Fc", [128, 4], F32) for _ in range(L)]
    Mc = [al.new("Mc", [128, 4], F32) for _ in range(L)]
    hq = [al.new("hq", [128, 8, 3], F32) for _ in range(L)]
    hf = [al.new("hf", [128, 44, 2], F32) for _ in range(L)]
    skT = [[al.new("skT", [128, 128 + SB], BF16) for _ in range(2)] for _ in range(L)]
    sv = [al.new("sv", [128, 1 + NT, 2, 65], BF16) for _ in range(L)]
    v_ext = al.new("vext", [128, NT, 4, 129], BF16)
    corrq = al.new("corrq", [128, 8, 3], F32)
    corrf = al.new("corrf", [128, 44, 2], F32)
    tmpq = al.new("tmpq", [128, 44], F32)
    gwork = {n: al.new("g_" + n, [128, NT * 4], F32) for n in
             ("lf", "F", "d", "wk", "gam", "clamp", "P", "mrun", "mprev", "tmb")}
    gif = al.new("gif", [128, NT, 8], F32)
    pdummy = al.new("pdummy", [128, 8], F32)
    smalls = al.new("smalls", [128, 512], F32)
    small_i = [0]
    Cb = [al.new("Cb", [128, 129], BF16) for _ in range(4)]
    kw_sb = [al.new("kw", [128, 128], BF16) for _ in range(4)]
    AT_sb = [al.new("AT", [128, 128], BF16) for _ in range(4)]
    hml = al.new("hml", [128, 4, 128], F32)
    PT_sb = [al.new("PT", [128, 4, 128], BF16) for _ in range(4)]
    accq = [al.new("accq", [128, 512], F32) for _ in range(2)]
    wfm = [al.new("wfm", [128, 8, 256], BF16) for _ in range(3)]
    wup = [al.new("wup", [128, 8, 256], BF16) for _ in range(4)]
    wvg = al.new("wvg", [128, 8, 136], BF16)
    lnv = [al.new("lnv", [128, 2048], F32) for _ in range(2)]
    x_sb = al.new("x", [128, NT, D], F32)
    xT = al.new("xT", [128, 8, SB], BF16)
    r2 = al.cur
    wbig = []
    o = r2
    for i in range(2):
        t, nb = al.at(o, "wbig", [128, 8, D], BF16)
        wbig.append(t)
        o += nb
    r2_end_a = o
    o = r2
    hT, nb = al.at(o, "hT", [128, 22, SB], BF16); o += nb
    accg = []; accv = []; gact = []
    for i in range(2):
        t, nb = al.at(o, "accg", [128, 512], F32); accg.append(t); o += nb
        t, nb = al.at(o, "accv", [128, 512], F32); accv.append(t); o += nb
        t, nb = al.at(o, "gact", [128, 512], BF16); gact.append(t); o += nb
    al.cur = max(r2_end_a, o)
    R2_A = ["wbig"]
    R2_E = ["hT", "accg", "accv", "gact"]
    r1 = al.cur
    memT, _nb = al.at(r1, "memT", [128, 8, NMEM], BF16)
    o = r1
    qT, nb = al.at(o, "qT", [128, 4, SB], BF16); o += nb
    kT, nb = al.at(o, "kT", [128, 4, SB], BF16); o += nb
    sqT, nb = al.at(o, "sqT", [128, 4, SB], BF16); o += nb
    og, nb = al.at(o, "og", [128, NT, 512], F32); o += nb
    hcat, nb = al.at(o, "hcat", [128, NT, D], BF16); o += nb
    hcT = []
    for i in range(2):
        t, nb = al.at(o, "hcT", [128, 8, 128], BF16); hcT.append(t); o += nb
    r1_end_ab = o
    o = r1
    qTx, nb = al.at(o, "qTx", [128, 8, SB], BF16); o += nb
    p_sb = []; pT_sb = []; oT_sb = []
    for i in range(2):
        t, nb = al.at(o, "p", [128, 4, NMEM], BF16); p_sb.append(t); o += nb
        t, nb = al.at(o, "pT", [128, 8, 128], BF16); pT_sb.append(t); o += nb
        t, nb = al.at(o, "oT", [128, 8, 128], BF16); oT_sb.append(t); o += nb
    r1_end_d = o
    o = r1
    wd, nb = al.at(o, "wd", [128, 22, D], BF16); o += nb
    r1_end_e = o
    al.cur = max(r1_end_ab, r1_end_d, r1_end_e)
    R1_AB = ["qT", "kT", "sqT", "og", "hcat", "hcT"]
    R1_D = ["qTx", "p", "pT", "oT"]
    R1_E = ["wd"]
    assert al.cur <= al.top, (al.cur, al.top)

    PS = [es.enter_context(nc.psum_tensor(f"ps{i}", [128, 1024], F32)) for i in range(4)]
    big_i = [0]
    small_i2 = [0]

    def ps_big():
        i = big_i[0] % 2
        big_i[0] += 1
        return PS[i], [k.R("ps", i, 0), k.R("ps", i, 1)]

    def ps_small():
        j = small_i2[0] % 4
        small_i2[0] += 1
        i, h = 2 + j // 2, j % 2
        return PS[i][:, h * 512:(h + 1) * 512], [k.R("ps", i, h)]

    def sm(n):
        assert n <= 16
        g_ = small_i[0] % 32
        small_i[0] += 1
        return smalls[:, g_ * 16:g_ * 16 + n], k.R("sm", g_)

    ident = cst[:, 0, :]
    triU = cst[:, 1, :]
    ones = cst[:, 2, :]
    Rc = k.R("cst")

    states = []
    tile_off = part * (SEQ // 128)
    k.op("sp", "dma_start", out=cst[:], in_=cst_d[:, :, :], wr=[Rc], dma=k.dma_tl("cst"))
    k.op("dve", "tensor_copy", out=ident_bf[:], in_=ident, rd=[Rc], wr=[k.R("identbf")])
    Rib = k.R("identbf")
    k.op("pool", "dma_start", out=biasT[:], in_=bias_d[:, :, :, :], wr=[k.R("biasT")], dma=k.dma_tl("biasT"))
    for l in range(L):
        k.op("sp", "dma_start", out=pp[l][:], in_=pp_d[l], wr=[k.R("pp", l)], dma=k.dma_tl("pp", l))
        k.op("sp", "dma_start", out=bvs[l][:], in_=bvs_d[l], wr=[k.R("bvs", l)], dma=k.dma_tl("bvs", l))
        k.op("act", "activation", out=esk[l][:], in_=bvs[l][:, 512:520], func=AF.Exp,
             rd=[k.R("bvs", l)], wr=[k.R("esk", l)])
        k.op("dve", "memset", Cst[l][:], 0.0, wr=[k.R("C", l, h) for h in range(4)])
        k.op("dve", "memset", Fc[l][:], 0.0, wr=[k.R("Fc", l)])
        k.op("dve", "memset", Mc[l][:], 0.0, wr=[k.R("Mc", l)])
        k.op("dve", "memset", hq[l][:], 0.0, wr=[k.R("hq", l)])
        k.op("dve", "memset", hf[l][:], 0.0, wr=[k.R("hf", l)])
        for g in range(2):
            k.op("dve", "memset", skT[l][g][:], 0.0, wr=[k.R("sk", l, g, "p"), k.R("sk", l, g, "c")])
        k.op("dve", "memset", sv[l][:], 0.0, wr=[k.R("sv", l, n) for n in range(-1, NT)])
        k.op("dve", "memset", sv[l][:, :, :, 64:65], 1.0, wr=[k.R("sv", l, n) for n in range(-1, NT)])
        states += [
            (f"C{l}", Cst[l][:], [128, 4, 129], [k.R("C", l, h) for h in range(4)]),
            (f"Fc{l}", Fc[l][:], [128, 4], [k.R("Fc", l)]),
            (f"Mc{l}", Mc[l][:], [128, 4], [k.R("Mc", l)]),
            (f"hq{l}", hq[l][:], [128, 8, 3], [k.R("hq", l)]),
            (f"hf{l}", hf[l][:], [128, 44, 2], [k.R("hf", l)]),
            (f"sk0{l}", skT[l][0][:, 0:128], [128, 128], [k.R("sk", l, 0, "p")]),
            (f"sk1{l}", skT[l][1][:, 0:128], [128, 128], [k.R("sk", l, 1, "p")]),
            (f"sv{l}", sv[l][:, 0, :, :], [128, 2, 65], [k.R("sv", l, -1)]),
        ]
    k.op("dve", "memset", v_ext[:], 1.0, wr=[k.R("v", n) for n in range(NT)])
    if part > 0:
        stg, _nb2 = al.at(r1 + 8192, "ststage", [128, 400], F32)
        so = 0
        for (nm, ap_, shp, rs) in states:
            d_ = dram("sti_" + nm, shp)
            if nm.startswith("sk") or nm.startswith("sv"):
                nel = int(np.prod(shp[1:]))
                if so + nel > 400:
                    so = 0
                st_ap = stg[:, so:so + nel]
                Rs = k.R("ststage", so)
                so += nel
                src = d_ if len(shp) == 2 else d_.rearrange("p a b -> p (a b)")
                k.op("sp", "dma_start", out=st_ap, in_=src, wr=[Rs], dma=k.dma_tl("st", nm))
                dst = ap_ if len(shp) == 2 else ap_
                if len(shp) == 3:
                    st_ap = st_ap.rearrange("p (a b) -> p a b", a=shp[1])
                k.op("dve", "tensor_copy", out=dst, in_=st_ap, rd=[Rs], wr=rs)
            else:
                k.op("sp", "dma_start", out=ap_, in_=d_, wr=rs, dma=k.dma_tl("st", nm))
    for mt in range(2):
        k.op("sp", "dma_start", out=x_sb[:, mt, :], in_=mem_d[mt * 128:(mt + 1) * 128, :],
             wr=[k.R("x", mt)], dma=k.dma_tl("xin", mt))
        ps, pr = ps_big()
        for kc in range(8):
            k.op("pe", "transpose", ps[:, kc * 128:(kc + 1) * 128], x_sb[:, mt, kc * 128:(kc + 1) * 128], ident,
                 rd=[k.R("x", mt), Rc], wr=[pr[kc // 4]], inc=(kc % 4 == 3), join=(kc % 4 != 0))
        k.op("act", "activation", out=memT[:, :, mt * 128:(mt + 1) * 128],
             in_=ps[:].rearrange("p (c t) -> p c t", c=8), func=AF.Copy, rd=pr, wr=[k.R("memT", mt)])
    for l in range(L):
        for c4 in range(4):
            slot = c4 % 2
            k.op("pool", "dma_start", out=wbig[slot][:, :, 0:512],
                 in_=wkv_d[l].rearrange("(kc p) n -> p kc n", p=128)[:, :, c4 * 512:(c4 + 1) * 512],
                 wr=[k.R("wbig", slot)], dma=k.dma_tl("wbig", slot))
            if c4 < 2:
                for dcl in range(4):
                    dc = c4 * 4 + dcl
                    ps, pr = ps_small()
                    for kc in range(8):
                        k.op("pe", "matmul", ps[:, 0:NMEM], lhsT=wbig[slot][:, kc, dcl * 128:(dcl + 1) * 128],
                             rhs=memT[:, kc, :], start=(kc == 0), stop=(kc == 7),
                             rd=[k.R("wbig", slot), k.R("memT", 0), k.R("memT", 1)], wr=pr,
                             inc=(kc == 7), join=(kc != 0))
                    k.op("act", "activation", out=kTm[l][:, dc, :], in_=ps[:, 0:NMEM], func=AF.Copy,
                         rd=pr, wr=[k.R("kTm", l, dc)])
            else:
                hh = c4 - 2
                for mt in range(2):
                    ps, pr = ps_small()
                    for kc in range(8):
                        k.op("pe", "matmul", ps[:, 0:512], lhsT=memT[:, kc, mt * 128:(mt + 1) * 128],
                             rhs=wbig[slot][:, kc, 0:512], start=(kc == 0), stop=(kc == 7),
                             rd=[k.R("wbig", slot), k.R("memT", mt)], wr=pr, inc=(kc == 7), join=(kc != 0))
                    k.op("act", "activation", out=vm[l][:, mt, hh * 512:(hh + 1) * 512], in_=ps[:, 0:512],
                         func=AF.Copy, rd=pr, wr=[k.R("vm", l, mt, hh)])

    k.poison_names(R1_AB + R1_D + R1_E, k.snapshot())
    lnv_i = [0]

    def load_lnv(l, j):
        s = lnv_i[0] % 2
        lnv_i[0] += 1
        k.op("sp", "dma_start", out=lnv[s][:], in_=lnv_d[l, j], wr=[k.R("lnv", s)], dma=k.dma_tl("lnv", s))
        return s

    def x_transposes(n):
        ps, pr = ps_big()
        for kc in range(8):
            k.op("pe", "transpose", ps[:, kc * 128:(kc + 1) * 128], x_sb[:, n, kc * 128:(kc + 1) * 128], ident,
                 rd=[k.R("x", n), Rc], wr=[pr[kc // 4]], inc=(kc % 4 == 3), join=(kc % 4 != 0))
        k.op("act", "activation", out=xT[:, :, n * 128:(n + 1) * 128],
             in_=ps[:].rearrange("p (c t) -> p c t", c=8), func=AF.Copy, rd=pr, wr=[k.R("xT", n)])

    def residual_ln(n, ps, pr, ls, last=False):
        Rx = k.R("x", n)
        xa = x_sb[:, n, :]
        k.op("dve", "scalar_tensor_tensor", out=xa, in0=xa, scalar=float(ALPHA), in1=ps[:, :],
             op0=ALU.mult, op1=ALU.add, rd=pr, wr=[Rx])
        st, Rst = sm(12)
        k.op("dve", "bn_stats", out=st[:, 0:6], in_=x_sb[:, n, 0:512], rd=[Rx], wr=[Rst])
        k.op("dve", "bn_stats", out=st[:, 6:12], in_=x_sb[:, n, 512:1024], rd=[Rx], wr=[Rst], join=True)
        mv, Rmv = sm(4)
        k.op("dve", "bn_aggr", out=mv[:, 0:2], in_=st, rd=[Rst], wr=[Rmv])
        k.op("dve", "tensor_scalar", out=mv[:, 2:3], in0=mv[:, 1:2], scalar1=float(EPS), scalar2=None,
             op0=ALU.add, rd=[Rmv], wr=[Rmv])
        k.op("act", "activation", out=mv[:, 2:3], in_=mv[:, 2:3], func=AF.Sqrt, rd=[Rmv], wr=[Rmv])
        k.op("dve", "reciprocal", out=mv[:, 2:3], in_=mv[:, 2:3], rd=[Rmv], wr=[Rmv])
        k.op("pool", "memset", pdummy[:, 0:4], 0.0, rd=[Rx], wr=[k.R("pdummy")])
        k.op("dve", "scalar_tensor_tensor", out=xa, in0=xa, scalar=mv[:, 0:1], in1=lnv[ls][:, 0:1024],
             op0=ALU.subtract, op1=ALU.mult, rd=[Rx, Rmv, k.R("lnv", ls)], wr=[Rx])
        k.op("dve", "scalar_tensor_tensor", out=xa, in0=xa, scalar=mv[:, 2:3], in1=lnv[ls][:, 1024:2048],
             op0=ALU.mult, op1=ALU.add, rd=[Rx, Rmv, k.R("lnv", ls)], wr=[Rx])

    def proj_tm(n, w_t, Rw, lhs_t, Rl):
        ps, pr = ps_big()
        nk = 8
        for half in range(2):
            for kc in range(nk):
                k.op("pe", "matmul", ps[:, half * 512:(half + 1) * 512], lhsT=lhs_t[:, kc, :],
                     rhs=w_t[:, kc, half * 512:(half + 1) * 512], start=(kc == 0), stop=(kc == nk - 1),
                     rd=[Rw, Rl], wr=[pr[half]], inc=(kc == nk - 1), join=(kc != 0))
        return ps, pr

    wfm_i = [0]
    wbig_i = [0]
    sbi_cur = [0]
    scratch = {}

    def wload(dst_ap, Rdst, tlkey, src_ap, key, join=False):
        if sbi_cur[0] == 0:
            k.op("pool", "dma_start", out=dst_ap, in_=src_ap, wr=[Rdst], dma=k.dma_tl(*tlkey), join=join)
            d = nc.dram_tensor("scr_" + "_".join(str(x_) for x_ in key), list(dst_ap.shape), BF16, kind="Internal").ap()
            scratch[key] = d
            k.op("sp", "dma_start", out=d, in_=dst_ap, rd=[Rdst], wr=[k.R("scr", key)], dma=k.dma_tl("wsc", *tlkey))
        else:
            k.op("sp", "dma_start", out=dst_ap, in_=scratch[key], rd=[k.R("scr", key)], wr=[Rdst],
                 dma=k.dma_tl(*tlkey), join=join)

    def load_wfm(src_v, c0, c1, key):
        s = wfm_i[0] % 3
        wfm_i[0] += 1
        wload(wfm[s][:, :, 0:c1 - c0], k.R("wfm", s), ("wfm", s), src_v[:, :, c0:c1], key + (c0,))
        return s

    def dump_x(t0):
        for n in range(NT):
            k.op("sp", "dma_start", out=out_d[t0 + n * 128:t0 + (n + 1) * 128, :], in_=x_sb[:, n, :],
                 rd=[k.R("x", n)], dma=k.dma_tl("xout", n))

    def chk(tag):
        if stop == tag:
            dump_x(0)
            raise _Stop()

    try:
        chk("pro")
        for sbi in range(NSB):
            t0 = sbi * SB
            first_sb = (sbi == 0)
            sbi_cur[0] = sbi
            chk('sb%d' % sbi)
            for n in range(NT):
                k.op("sp", "dma_start", out=x_sb[:, n, :], in_=x_d[t0 + n * 128:t0 + (n + 1) * 128, :],
                     wr=[k.R("x", n)], dma=k.dma_tl("xin", n))
            for n in range(NT):
                x_transposes(n)
            chk('X')
            done = False
            for l in range(L):
                Rpp = k.R("pp", l)
                Rbv = k.R("bvs", l)
                cw = pp[l][:, 0:32].rearrange("p (c j) -> p c j", j=4)
                cb = pp[l][:, 32:40]
                fw = pp[l][:, 40:172].rearrange("p (c j) -> p c j", j=3)
                fb = pp[l][:, 172:216]
                win = w_in_d[l].rearrange("(kc p) n -> p kc n", p=128)
                sv_slot = 0
                wload(wbig[sv_slot][:], k.R("wbig", sv_slot), ("wbig", sv_slot), win[:, :, N_FM:N_FM + 1024], ("wvo", l))
                wload(wvg[:], k.R("wvg"), ("wvg",), win[:, :, N_FM + 1024:N_FM + 1160], ("wvg", l))
                wload(wbig[1][:], k.R("wbig", 1), ("wbig", 1), w_out_d[l].rearrange("(kc p) n -> p kc n", p=128), ("wout", l))
                Rhq = k.R("hq", l)
                Rcq = k.R("corrq")
                h0, h1, h2 = hq[l][:, :, 0], hq[l][:, :, 1], hq[l][:, :, 2]
                w0, w1, w2 = cw[:, :, 0], cw[:, :, 1], cw[:, :, 2]
                tq = tmpq[:, 0:8]
                Rtq = k.R("tmpq")

                def tt(out, a, b, op, rd, wr):
                    k.op("dve", "tensor_tensor", out=out, in0=a, in1=b, op=op, rd=rd, wr=wr)
                tt(corrq[:, :, 0], h2, w2, ALU.mult, [Rhq, Rpp], [Rcq])
                tt(tq, h1, w1, ALU.mult, [Rhq, Rpp], [Rtq])
                tt(corrq[:, :, 0], corrq[:, :, 0], tq, ALU.add, [Rcq, Rtq], [Rcq])
                tt(tq, h0, w0, ALU.mult, [Rhq, Rpp], [Rtq])
                tt(corrq[:, :, 0], corrq[:, :, 0], tq, ALU.add, [Rcq, Rtq], [Rcq])
                tt(corrq[:, :, 1], h2, w1, ALU.mult, [Rhq, Rpp], [Rcq])
                tt(tq, h1, w0, ALU.mult, [Rhq, Rpp], [Rtq])
                tt(corrq[:, :, 1], corrq[:, :, 1], tq, ALU.add, [Rcq, Rtq], [Rcq])
                tt(corrq[:, :, 2], h2, w0, ALU.mult, [Rhq, Rpp], [Rcq])
                chk('A0')
                slots = {}
                slots[0] = load_wfm(win, 0, 256, ('win', l))
                slots[1] = load_wfm(win, 256, 512, ('win', l))
                RxT = [k.R("xT", n) for n in range(NT)]
                for c in range(13):
                    bq = c // 2
                    if c % 2 == 0 and bq + 2 < 7:
                        slots[bq + 2] = load_wfm(win, (bq + 2) * 256, min((bq + 3) * 256, N_FM), ('win', l))
                    s = slots[bq]
                    sub = c % 2
                    ps, pr = ps_small()
                    for kc in range(8):
                        k.op("pe", "matmul", ps[:, 0:SB], lhsT=wfm[s][:, kc, sub * 128:(sub + 1) * 128], rhs=xT[:, kc, :],
                             start=(kc == 0), stop=(kc == 7), rd=[k.R("wfm", s)] + RxT, wr=pr,
                             inc=(kc == 7), join=(kc != 0))
                    if c < 8:
                        dst = qT if c < 4 else kT
                        Rd = k.R("qT" if c < 4 else "kT", c % 4)
                        a = c % 2
                        Ra = k.R("accq", a)
                        acc = accq[a]
                        k.op("act", "activation", out=acc[:, :], in_=ps[:, 0:SB], func=AF.Identity,
                             scale=cw[:, c, 3:4], bias=cb[:, c:c + 1], rd=pr + [Rpp], wr=[Ra])
                        for j, sh in ((2, 1), (1, 2), (0, 3)):
                            k.op("dve", "scalar_tensor_tensor", out=acc[:, sh:SB], in0=ps[:, 0:SB - sh],
                                 scalar=cw[:, c, j:j + 1], in1=acc[:, sh:SB], op0=ALU.mult, op1=ALU.add,
                                 rd=pr + [Rpp, Ra], wr=[Ra])
                        k.op("dve", "tensor_tensor", out=acc[:, 0:3], in0=acc[:, 0:3], in1=corrq[:, c, :],
                             op=ALU.add, rd=[Ra, Rcq], wr=[Ra])
                        k.op("act", "activation", out=hq[l][:, c, :], in_=ps[:, SB - 3:SB], func=AF.Copy,
                             rd=pr, wr=[Rhq])
                        k.op("act", "activation", out=dst[:, c % 4, :], in_=acc[:, :], func=AF.Silu,
                             rd=[Ra], wr=[Rd])
                    elif c < 12:
                        k.op("act", "activation", out=sqT[:, c - 8, :], in_=ps[:, 0:SB], func=AF.Copy, scale=0.125,
                             rd=pr, wr=[k.R("sqT", c - 8)])
                    else:
                        k.op("act", "activation", out=skT[l][0][0:64, 128:128 + SB], in_=ps[0:64, 0:SB],
                             func=AF.Copy, rd=pr, wr=[k.R("sk", l, 0, "c")])
                        k.op("dve", "tensor_copy", out=skT[l][1][64:128, 128:128 + SB], in_=ps[64:128, 0:SB],
                             rd=pr, wr=[k.R("sk", l, 1, "c")])
                chk('A1')
                Rwv = k.R("wbig", sv_slot)
                for n in range(NT):
                    ps, pr = ps_small()
                    for kc in range(8):
                        k.op("pe", "matmul", ps[:, 0:512], lhsT=xT[:, kc, n * 128:(n + 1) * 128],
                             rhs=wbig[sv_slot][:, kc, 0:512], start=(kc == 0), stop=(kc == 7),
                             rd=[Rwv, RxT[n]], wr=pr, inc=(kc == 7), join=(kc != 0))
                    k.op("act", "activation", out=v_ext[:, n, :, 0:128],
                         in_=ps[:, 0:512].rearrange("p (h d) -> p h d", h=4), func=AF.Copy, rd=pr, wr=[k.R("v", n)])
                    chk('A2v')
                    ps, pr = ps_small()
                    for kc in range(8):
                        k.op("pe", "matmul", ps[:, 0:512], lhsT=xT[:, kc, n * 128:(n + 1) * 128],
                             rhs=wbig[sv_slot][:, kc, 512:1024], start=(kc == 0), stop=(kc == 7),
                             rd=[Rwv, RxT[n]], wr=pr, inc=(kc == 7), join=(kc != 0))
                    k.op("act", "activation", out=og[:, n, :], in_=ps[:, 0:512], func=AF.Sigmoid, rd=pr,
                         wr=[k.R("og", n)])
                    chk('A2o')
                    ps, pr = ps_small()
                    for kc in range(8):
                        k.op("pe", "matmul", ps[:, 0:136], lhsT=xT[:, kc, n * 128:(n + 1) * 128],
                             rhs=wvg[:, kc, :], start=(kc == 0), stop=(kc == 7),
                             rd=[k.R("wvg"), RxT[n]], wr=pr, inc=(kc == 7), join=(kc != 0))
                    k.op("act", "activation", out=sv[l][:, 1 + n, :, 0:64],
                         in_=ps[:, 0:128].rearrange("p (h d) -> p h d", h=2), func=AF.Copy, rd=pr,
                         wr=[k.R("sv", l, n)])
                    chk('A2s%d' % n)
                    k.op("act", "activation", out=gif[:, n, :], in_=ps[:, 128:136], func=AF.Copy, rd=pr, wr=[k.R("gif", n)])
                    chk('A2g%d' % n)
                chk('A')
                Rg = k.R("gates")
                Rgif = [k.R("gif", n) for n in range(NT)]
                G = gwork
                NG4 = NT * 4

                def v3(t):
                    return t[:, :].rearrange("p (n h) -> p n h", h=4)
                ib = bvs[l][:, 520:524]
                fbias = bvs[l][:, 524:528]
                for n in range(NT):
                    k.op("dve", "tensor_tensor", out=v3(G["lf"])[:, n, :], in0=gif[:, n, 4:8], in1=fbias, op=ALU.add,
                         rd=Rgif + [Rbv], wr=[Rg])
                    k.op("dve", "tensor_tensor", out=v3(G["d"])[:, n, :], in0=gif[:, n, 0:4], in1=ib, op=ALU.add,
                         rd=Rgif + [Rbv], wr=[Rg])
                k.op("act", "activation", out=G["lf"][:, :], in_=G["lf"][:, :], func=AF.Exp, scale=-1.0, rd=[Rg], wr=[Rg])
                k.op("act", "activation", out=G["lf"][:, :], in_=G["lf"][:, :], func=AF.Ln, bias=1.0, rd=[Rg], wr=[Rg])
                k.op("dve", "tensor_scalar", out=G["lf"][:, :], in0=G["lf"][:, :], scalar1=-1.0, scalar2=None,
                     op0=ALU.mult, rd=[Rg], wr=[Rg])
                ps, pr = ps_small()
                k.op("pe", "matmul", ps[:, 0:NG4], lhsT=triU, rhs=G["lf"][:, :], start=True, stop=True,
                     rd=[Rc, Rg], wr=pr)
                ps2, pr2 = ps_small()
                k.op("pe", "matmul", ps2[:, 0:NG4], lhsT=ones, rhs=G["lf"][:, :], start=True, stop=True,
                     rd=[Rc, Rg], wr=pr2)
                k.op("dve", "tensor_copy", out=G["tmb"][:, :], in_=ps2[:, 0:NG4], rd=pr2, wr=[Rg])
                RF = k.R("Fc", l)
                P3 = v3(G["P"])
                T3 = v3(G["tmb"])
                k.op("dve", "tensor_copy", out=P3[:, 0, :], in_=Fc[l][:, :], rd=[RF], wr=[Rg])
                for n in range(1, NT):
                    k.op("dve", "tensor_tensor", out=P3[:, n, :], in0=P3[:, n - 1, :], in1=T3[:, n - 1, :], op=ALU.add,
                         rd=[Rg], wr=[Rg])
                k.op("dve", "tensor_tensor", out=Fc[l][:, :], in0=P3[:, NT - 1, :], in1=T3[:, NT - 1, :], op=ALU.add,
                     rd=[Rg], wr=[RF])
                k.op("dve", "tensor_tensor", out=G["F"][:, :], in0=G["P"][:, :], in1=ps[:, 0:NG4], op=ALU.add,
                     rd=[Rg] + pr, wr=[Rg])
                k.op("dve", "tensor_tensor", out=G["d"][:, :], in0=G["d"][:, :], in1=G["F"][:, :], op=ALU.subtract,
                     rd=[Rg], wr=[Rg])
                ps, pr = ps_small()
                k.op("pe", "transpose", ps[0:NG4, 0:128], G["d"][:, :], ident, rd=[Rc, Rg], wr=pr)
                tm, Rtm = sm(1)
                k.op("dve", "tensor_reduce", out=tm[0:NG4, :], in_=ps[0:NG4, 0:128], axis=AX.X, op=ALU.max,
                     rd=pr, wr=[Rtm])
                dg, Rdg = sm(NG4)
                k.op("dve", "tensor_scalar", out=dg[0:NG4, :], in0=ident[0:NG4, 0:NG4], scalar1=tm[0:NG4, 0:1],
                     scalar2=None, op0=ALU.mult, rd=[Rtm, Rc], wr=[Rdg])
                ps, pr = ps_small()
                k.op("pe", "matmul", ps[:, 0:NG4], lhsT=ones[0:NG4, :], rhs=dg[0:NG4, :], start=True, stop=True,
                     rd=[Rc, Rdg], wr=pr)
                RM = k.R("Mc", l)
                M3 = v3(G["mrun"])
                MP3 = v3(G["mprev"])
                pst = ps[:, 0:NG4].rearrange("p (n h) -> p n h", h=4)
                k.op("dve", "tensor_copy", out=MP3[:, 0, :], in_=Mc[l][:, :], rd=[RM], wr=[Rg])
                for n in range(NT):
                    k.op("dve", "tensor_tensor", out=M3[:, n, :], in0=MP3[:, n, :], in1=pst[:, n, :], op=ALU.max,
                         rd=[Rg] + pr, wr=[Rg])
                    if n + 1 < NT:
                        k.op("dve", "tensor_copy", out=MP3[:, n + 1, :], in_=M3[:, n, :], rd=[Rg], wr=[Rg])
                k.op("dve", "tensor_copy", out=Mc[l][:, :], in_=M3[:, NT - 1, :], rd=[Rg], wr=[RM])
                k.op("dve", "tensor_tensor", out=G["wk"][:, :], in0=G["d"][:, :], in1=G["mrun"][:, :], op=ALU.subtract,
                     rd=[Rg], wr=[Rg])
                k.op("act", "activation", out=G["wk"][:, :], in_=G["wk"][:, :], func=AF.Exp, rd=[Rg], wr=[Rg])
                k.op("dve", "tensor_scalar", out=G["wk"][:, :], in0=G["wk"][:, :], scalar1=float(128 ** -0.5),
                     scalar2=None, op0=ALU.mult, rd=[Rg], wr=[Rg])
                k.op("dve", "tensor_tensor", out=G["gam"][:, :], in0=G["mprev"][:, :], in1=G["mrun"][:, :],
                     op=ALU.subtract, rd=[Rg], wr=[Rg])
                k.op("act", "activation", out=G["gam"][:, :], in_=G["gam"][:, :], func=AF.Exp, rd=[Rg], wr=[Rg])
                k.op("dve", "tensor_tensor", out=G["clamp"][:, :], in0=G["F"][:, :], in1=G["mrun"][:, :], op=ALU.add,
                     rd=[Rg], wr=[Rg])
                k.op("act", "activation", out=G["clamp"][:, :], in_=G["clamp"][:, :], func=AF.Exp, scale=-1.0,
                     rd=[Rg], wr=[Rg])
                chk('B1')
                ls = load_lnv(l, 0)
                Rwo = k.R("wbig", 1)
                for n in range(NT):
                    tsl = slice(n * 128, (n + 1) * 128)
                    pst_, prt = ps_small()
                    pst_bf = pst_.bitcast(BF16)
                    pss, prs = ps_small()
                    for h in range(4):
                        k.op("pe", "transpose", pst_bf[:, h * 128:(h + 1) * 128], kT[:, h, tsl], ident_bf[:],
                             rd=[k.R("kT", h), Rib], wr=prt, inc=(h == 3), join=(h != 0))
                    for h in range(4):
                        k.op("pe", "matmul", pss[:, h * 128:(h + 1) * 128], lhsT=kT[:, h, tsl], rhs=qT[:, h, tsl],
                             start=True, stop=True, rd=[k.R("kT", h), k.R("qT", h)], wr=prs, inc=(h == 3),
                             join=(h != 0))
                    for h in range(4):
                        wkc = G["wk"][:, n * 4 + h:n * 4 + h + 1]
                        k.op("act", "activation", out=kw_sb[h][:, :], in_=pst_bf[:, h * 128:(h + 1) * 128],
                             func=AF.Copy, scale=wkc, rd=prt + [Rg], wr=[k.R("kw", h)])
                        k.op("dve", "scalar_tensor_tensor", out=AT_sb[h][:, :], in0=pss[:, h * 128:(h + 1) * 128],
                             scalar=wkc, in1=triU, op0=ALU.mult, op1=ALU.mult, rd=prs + [Rg, Rc],
                             wr=[k.R("AT", h)])
                        gmc = G["gam"][:, n * 4 + h:n * 4 + h + 1]
                        k.op("act", "activation", out=Cb[h][:, :], in_=Cst[l][:, h, :], func=AF.Copy, scale=gmc,
                             rd=[k.R("C", l, h), Rg], wr=[k.R("Cb", h)])
                    for h in range(4):
                        gmc = G["gam"][:, n * 4 + h:n * 4 + h + 1]
                        psn, prn = ps_small()
                        k.op("pe", "matmul", psn[:, 0:129], lhsT=AT_sb[h][:, :], rhs=v_ext[:, n, h, :],
                             start=True, stop=False, rd=[k.R("AT", h), k.R("v", n)], wr=prn, inc=False)
                        k.op("pe", "matmul", psn[:, 0:129], lhsT=qT[:, h, tsl], rhs=Cb[h][:, :],
                             start=False, stop=True, rd=[k.R("qT", h), k.R("Cb", h)], wr=prn, join=True)
                        psc, prc = ps_small()
                        k.op("pe", "matmul", psc[:, 0:129], lhsT=kw_sb[h][:, :], rhs=v_ext[:, n, h, :],
                             start=True, stop=True, rd=[k.R("kw", h), k.R("v", n)], wr=prc)
                        k.op("dve", "scalar_tensor_tensor", out=Cst[l][:, h, :], in0=Cst[l][:, h, :], scalar=gmc,
                             in1=psc[:, 0:129], op0=ALU.mult, op1=ALU.add, rd=prc + [Rg], wr=[k.R("C", l, h)])
                        dn, Rdn = sm(2)
                        k.op("act", "activation", out=dn[:, 0:1], in_=psn[:, 128:129], func=AF.Abs, rd=prn, wr=[Rdn])
                        k.op("dve", "tensor_tensor", out=dn[:, 0:1], in0=dn[:, 0:1],
                             in1=G["clamp"][:, n * 4 + h:n * 4 + h + 1], op=ALU.max, rd=[Rdn, Rg], wr=[Rdn])
                        k.op("dve", "reciprocal", out=dn[:, 1:2], in_=dn[:, 0:1], rd=[Rdn], wr=[Rdn])
                        k.op("act", "activation", out=hml[:, h, :], in_=psn[:, 0:128], func=AF.Copy, scale=dn[:, 1:2],
                             rd=prn + [Rdn], wr=[k.R("hml", h)])
                        st, Rst = sm(6)
                        k.op("dve", "bn_stats", out=st, in_=hml[:, h, :], rd=[k.R("hml", h)], wr=[Rst])
                        mv, Rmv = sm(4)
                        k.op("dve", "bn_aggr", out=mv[:, 0:2], in_=st, rd=[Rst], wr=[Rmv])
                        k.op("dve", "tensor_scalar", out=mv[:, 2:3], in0=mv[:, 1:2], scalar1=float(EPS), scalar2=None,
                             op0=ALU.add, rd=[Rmv], wr=[Rmv])
                        k.op("act", "activation", out=mv[:, 2:3], in_=mv[:, 2:3], func=AF.Sqrt, rd=[Rmv], wr=[Rmv])
                        k.op("dve", "reciprocal", out=mv[:, 2:3], in_=mv[:, 2:3], rd=[Rmv], wr=[Rmv])
                        k.op("dve", "tensor_scalar", out=hml[:, h, :], in0=hml[:, h, :], scalar1=mv[:, 0:1],
                             scalar2=mv[:, 2:3], op0=ALU.subtract, op1=ALU.mult, rd=[Rmv, k.R("hml", h)],
                             wr=[k.R("hml", h)])
                    Rh = [k.R("hml", h) for h in range(4)]
                    hml2 = hml[:].rearrange("p h d -> p (h d)")
                    k.op("dve", "tensor_tensor", out=hml2, in0=hml2, in1=bvs[l][:, 0:512], op=ALU.mult,
                         rd=Rh + [Rbv], wr=Rh)
                    k.op("dve", "tensor_tensor", out=hcat[:, n, 0:512], in0=hml2, in1=og[:, n, :], op=ALU.mult,
                         rd=Rh + [k.R("og", n)], wr=[k.R("hcat", n, 0)])
                    absn = tile_off + sbi * NT + n
                    jl = [1] if absn == 0 else [0, 1]
                    for g in range(2):
                        pso, pro = ps_small()
                        for j in jl:
                            psl, prl = ps_small()
                            kcol = slice((n + j) * 128, (n + j + 1) * 128)
                            Rk = k.R("sk", l, g, "c") if (n + j) >= 1 else k.R("sk", l, g, "p")
                            k.op("pe", "matmul", psl[:, 0:512], lhsT=skT[l][g][:, kcol], rhs=sqT[:, :, tsl],
                                 start=True, stop=False, rd=[Rk] + [k.R("sqT", c) for c in range(4)], wr=prl,
                                 inc=False)
                            k.op("pe", "matmul", psl[:, 0:512], lhsT=ident_bf[:], rhs=biasT[:, j, g * 4:(g + 1) * 4, :],
                                 start=False, stop=True, rd=[Rib, k.R("biasT")], wr=prl, join=True)
                            pi = 2 * g + j
                            k.op("act", "activation", out=PT_sb[pi][:].rearrange("p c q -> p (c q)"), in_=psl[:, 0:512],
                                 func=AF.Exp, rd=prl, wr=[k.R("PT", pi)])
                        for c in range(4):
                            for j in jl:
                                pi = 2 * g + j
                                k.op("pe", "matmul", pso[:, c * 65:(c + 1) * 65], lhsT=PT_sb[pi][:, c, :],
                                     rhs=sv[l][:, n + j, g, :], start=(j == jl[0]), stop=(j == 1),
                                     rd=[k.R("PT", pi), k.R("sv", l, n + j - 1)], wr=pro, inc=(c == 3 and j == 1),
                                     join=not (c == 0 and j == jl[0]))
                        den, Rden = sm(8)
                        po3 = pso[:, 0:260].rearrange("p (c e) -> p c e", e=65)
                        k.op("dve", "tensor_tensor", out=den[:, 0:4], in0=po3[:, :, 64], in1=esk[l][:, g * 4:(g + 1) * 4],
                             op=ALU.add, rd=pro + [k.R("esk", l)], wr=[Rden])
                        k.op("dve", "reciprocal", out=den[:, 4:8], in_=den[:, 0:4], rd=[Rden], wr=[Rden])
                        for c in range(4):
                            hh = g * 4 + c
                            eng = "act" if c % 2 == 0 else "dve"
                            if eng == "act":
                                k.op("act", "activation", out=hcat[:, n, 512 + hh * 64:512 + (hh + 1) * 64],
                                     in_=po3[:, c, 0:64], func=AF.Copy, scale=den[:, 4 + c:5 + c],
                                     rd=pro + [Rden], wr=[k.R("hcat", n, 1 + hh)])
                            else:
                                k.op("dve", "tensor_scalar", out=hcat[:, n, 512 + hh * 64:512 + (hh + 1) * 64],
                                     in0=po3[:, c, 0:64], scalar1=den[:, 4 + c:5 + c], scalar2=None, op0=ALU.mult,
                                     rd=pro + [Rden], wr=[k.R("hcat", n, 1 + hh)])
                    Rhc = [k.R("hcat", n, i) for i in range(9)]
                    hs = n % 2
                    psT, prT = ps_small()
                    psT_bf = psT.bitcast(BF16)
                    for kc in range(8):
                        k.op("pe", "transpose", psT_bf[:, kc * 128:(kc + 1) * 128], hcat[:, n, kc * 128:(kc + 1) * 128],
                             ident_bf[:], rd=Rhc + [Rib], wr=prT, inc=(kc == 7), join=(kc != 0))
                    k.op("dve", "tensor_copy", out=hcT[hs][:].rearrange("p c t -> p (c t)"), in_=psT_bf[:, 0:1024],
                         rd=prT, wr=[k.R("hcT", hs)])
                    ps, pr = proj_tm(n, wbig[1], Rwo, hcT[hs], k.R("hcT", hs))
                    residual_ln(n, ps, pr, ls)
                for g in range(2):
                    k.op("act", "activation", func=AF.Copy, out=skT[l][g][:, 0:128], in_=skT[l][g][:, SB:SB + 128],
                         rd=[k.R("sk", l, g, "c")], wr=[k.R("sk", l, g, "p")])
                k.op("act", "activation", func=AF.Copy, out=sv[l][:, 0, :, :], in_=sv[l][:, NT, :, :],
                     rd=[k.R("sv", l, NT - 1)], wr=[k.R("sv", l, -1)])
                if stop == (l, 1):
                    dump_x(t0); done = True; break
                k.poison_names(R1_D, k.snapshot())
                wload(wbig[0][:], k.R("wbig", 0), ("wbig", 0), wxo_d[l].rearrange("(kc p) n -> p kc n", p=128), ("wxo", l))
                for n in range(NT):
                    x_transposes(n)
                RxT = [k.R("xT", n) for n in range(NT)]
                wqv = wq_d[l].rearrange("(kc p) n -> p kc n", p=128)
                slots = {0: load_wfm(wqv, 0, 256, ('wq', l)), 1: load_wfm(wqv, 256, 512, ('wq', l))}
                for dc in range(8):
                    bq = dc // 2
                    if dc % 2 == 0 and bq + 2 < 4:
                        slots[bq + 2] = load_wfm(wqv, (bq + 2) * 256, (bq + 3) * 256, ('wq', l))
                    s = slots[bq]
                    sub = dc % 2
                    ps, pr = ps_small()
                    for kc in range(8):
                        k.op("pe", "matmul", ps[:, 0:SB], lhsT=wfm[s][:, kc, sub * 128:(sub + 1) * 128], rhs=xT[:, kc, :],
                             start=(kc == 0), stop=(kc == 7), rd=[k.R("wfm", s)] + RxT, wr=pr,
                             inc=(kc == 7), join=(kc != 0))
                    k.op("act", "activation", out=qTx[:, dc, :], in_=ps[:, 0:SB], func=AF.Copy, scale=1.0 / 16.0,
                         rd=pr, wr=[k.R("qTx", dc)])
                ls = load_lnv(l, 1)
                Rwx = k.R("wbig", 0)
                RkT = [k.R("kTm", l, dc) for dc in range(8)]
                for n in range(NT):
                    tsl = slice(n * 128, (n + 1) * 128)
                    bs = n % 2
                    psl, prl = ps_big()
                    for h in range(4):
                        for hf_ in range(2):
                            dc = 2 * h + hf_
                            k.op("pe", "matmul", psl[:, h * 256:(h + 1) * 256], lhsT=qTx[:, dc, tsl], rhs=kTm[l][:, dc, :],
                                 start=(hf_ == 0), stop=(hf_ == 1), rd=[k.R("qTx", dc), RkT[dc]], wr=[prl[h // 2]],
                                 inc=(hf_ == 1 and h % 2 == 1), join=not (hf_ == 0 and h % 2 == 0))
                    mx, Rmx = sm(12)
                    k.op("dve", "tensor_reduce", out=mx[:, 0:4], in_=psl[:, :].rearrange("p (h m) -> p h m", h=4),
                         axis=AX.X, op=ALU.max, rd=prl, wr=[Rmx])
                    k.op("dve", "tensor_scalar", out=mx[:, 4:8], in0=mx[:, 0:4], scalar1=-1.0, scalar2=None,
                         op0=ALU.mult, rd=[Rmx], wr=[Rmx])
                    Rp = k.R("p", bs)
                    for h in range(4):
                        k.op("act", "activation", out=p_sb[bs][:, h, :], in_=psl[:, h * 256:(h + 1) * 256], func=AF.Exp,
                             bias=mx[:, 4 + h:5 + h], accum_out=mx[:, 8 + h:9 + h], rd=prl + [Rmx], wr=[Rp, Rmx],
                             join=(h != 0))
                    k.op("dve", "reciprocal", out=mx[:, 0:4], in_=mx[:, 8:12], rd=[Rmx], wr=[Rmx])
                    for h in range(4):
                        k.op("dve", "tensor_scalar", out=p_sb[bs][:, h, :], in0=p_sb[bs][:, h, :], scalar1=mx[:, h:h + 1],
                             scalar2=None, op0=ALU.mult, rd=[Rp, Rmx], wr=[Rp])
                    psT, prT = ps_small()
                    psT_bf = psT.bitcast(BF16)
                    for h in range(4):
                        for mt in range(2):
                            i = h * 2 + mt
                            k.op("pe", "transpose", psT_bf[:, i * 128:(i + 1) * 128], p_sb[bs][:, h, mt * 128:(mt + 1) * 128],
                                 ident_bf[:], rd=[Rp, Rib], wr=prT, inc=(i == 7), join=(i != 0))
                    k.op("dve", "tensor_copy", out=pT_sb[bs][:].rearrange("p c t -> p (c t)"), in_=psT_bf[:, 0:1024],
                         rd=prT, wr=[k.R("pT", bs)])
                    pso, pro = ps_big()
                    for dc in range(8):
                        h = dc // 2
                        for mt in range(2):
                            k.op("pe", "matmul", pso[:, dc * 128:(dc + 1) * 128], lhsT=vm[l][:, mt, dc * 128:(dc + 1) * 128],
                                 rhs=pT_sb[bs][:, h * 2 + mt, :], start=(mt == 0), stop=(mt == 1),
                                 rd=[k.R("pT", bs), k.R("vm", l, mt, dc // 4)], wr=[pro[dc // 4]],
                                 inc=(mt == 1 and dc % 4 == 3), join=not (mt == 0 and dc % 4 == 0))
                    k.op("act", "activation", out=oT_sb[bs][:].rearrange("p c t -> p (c t)"), in_=pso[:, :], func=AF.Copy,
                         rd=pro, wr=[k.R("oT", bs)])
                    ps, pr = proj_tm(n, wbig[0], Rwx, oT_sb[bs], k.R("oT", bs))
                    residual_ln(n, ps, pr, ls)
                if stop == (l, 2):
                    dump_x(t0); done = True; break
                snap = k.snapshot()
                k.poison_names(R1_E + R2_E, snap)
                wdv = wdn_d[l].rearrange("(j p) n -> p j n", p=128)
                Rwd = k.R("wd")
                for wp in range(2):
                    wload(wd[:, wp * 11:(wp + 1) * 11, :], Rwd, ("wd",), wdv[:, wp * 11:(wp + 1) * 11, :], ("wd", l, wp),
                          join=(wp == 1))
                for n in range(NT):
                    x_transposes(n)
                RxT = [k.R("xT", n) for n in range(NT)]
                Rhf = k.R("hf", l)
                Rcf = k.R("corrf")
                Rtq = k.R("tmpq")
                f0, f1 = hf[l][:, :, 0], hf[l][:, :, 1]
                tt(corrf[:, :, 0], f1, fw[:, :, 1], ALU.mult, [Rhf, Rpp], [Rcf])
                tt(tmpq[:, :], f0, fw[:, :, 0], ALU.mult, [Rhf, Rpp], [Rtq])
                tt(corrf[:, :, 0], corrf[:, :, 0], tmpq[:, :], ALU.add, [Rcf, Rtq], [Rcf])
                tt(corrf[:, :, 1], f1, fw[:, :, 0], ALU.mult, [Rhf, Rpp], [Rcf])
                wupv = wup_d[l].rearrange("(kc p) n -> p kc n", p=128)
                wup_i = [0]

                def load_wup(kind, b):
                    s = wup_i[0] % 4
                    wup_i[0] += 1
                    c0 = (0 if kind == "g" else DFF) + b * 256
                    wload(wup[s][:], k.R("wup", s), ("wup", s), wupv[:, :, c0:c0 + 256], ("wup", l, kind, b))
                    return s
                order = []
                for j in range(22):
                    order += [j, 22 + j]
                slots = {}
                for b in range(2):
                    slots[("g", b)] = load_wup("g", b)
                    slots[("v", b)] = load_wup("v", b)
                for idx, c in enumerate(order):
                    jj_ = c % 22
                    if c < 22 and jj_ % 2 == 0 and jj_ // 2 >= 1 and jj_ // 2 + 1 < 11:
                        b = jj_ // 2 + 1
                        slots[("g", b)] = load_wup("g", b)
                        slots[("v", b)] = load_wup("v", b)
                    s = slots[("g" if c < 22 else "v", jj_ // 2)]
                    sub = jj_ % 2
                    j = c % 22
                    isg = c < 22
                    ps, pr = ps_small()
                    for kc in range(8):
                        k.op("pe", "matmul", ps[:, 0:SB], lhsT=wup[s][:, kc, sub * 128:(sub + 1) * 128], rhs=xT[:, kc, :],
                             start=(kc == 0), stop=(kc == 7), rd=[k.R("wup", s)] + RxT, wr=pr,
                             inc=(kc == 7), join=(kc != 0))
                    a = j % 2
                    acc = accg[a] if isg else accv[a]
                    Ra = k.R("accg" if isg else "accv", a)
                    k.op("act", "activation", out=acc[:, :], in_=ps[:, 0:SB], func=AF.Identity,
                         scale=fw[:, c, 2:3], bias=fb[:, c:c + 1], rd=pr + [Rpp], wr=[Ra])
                    for jj, sh in ((1, 1), (0, 2)):
                        k.op("dve", "scalar_tensor_tensor", out=acc[:, sh:SB], in0=ps[:, 0:SB - sh],
                             scalar=fw[:, c, jj:jj + 1], in1=acc[:, sh:SB], op0=ALU.mult, op1=ALU.add,
                             rd=pr + [Rpp, Ra], wr=[Ra])
                    k.op("dve", "tensor_tensor", out=acc[:, 0:2], in0=acc[:, 0:2], in1=corrf[:, c, :],
                         op=ALU.add, rd=[Ra, Rcf], wr=[Ra])
                    k.op("act", "activation", out=hf[l][:, c, :], in_=ps[:, SB - 2:SB], func=AF.Copy, rd=pr, wr=[Rhf])
                    if isg:
                        k.op("act", "activation", out=gact[a][:, :], in_=acc[:, :], func=AF.Gelu_apprx_tanh,
                             rd=[Ra], wr=[k.R("gact", a)])
                    else:
                        k.op("dve", "tensor_tensor", out=hT[:, j, :], in0=gact[a][:, :], in1=acc[:, :], op=ALU.mult,
                             rd=[k.R("gact", a), Ra], wr=[k.R("hT", j)])
                ls = load_lnv(l, 2)
                RhT = [k.R("hT", j) for j in range(22)]
                for n in range(NT):
                    tsl = slice(n * 128, (n + 1) * 128)
                    ps, pr = ps_big()
                    for half in range(2):
                        for j in range(22):
                            k.op("pe", "matmul", ps[:, half * 512:(half + 1) * 512], lhsT=hT[:, j, tsl],
                                 rhs=wd[:, j, half * 512:(half + 1) * 512], start=(j == 0), stop=(j == 21),
                                 rd=[Rwd, RhT[j]], wr=[pr[half]], inc=(j == 21), join=(j != 0))
                    residual_ln(n, ps, pr, ls)
                snap = k.snapshot()
                k.poison_names(R1_AB + R2_A, snap)
                if stop == (l, 3):
                    dump_x(t0); done = True; break
                if l + 1 < L:
                    for n in range(NT):
                        x_transposes(n)
            if not done:
                dump_x(t0)
    except _Stop:
        pass
    if part < nparts - 1:
        k.poison_names(["ststage_o"], k.snapshot())
        stg_o, _nb3 = al.at(r1 + 16384, "ststage_o", [128, 800], F32)
        so = 0
        for (nm, ap_, shp, rs) in states:
            d_ = dram("sto_" + nm, shp, kind="ExternalOutput")
            if nm.startswith("sk") or nm.startswith("sv"):
                nel = int(np.prod(shp[1:]))
                st_ap = stg_o[:, so:so + nel]
                Rs = k.R("ststage_o", so)
                so += nel
                src = st_ap if len(shp) == 2 else st_ap.rearrange("p (a b) -> p a b", a=shp[1])
                k.op("dve", "tensor_copy", out=src, in_=ap_, rd=rs, wr=[Rs])
                dst = d_ if len(shp) == 2 else d_.rearrange("p a b -> p (a b)")
                k.op("sp", "dma_start", out=dst, in_=st_ap, rd=[Rs], dma=k.dma_tl("st", nm))
            else:
                k.op("sp", "dma_start", out=d_, in_=ap_, rd=rs, dma=k.dma_tl("st", nm))
    k.finish()
    es.close()
    return nc


def _t5_bucket(dist):
    n = np.maximum(dist, 0)
    max_exact = 16
    nf = np.maximum(n, 1).astype(np.float32)
    large = max_exact + (np.log(nf / max_exact) / math.log(128 / max_exact) * (32 - max_exact)).astype(np.int32)
    large = np.minimum(large, 31)
    return np.where(n < max_exact, n, large)


def _perm_cols():
    ML = 512
    q = list(range(0, 512))
    kk = list(range(512, 1024))
    v = list(range(1024, 1536))
    o = list(range(1536, 2048))
    i_ = list(range(2048, 2052))
    f_ = list(range(2052, 2056))
    sq0 = 2056
    sk0 = 2056 + 512
    sv0 = sk0 + 128
    sq = []
    for c in range(4):
        sq += list(range(sq0 + c * 64, sq0 + (c + 1) * 64))
        sq += list(range(sq0 + (4 + c) * 64, sq0 + (5 + c) * 64))
    sk = list(range(sk0, sk0 + 128))
    sv_ = list(range(sv0, sv0 + 128))
    perm = q + kk + sq + sk + v + o + sv_ + i_ + f_
    assert len(perm) == NIN and sorted(perm) == list(range(NIN))
    return np.array(perm)


def prep_inputs(inp, depth=DEPTH):
    f = lambda a: np.ascontiguousarray(np.asarray(a, dtype=np.float32))
    L = DEPTH
    perm = _perm_cols()
    w_in = f(np.asarray(inp["w_in"])[:, :, perm])
    cst = np.zeros((128, 3, 128), np.float32)
    cst[:, 0, :] = np.eye(128, dtype=np.float32)
    cst[:, 1, :] = np.triu(np.ones((128, 128), np.float32))
    cst[:, 2, :] = 1.0
    rel = np.asarray(inp["rel_bias"], np.float32)
    kk = np.arange(128)[:, None]
    qq = np.arange(128)[None, :]
    biasT = np.zeros((128, 2, 8, 128), np.float32)
    for j in range(2):
        dist = qq - kk + (128 if j == 0 else 0)
        valid = (dist >= 0) & (dist < 128)
        bucket = _t5_bucket(dist)
        gathered = rel[bucket]
        for h in range(8):
            biasT[:, j, h, :] = np.where(valid, gathered[:, :, h], np.float32(NEG))
    pp = np.zeros((L, 128, 216), np.float32)
    bvs = np.zeros((L, 128, 528), np.float32)
    lnv = np.zeros((L, 3, 128, 2048), np.float32)
    for l in range(L):
        pp[l, :, 0:32] = np.transpose(np.asarray(inp["ml_conv_w"])[l].reshape(4, 8, 128), (2, 1, 0)).reshape(128, 32)
        pp[l, :, 32:40] = np.asarray(inp["ml_conv_b"])[l].reshape(8, 128).T
        pp[l, :, 40:172] = np.transpose(np.asarray(inp["ffn_conv_w"])[l].reshape(3, 44, 128), (2, 1, 0)).reshape(128, 132)
        pp[l, :, 172:216] = np.asarray(inp["ffn_conv_b"])[l].reshape(44, 128).T
        bvs[l, :, 0:512] = np.asarray(inp["ml_norm_g"])[l][None, :]
        bvs[l, :, 512:520] = np.asarray(inp["swa_sinks"])[l][None, :]
        bvs[l, :, 520:524] = np.asarray(inp["ml_i_bias"])[l][None, :]
        bvs[l, :, 524:528] = np.asarray(inp["ml_f_bias"])[l][None, :]
        for j, (g, b) in enumerate((("ln1_g", "ln1_b"), ("ln2_g", "ln2_b"), ("ln3_g", "ln3_b"))):
            lnv[l, j, :, 0:1024] = np.asarray(inp[g])[l][None, :]
            lnv[l, j, :, 1024:2048] = np.asarray(inp[b])[l][None, :]
    shared = {
        "cst": cst, "biasT": biasT, "w_in": w_in, "w_out": f(inp["w_out"]), "xa_wq": f(inp["xa_wq"]),
        "xa_wkv": f(inp["xa_wkv"]), "xa_wo": f(inp["xa_wo"]), "ffn_w_up": f(inp["ffn_w_up"]),
        "ffn_w_down": f(inp["ffn_w_down"]), "pp": pp, "bvs": bvs, "lnv": lnv,
    }
    return shared


def kernel(**inputs):
    x = np.asarray(inputs["x"], dtype=np.float32)
    mem = np.asarray(inputs["mem"], dtype=np.float32)
    B, S, _ = x.shape
    shared = prep_inputs(inputs)
    NPARTS = 1
    SP = S // NPARTS
    outs = []
    carry = [dict() for _ in range(B)]
    for part in range(NPARTS):
        nc = build(SEQ=SP, SB=512, depth=DEPTH, part=part, nparts=NPARTS)
        in_maps = []
        for b in range(B):
            m = dict(shared)
            m["x"] = np.ascontiguousarray(x[b, part * SP:(part + 1) * SP])
            m["mem"] = np.ascontiguousarray(mem[b])
            m.update(carry[b])
            in_maps.append(m)
        res = run_bass_kernel_spmd(nc, in_maps, core_ids=list(range(B)))
        outs.append(np.stack([np.asarray(r["out"], dtype=np.float32) for r in res.results], axis=0))
        carry = [{("sti_" + kk[4:]): np.asarray(v) for kk, v in r.items() if kk.startswith("sto_")}
                 for r in res.results]
    return np.concatenate(outs, axis=1)
```
